# Optimizing a Trainium2 kernel written in Bass

```python
import jax, jax.numpy as jnp
from jax import lax
import numpy as np

D_MODEL = 1024
BATCH = 4
SEQ = 8192
DEPTH = 4

D_MIX = D_MODEL
MLA_V = 128
MLA_NOPE = 128
MLA_ROPE = 64
MLA_WIDTH = D_MIX // 2
MLA_HEADS = MLA_WIDTH // MLA_V
MLA_Q_RANK = 384
MLA_KV_RANK = 256
GDN_DK = 128
GDN_DV = 128
GDN_WIDTH = D_MIX - MLA_WIDTH
GDN_HEADS = GDN_WIDTH // GDN_DV
GDN_CONV = 4
GDN_CHUNK = 64
Q_BLOCK = 128
ROPE_THETA = 10000.0
NORM_EPS = 1e-6
GDN_QKV = 2 * GDN_HEADS * GDN_DK + GDN_HEADS * GDN_DV

IN_SIZES = (MLA_Q_RANK, MLA_KV_RANK, MLA_ROPE, MLA_WIDTH, GDN_QKV, GDN_HEADS, GDN_HEADS, GDN_WIDTH)
IN_COLS = MLA_Q_RANK + MLA_KV_RANK + MLA_ROPE + MLA_WIDTH + GDN_QKV + GDN_HEADS + GDN_HEADS + GDN_WIDTH
IN_SPLIT_POINTS = tuple(int(s) for s in np.cumsum(IN_SIZES)[:-1])

kernel_name = "hybrid_mla_gdn_parallel_heads"


def rmsnorm(x, w):
    xf = x.astype(jnp.float32)
    y = xf * lax.rsqrt(jnp.mean(xf * xf, axis=-1, keepdims=True) + NORM_EPS)
    return (y * w.astype(jnp.float32)).astype(x.dtype)


def l2norm(x):
    return x * lax.rsqrt(jnp.sum(x * x, axis=-1, keepdims=True) + NORM_EPS)


def apply_rope(x, pos):
    r = x.shape[-1]
    half = r // 2
    inv_freq = jnp.power(ROPE_THETA, -jnp.arange(half, dtype=jnp.float32) * 2.0 / r)
    ang = pos.astype(jnp.float32)[:, :, None, None] * inv_freq
    cos, sin = jnp.cos(ang), jnp.sin(ang)
    xf = x.astype(jnp.float32)
    x1, x2 = xf[..., :half], xf[..., half:]
    return jnp.concatenate([x1 * cos - x2 * sin, x2 * cos + x1 * sin], axis=-1).astype(x.dtype)


def causal_attention(q, k, v, scale):
    b, s, h, dq = q.shape
    dv = v.shape[-1]
    nb = s // Q_BLOCK
    qb = q.reshape(b, nb, Q_BLOCK, h, dq).transpose(1, 0, 2, 3, 4)
    key_pos = jnp.arange(s)

    def one_block(args):
        i, q_blk = args
        sc = jnp.einsum('bqhd,bkhd->bhqk', q_blk, k, preferred_element_type=jnp.float32) * scale
        q_pos = i * Q_BLOCK + jnp.arange(Q_BLOCK)
        sc = jnp.where(key_pos[None, :] <= q_pos[:, None], sc, -1e30)
        p = jax.nn.softmax(sc, axis=-1)
        return jnp.einsum('bhqk,bkhd->bqhd', p.astype(v.dtype), v)

    o = lax.map(one_block, (jnp.arange(nb), qb))
    return o.transpose(1, 0, 2, 3, 4).reshape(b, s, h, dv)


def mla_branch(q_lat, kv_lat, k_pe_raw, z, pos, q_norm_w, q_up, kv_norm_w, kv_up):
    b, s, _ = q_lat.shape
    q = (rmsnorm(q_lat, q_norm_w) @ q_up).reshape(b, s, MLA_HEADS, MLA_NOPE + MLA_ROPE)
    q = jnp.concatenate([q[..., :MLA_NOPE], apply_rope(q[..., MLA_NOPE:], pos)], axis=-1)
    kv = (rmsnorm(kv_lat, kv_norm_w) @ kv_up).reshape(b, s, MLA_HEADS, MLA_NOPE + MLA_V)
    k_nope, v = kv[..., :MLA_NOPE], kv[..., MLA_NOPE:]
    k_pe = apply_rope(k_pe_raw[:, :, None, :], pos)
    k = jnp.concatenate([k_nope, jnp.broadcast_to(k_pe, (b, s, MLA_HEADS, MLA_ROPE))], axis=-1)
    o = causal_attention(q, k, v, (MLA_NOPE + MLA_ROPE) ** -0.5)
    return o.reshape(b, s, MLA_WIDTH) * jax.nn.silu(z)


def causal_conv(x, w):
    kw = w.shape[0]
    return lax.conv_general_dilated(x, w[:, None, :], window_strides=(1,), padding=[(kw - 1, 0)],
                                    dimension_numbers=('NWC', 'WIO', 'NWC'),
                                    feature_group_count=x.shape[-1])


def chunk_gated_delta_rule(q, k, v, g, beta):
    b, s, h, dk = q.shape
    dv = v.shape[-1]
    c = GDN_CHUNK
    n = s // c

    def chunks(t):
        t = t.reshape((b, n, c, h) + t.shape[3:])
        return jnp.moveaxis(t, 3, 1)

    q, k, v, g, beta = chunks(q), chunks(k), chunks(v), chunks(g), chunks(beta)
    g = jnp.cumsum(g, axis=-1)
    incl = jnp.tril(jnp.ones((c, c), dtype=bool))
    strict = jnp.tril(jnp.ones((c, c), dtype=bool), -1)
    decay = jnp.exp(jnp.where(incl, g[..., :, None] - g[..., None, :], -jnp.inf))
    kb = k * beta[..., None]
    kk = jnp.einsum('bhnid,bhnjd->bhnij', kb, k)
    a_mat = jnp.eye(c, dtype=jnp.float32) + jnp.where(strict, kk * decay, 0.0)
    rhs = jnp.concatenate([v * beta[..., None], kb * jnp.exp(g)[..., None]], axis=-1)
    sol = lax.linalg.triangular_solve(a_mat, rhs, left_side=True, lower=True, unit_diagonal=True)
    u, w = sol[..., :dv], sol[..., dv:]
    qk = jnp.einsum('bhnid,bhnjd->bhnij', q, k) * decay
    q_dec = q * jnp.exp(g)[..., None]
    k_dec = k * jnp.exp(g[..., -1:] - g)[..., None]
    g_last = jnp.exp(g[..., -1])

    def step(state, inp):
        qd, kd, u_c, w_c, qk_c, gl = inp
        v_new = u_c - jnp.einsum('bhcd,bhde->bhce', w_c, state)
        o = jnp.einsum('bhcd,bhde->bhce', qd, state) + jnp.einsum('bhij,bhje->bhie', qk_c, v_new)
        state = state * gl[..., None, None] + jnp.einsum('bhcd,bhce->bhde', kd, v_new)
        return state, o

    xs = tuple(jnp.moveaxis(t, 2, 0) for t in (q_dec, k_dec, u, w, qk, g_last))
    _, o = lax.scan(step, jnp.zeros((b, h, dk, dv), jnp.float32), xs)
    return o.transpose(1, 0, 3, 2, 4).reshape(b, s, h, dv)


def gdn_branch(qkv, a, bt, z, conv_w, a_log, dt_bias, o_norm_w):
    bsz, s, _ = qkv.shape
    qkv = jax.nn.silu(causal_conv(qkv, conv_w)).astype(jnp.float32)
    nqk = GDN_HEADS * GDN_DK
    q = l2norm(qkv[..., :nqk].reshape(bsz, s, GDN_HEADS, GDN_DK)) * (GDN_DK ** -0.5)
    k = l2norm(qkv[..., nqk:2 * nqk].reshape(bsz, s, GDN_HEADS, GDN_DK))
    v = qkv[..., 2 * nqk:].reshape(bsz, s, GDN_HEADS, GDN_DV)
    beta = jax.nn.sigmoid(bt.astype(jnp.float32))
    g = -jnp.exp(a_log.astype(jnp.float32)) * jax.nn.softplus(a.astype(jnp.float32) + dt_bias.astype(jnp.float32))
    o = chunk_gated_delta_rule(q, k, v, g, beta)
    o = rmsnorm(o, o_norm_w).astype(z.dtype).reshape(bsz, s, GDN_WIDTH)
    return o * jax.nn.silu(z)


def setup_inputs(seed: int = 0) -> dict:
    key = jax.random.key(seed)
    ks = jax.random.split(key, 20)
    f32 = jnp.float32
    nrm = lambda k, shape, sc: jax.random.normal(k, shape, f32) * sc
    dt = jnp.exp(jax.random.uniform(ks[14], (DEPTH, GDN_HEADS), f32, np.log(1e-3), np.log(1e-1)))
    offsets = jax.random.randint(ks[2], (BATCH, 1), 0, 4096, dtype=jnp.int32)
    return {
        "x": nrm(ks[0], (BATCH, SEQ, D_MODEL), 1.0),
        "c": nrm(ks[1], (BATCH, D_MODEL), 1.0),
        "positions": offsets + jnp.arange(SEQ, dtype=jnp.int32)[None, :],
        "w_mod": nrm(ks[3], (DEPTH, D_MODEL, 3 * D_MODEL), 0.5 * D_MODEL ** -0.5),
        "b_mod": nrm(ks[4], (DEPTH, 3 * D_MODEL), 0.02),
        "pre_norm_w": 1.0 + nrm(ks[5], (DEPTH, D_MODEL), 0.05),
        "post_norm_w": 1.0 + nrm(ks[6], (DEPTH, D_MODEL), 0.05),
        "w_in": nrm(ks[7], (DEPTH, D_MODEL, IN_COLS), D_MODEL ** -0.5),
        "mla_q_norm_w": 1.0 + nrm(ks[8], (DEPTH, MLA_Q_RANK), 0.05),
        "mla_q_up": nrm(ks[9], (DEPTH, MLA_Q_RANK, MLA_HEADS * (MLA_NOPE + MLA_ROPE)), MLA_Q_RANK ** -0.5),
        "mla_kv_norm_w": 1.0 + nrm(ks[10], (DEPTH, MLA_KV_RANK), 0.05),
        "mla_kv_up": nrm(ks[11], (DEPTH, MLA_KV_RANK, MLA_HEADS * (MLA_NOPE + MLA_V)), MLA_KV_RANK ** -0.5),
        "gdn_conv_w": nrm(ks[12], (DEPTH, GDN_CONV, GDN_QKV), 0.5),
        "gdn_a_log": jnp.log(jax.random.uniform(ks[13], (DEPTH, GDN_HEADS), f32, 1.0, 16.0)),
        "gdn_dt_bias": dt + jnp.log(-jnp.expm1(-dt)),
        "gdn_o_norm_w": 1.0 + nrm(ks[15], (DEPTH, GDN_DV), 0.05),
        "w_out": nrm(ks[16], (DEPTH, D_MIX, D_MODEL), D_MIX ** -0.5),
    }


def reference(x, c, positions, w_mod, b_mod, pre_norm_w, post_norm_w, w_in, mla_q_norm_w, mla_q_up,
              mla_kv_norm_w, mla_kv_up, gdn_conv_w, gdn_a_log, gdn_dt_bias, gdn_o_norm_w, w_out):
    c_act = jax.nn.silu(c)
    for l in range(DEPTH):
        mod = c_act @ w_mod[l] + b_mod[l]
        shift, scale, gate = jnp.split(mod, 3, axis=-1)
        h = rmsnorm(x, pre_norm_w[l]) * (1.0 + scale[:, None, :]) + shift[:, None, :]
        proj = h @ w_in[l]
        q_lat, kv_lat, k_pe, z_mla, qkv, a, bt, z_gdn = jnp.split(proj, IN_SPLIT_POINTS, axis=-1)
        y_mla = mla_branch(q_lat, kv_lat, k_pe, z_mla, positions, mla_q_norm_w[l], mla_q_up[l],
                           mla_kv_norm_w[l], mla_kv_up[l])
        y_gdn = gdn_branch(qkv, a, bt, z_gdn, gdn_conv_w[l], gdn_a_log[l], gdn_dt_bias[l],
                           gdn_o_norm_w[l])
        y = jnp.concatenate([y_mla, y_gdn], axis=-1) @ w_out[l]
        x = x + gate[:, None, :] * rmsnorm(y, post_norm_w[l])
    return x
```

```python
import math
import os
CUT = int(os.environ.get('KCUT', '99'))
import numpy as np
from contextlib import ExitStack
import concourse.bass as bass
import concourse.mybir as mybir
from concourse.bass_utils import run_bass_kernel_spmd

F32 = mybir.dt.float32
BF16 = mybir.dt.bfloat16
I32 = mybir.dt.int32
ALU = mybir.AluOpType
AF = mybir.ActivationFunctionType

D = 1024
NIN = 3272
NWB = 3336
EPS = 1e-6
ENGS = ("pe", "act", "dve", "pool", "sp")
SEM_ROT = 30000
HMAP = {"pe": "tensor", "act": "scalar", "dve": "vector", "pool": "gpsimd", "sp": "sync"}


class _Stop(Exception):
    pass


class Prog:
    def __init__(self, nc, stack):
        self.nc = nc
        self.stack = stack
        self.cur_sem = {}
        self.cur_cnt = {e: 0 for e in ENGS}
        self.nsem = 0
        self.all_eng_sems = {e: [] for e in ENGS}
        for e in ENGS:
            self._new_eng_sem(e)
        self.known = {e: {} for e in ENGS}
        self.last_write = {}
        self.readers = {}
        self.dma_sems = {}
        self.n_inst = {e: 0 for e in ENGS}
        self.n_wait = {e: 0 for e in ENGS}

    def _sem(self, name):
        s = self.stack.enter_context(self.nc.semaphore(name))
        self.nsem += 1
        return s

    def _new_eng_sem(self, e):
        self.cur_sem[e] = self._sem(f"c_{e}_{self.nsem}")
        self.cur_cnt[e] = 0

    def _deps(self, eng, reads, writes):
        deps = {}

        def add(ev):
            if ev is None:
                return
            s, v = ev
            k = id(s)
            if k not in deps or deps[k][1] < v:
                deps[k] = (s, v)

        for r in reads:
            add(self.last_write.get(r))
        for w in writes:
            add(self.last_write.get(w))
            rd = self.readers.get(w)
            if rd:
                for ev in rd.values():
                    add(ev)
        out = []
        kn = self.known[eng]
        own = id(self.cur_sem[eng])
        for k, (s, v) in deps.items():
            if k == own and eng == "pe":
                continue
            if kn.get(k, 0) >= v:
                continue
            kn[k] = v
            out.append((s, v))
        return out

    def _record(self, ev, reads, writes):
        for r in reads:
            self.readers.setdefault(r, {})[id(ev[0])] = ev
        for w in writes:
            self.last_write[w] = ev
            self.readers[w] = {}

    def _emit_now(self, eng, waits, fn, sem, inc):
        e = getattr(self.nc, HMAP[eng])
        for (s, v) in waits:
            e.wait_ge(s, v)
        self.n_wait[eng] += len(waits)
        if fn is not None:
            fn(e).then_inc(sem, inc)
            self.n_inst[eng] += 1

    def op(self, eng, fn, reads=(), writes=()):
        waits = self._deps(eng, reads, writes)
        if self.cur_cnt[eng] >= SEM_ROT:
            self._new_eng_sem(eng)
        sem = self.cur_sem[eng]
        self.cur_cnt[eng] += 1
        val = self.cur_cnt[eng]
        self._emit_now(eng, waits, fn, sem, 1)
        self._record((sem, val), reads, writes)

    def dma(self, q, semname, out, in_, reads=(), writes=(), **kw):
        if semname not in self.dma_sems:
            self.dma_sems[semname] = [self._sem("d_" + semname), 0]
        ent = self.dma_sems[semname]
        waits = self._deps(q, reads, writes)
        ent[1] += 16
        sem, val = ent[0], ent[1]
        self._emit_now(q, waits, lambda e: e.dma_start(out=out, in_=in_, **kw), sem, 16)
        self._record((sem, val), reads, writes)

    def barrier(self):
        evs = [(self.cur_sem[e], self.cur_cnt[e]) for e in ENGS if self.cur_cnt[e] > 0]
        evs += [(s, v) for (s, v) in self.dma_sems.values() if v > 0]
        for eng in ENGS:
            kn = self.known[eng]
            own = id(self.cur_sem[eng])
            waits = []
            for (s, v) in evs:
                if id(s) == own or kn.get(id(s), 0) >= v:
                    continue
                kn[id(s)] = v
                waits.append((s, v))
            self._emit_now(eng, waits, None, None, 0)

    def final_wait(self, eng, keys):
        waits = self._deps(eng, keys, ())
        self._emit_now(eng, waits, None, None, 0)


def build(S, NL, dbg=False, stop_after=None):
    nc = bass.Bass("TRN2", target_bir_lowering=False)
    NG = S // 512
    NT = S // 128

    def din(name, shape, dt=F32):
        return nc.dram_tensor(name, list(shape), dt, kind="ExternalInput").ap()

    def dscr(name, shape, dt):
        return nc.dram_tensor(name, list(shape), dt, kind="ExternalOutput" if dbg else "Internal").ap()

    x_in = din("x", [S, D])
    ccol = din("ccol", [128, 8])
    posrep = din("posrep", [64, S], I32)
    w_mod = din("w_mod", [NL, D, 3 * D])
    b_mod = din("b_mod", [NL, 3 * D])
    prew = din("prew", [NL, D])
    postw = din("postw", [NL, D])
    w_in = din("w_in", [NL, D, NIN])
    qnw = din("qnw", [NL, 384])
    q_up = din("q_up", [NL, 384, 768])
    kvnw = din("kvnw", [NL, 256])
    kv_up = din("kv_up", [NL, 256, 1024])
    convw = din("convw", [NL, 4 * 1536])
    alog8 = din("alog8", [8, NL])
    dtb8 = din("dtb8", [8, NL])
    onw = din("onw", [NL, 128])
    w_out = din("w_out", [NL, D, D])
    c_ident = din("c_ident", [128, 128])
    c_attm = din("c_attm", [128, 4 * 512])
    c_gmask = din("c_gmask", [128, 3 * 128])
    c_sel = din("c_sel", [128, 8 * 128])
    c_pm = din("c_pm", [128, 2])
    c_small = din("c_small", [64, 4])
    out = nc.dram_tensor("out", [S, D], F32, kind="ExternalOutput").ap()

    XR = dscr("XR", [S, D], F32)
    QT = dscr("QT", [4, 192, S], BF16)
    KT = dscr("KT", [4, 128, S], BF16)
    KPE = dscr("KPE", [64, S], BF16)
    VV = dscr("VV", [S, 512], BF16)
    ZM = dscr("ZM", [4, 128, S], BF16)
    ZG = dscr("ZG", [4, 128, S], BF16)
    GQ = dscr("GQ", [4, 128, S], BF16)
    GK = dscr("GK", [4, 128, S], BF16)
    GV = dscr("GV", [4, 128, S], BF16)
    COMB = dscr("COMB", [8, S], F32)
    YT = dscr("YT", [8, 128, S], BF16)
    COS = dscr("COS", [64, S], F32)
    SIN = dscr("SIN", [64, S], F32)
    GP = dscr("GP", [NL, D], F32)

    with ExitStack() as st:
        P = Prog(nc, st)

        uid = [0]

        def sbuf(stack, name, shape, dt):
            uid[0] += 1
            return stack.enter_context(nc.sbuf_tensor(f"{name}_u{uid[0]}", list(shape), dt))

        def mm(o, lhsT, rhs, start, stop, r, w):
            P.op("pe", lambda e: e.matmul(o, lhsT=lhsT, rhs=rhs, start=start, stop=stop), r, w)

        def tr(o, i, ident, r, w):
            P.op("pe", lambda e: e.transpose(o, i, ident), r, w)

        def act(o, i, func, r, w, bias=None, scale=None, accum=None):
            kw = {}
            if bias is not None:
                kw["bias"] = bias
            if scale is not None:
                kw["scale"] = scale
            if accum is not None:
                kw["accum_out"] = accum
            P.op("act", lambda e: e.activation(out=o, in_=i, func=func, **kw), r, w)

        def cp(eng, o, i, r, w):
            if eng == "act":
                P.op("act", lambda e: e.activation(out=o, in_=i, func=AF.Copy), r, w)
            else:
                P.op(eng, lambda e: e.tensor_copy(out=o, in_=i), r, w)

        def tsc(eng, o, i, s1, s2, op0, op1, r, w):
            if op1 is None:
                P.op(eng, lambda e: e.tensor_scalar(out=o, in0=i, scalar1=s1, scalar2=None, op0=op0), r, w)
            else:
                P.op(eng, lambda e: e.tensor_scalar(out=o, in0=i, scalar1=s1, scalar2=s2, op0=op0, op1=op1), r, w)

        def tt(eng, o, a, b, op, r, w):
            P.op(eng, lambda e: e.tensor_tensor(out=o, in0=a, in1=b, op=op), r, w)

        def stt(eng, o, a, sc, b, op0, op1, r, w):
            eng = "dve"
            P.op(eng, lambda e: e.scalar_tensor_tensor(out=o, in0=a, scalar=sc, in1=b, op0=op0, op1=op1), r, w)

        def rcp(o, i, r, w):
            P.op("dve", lambda e: e.reciprocal(out=o, in_=i), r, w)

        def memset(eng, o, val, w):
            P.op(eng, lambda e: e.memset(o, val), (), w)

        def dma(q, sem, o, i, r=(), w=(), **kw):
            P.dma(q, sem, o, i, reads=r, writes=w, **kw)

        def rsqrt(o, i, scale, bias_ap, r, w):
            act(o, i, AF.Ln, r, w, bias=bias_ap, scale=scale)
            act(o, o, AF.Exp, w, w, scale=-0.5)

        pbs = [st.enter_context(nc.psum_tensor(f"pb{i}", [128, 512], F32)) for i in range(8)]

        def PBK(i):
            return [("pb", i)]

        identf = sbuf(st, "identf", [128, 128], F32)
        identb = sbuf(st, "identb", [128, 128], BF16)
        onesf = sbuf(st, "onesf", [128, 128], F32)
        attm = sbuf(st, "attm", [128, 4, 512], BF16)
        gmask = sbuf(st, "gmask", [128, 3, 128], F32)
        sel = sbuf(st, "sel", [128, 8, 128], F32)
        pmask = sbuf(st, "pmask", [128, 2], F32)
        small = sbuf(st, "small", [64, 4], F32)
        epsc = sbuf(st, "epsc", [128, 1], F32)
        cols = sbuf(st, "cols", [128, NL, 80], F32)
        nA = sbuf(st, "nA", [8, NL], F32)
        dtb = sbuf(st, "dtb", [8, NL], F32)
        gpb = sbuf(st, "gpb", [128, D], F32)

        dma("sp", "c0", identf[:], c_ident, w=["identf"])
        dma("sp", "c1", gmask[:].rearrange("p m j -> p (m j)"), c_gmask, w=["gmask"])
        dma("sp", "c2", sel[:].rearrange("p m j -> p (m j)"), c_sel, w=["sel"])
        dma("sp", "c3", small[:], c_small, w=["small"])
        dma("sp", "c3b", pmask[:], c_pm, w=["pmask"])
        dma("sp", "c4", nA[:], alog8, w=["nA"])
        dma("sp", "c5", dtb[:], dtb8, w=["dtb"])
        cp("dve", identb[:], identf[:], ["identf"], ["identb"])
        memset("pool", onesf[:], 1.0, ["onesf"])
        memset("pool", epsc[:], EPS, ["epsc"])
        act(nA[:], nA[:], AF.Exp, ["nA"], ["nA"])
        tsc("dve", nA[:], nA[:], small[0:8, 2:3], None, ALU.mult, None, ["nA", "small"], ["nA"])

        with ExitStack() as ps_:
            amst = sbuf(ps_, "amst", [128, 4 * 512], F32)
            dma("sp", "c6", amst[:], c_attm, w=["amst"])
            cp("dve", attm[:].rearrange("p m j -> p (m j)"), amst[:], ["amst"], ["attm"])

            cact = sbuf(ps_, "cact", [128, 8], F32)
            dma("sp", "c7", cact[:], ccol, w=["cact"])
            act(cact[:], cact[:], AF.Silu, ["cact"], ["cact"])
            NROW = 3072 + 1792 + 6144
            rowbuf = sbuf(ps_, "rowbuf", [1, NROW], F32)
            bmrow = sbuf(ps_, "bmrow", [1, 3072], F32)
            pwrow = sbuf(ps_, "pwrow", [1, D], F32)
            wmst = [sbuf(ps_, f"wmst{i}", [128, 8, 512], F32) for i in range(2)]
            for l in range(NL):
                dma("sp", "r0", bmrow[:], b_mod[l:l + 1, :], w=["bmrow"])
                dma("sp", "r1", rowbuf[0:1, 3072:4096], prew[l:l + 1, :], w=["rowbuf_s"])
                dma("sp", "r1", rowbuf[0:1, 4096:4480], qnw[l:l + 1, :], w=["rowbuf_s"])
                dma("sp", "r1", rowbuf[0:1, 4480:4736], kvnw[l:l + 1, :], w=["rowbuf_s"])
                dma("sp", "r1", rowbuf[0:1, 4736:4864], onw[l:l + 1, :], w=["rowbuf_s"])
                dma("sp", "r1", rowbuf[0:1, 4864:NROW], convw[l:l + 1, :], w=["rowbuf_s"])
                dma("sp", "r2", pwrow[:], postw[l:l + 1, :], w=["pwrow"])
                for cg in range(6):
                    ws_ = wmst[cg % 2]
                    wk = f"wmst{cg % 2}"
                    dma("sp", wk, ws_[:], w_mod[l, :, cg * 512:(cg + 1) * 512].rearrange("(c p) n -> p c n", p=128), w=[wk])
                    pb = pbs[cg % 2]
                    for c in range(8):
                        mm(pb[0:1, :], cact[:, c:c + 1], ws_[:, c, :], c == 0, c == 7, [wk, "cact"], PBK(cg % 2))
                    tt("dve", rowbuf[0:1, cg * 512:(cg + 1) * 512], pb[0:1, :], bmrow[0:1, cg * 512:(cg + 1) * 512], ALU.add,
                       PBK(cg % 2) + ["bmrow"], ["rowbuf_m"])
                nchunk = 16 + 62
                for j in range(nchunk):
                    off = j * 128 if j < 16 else 3072 + (j - 16) * 128
                    mm(pbs[2][:, j:j + 1], rowbuf[0:1, off:off + 128], onesf[0:1, 0:1], True, True,
                       ["rowbuf_m", "rowbuf_s", "onesf"], PBK(2))
                cp("dve", cols[:, l, 0:nchunk], pbs[2][:, 0:nchunk], PBK(2), [("cols", l)])
                stt("dve", cols[:, l, 8:16], cols[:, l, 8:16], 1.0, cols[:, l, 16:24], ALU.add, ALU.mult, [("cols", l)], [("cols", l)])
                tt("dve", pwrow[:], pwrow[:], rowbuf[0:1, 2048:3072], ALU.mult, ["pwrow", "rowbuf_m"], ["pwrow"])
                dma("sp", "r3", GP[l:l + 1, :], pwrow[:], r=["pwrow"], w=[("GP", l)])

            CH = min(2048, S)
            posi = sbuf(ps_, "posi", [64, CH], I32)
            ang = sbuf(ps_, "ang", [64, CH], F32)
            a2 = sbuf(ps_, "a2", [64, CH], F32)
            kf = sbuf(ps_, "kf", [64, CH], F32)
            ki = sbuf(ps_, "ki", [64, CH], I32)
            fx = sbuf(ps_, "fx", [64, CH], F32)
            tab = [sbuf(ps_, f"tab{i}", [64, CH], F32) for i in range(2)]
            C1 = 6.28125
            C2 = 2.0 * math.pi - C1
            for ch in range(S // CH):
                cs = slice(ch * CH, (ch + 1) * CH)
                dma("sp", "rp0", posi[:], posrep[:, cs], w=["posi"])
                cp("dve", ang[:], posi[:], ["posi"], ["ang"])
                tsc("dve", ang[:], ang[:], small[:, 0:1], None, ALU.mult, None, ["ang", "small"], ["ang"])
                for ti, offv in ((0, math.pi / 2), (1, 0.0)):
                    tsc("dve", a2[:], ang[:], offv, None, ALU.add, None, ["ang"], ["a2"])
                    tsc("dve", ki[:], a2[:], 1.0 / (2 * math.pi), None, ALU.mult, None, ["a2"], ["ki"])
                    cp("dve", kf[:], ki[:], ["ki"], ["kf"])
                    stt("dve", a2[:], kf[:], -C1, a2[:], ALU.mult, ALU.add, ["kf", "a2"], ["a2"])
                    stt("dve", a2[:], kf[:], -C2, a2[:], ALU.mult, ALU.add, ["kf", "a2"], ["a2"])
                    tsc("dve", fx[:], a2[:], math.pi, 2 * math.pi, ALU.is_gt, ALU.mult, ["a2"], ["fx"])
                    tt("dve", a2[:], a2[:], fx[:], ALU.subtract, ["a2", "fx"], ["a2"])
                    tsc("dve", fx[:], a2[:], -math.pi, 2 * math.pi, ALU.is_lt, ALU.mult, ["a2"], ["fx"])
                    tt("dve", a2[:], a2[:], fx[:], ALU.add, ["a2", "fx"], ["a2"])
                    tsc("dve", a2[:], a2[:], math.pi, -math.pi, ALU.min, ALU.max, ["a2"], ["a2"])
                    tk = f"tab{ti}"
                    act(tab[ti][:], a2[:], AF.Sin, ["a2"], [tk])
                    if ti == 1:
                        tsc("dve", tab[ti][:], tab[ti][:], small[:, 1:2], None, ALU.mult, None, [tk, "small"], [tk])
                    dma("sp", "rp" + tk, (COS if ti == 0 else SIN)[:, cs], tab[ti][:], r=[tk], w=[("ROPE", ti, ch)])
            P.barrier()
        ROPE_KEYS = [("ROPE", ti, ch) for ti in range(2) for ch in range(S // min(2048, S))]

        def chk(tag):
            if stop_after == tag:
                P.barrier()
                raise _Stop()

        try:
          for l in range(NL if stop_after != "P" else 0):
              xsrc = x_in if l == 0 else XR
              xdst = out if l == NL - 1 else XR
              xkey = "XIN" if l == 0 else "XR"
              xdkey = "OUT" if l == NL - 1 else "XR"
              dma("sp", "gpb", gpb[:], GP[l:l + 1, :].partition_broadcast(128), r=[("GP", l)], w=["gpb"])
              chk("G")

              with ExitStack() as pa:
                  Wb = sbuf(pa, "Wb", [128, 8, NWB], BF16)
                  Qb = sbuf(pa, "Qb", [128, 3, 4, 256], BF16)
                  KVb = sbuf(pa, "KVb", [128, 2, 1024], BF16)
                  HS = NIN // 2
                  pa_w = ExitStack()
                  stg = [sbuf(pa_w, f"stg{i}", [128, HS], F32) for i in range(2)]
                  si = 0
                  ceng = ["dve", "pool"]
                  for c in range(8):
                      for hf in range(2):
                          s_ = stg[si % 2]
                          sk = f"stg{si % 2}"
                          dma("sp", sk, s_[:], w_in[l, c * 128:(c + 1) * 128, hf * HS:(hf + 1) * HS], w=[sk])
                          lo, hi = hf * HS, (hf + 1) * HS
                          for (a, b, dst) in ((0, 704, 0), (672, 704, 704), (640, 672, 736), (704, NIN, 768)):
                              a2_, b2_ = max(a, lo), min(b, hi)
                              if a2_ >= b2_:
                                  continue
                              d0 = dst + (a2_ - a)
                              cp(ceng[si % 2], Wb[:, c, d0:d0 + (b2_ - a2_)], s_[:, a2_ - lo:b2_ - lo], [sk], [("Wb", c)])
                          si += 1
                  for c in range(3):
                      s_ = stg[si % 2]
                      sk = f"stg{si % 2}"
                      dma("sp", sk, s_[:, 0:768], q_up[l, c * 128:(c + 1) * 128, :], w=[sk])
                      sv = s_[:, 0:768].rearrange("p (h f) -> p h f", h=4)
                      qs = cols[:, l, 24 + c:25 + c]
                      for (a, b, dst) in ((0, 128, 0), (128, 192, 128), (160, 192, 192), (128, 160, 224)):
                          tsc("dve", Qb[:, c, :, dst:dst + (b - a)], sv[:, :, a:b], qs, None, ALU.mult, None, [sk, ("cols", l)], ["Qb"])
                      si += 1
                  for c in range(2):
                      s_ = stg[si % 2]
                      sk = f"stg{si % 2}"
                      dma("sp", sk, s_[:, 0:1024], kv_up[l, c * 128:(c + 1) * 128, :], w=[sk])
                      tsc("dve", KVb[:, c, :], s_[:, 0:1024], cols[:, l, 27 + c:28 + c], None, ALU.mult, None, [sk, ("cols", l)], ["KVb"])
                      si += 1
                  WBK = [("Wb", c) for c in range(8)]
                  P.barrier()
                  pa_w.close()
                  NGr = 0 if stop_after == "W" else (int(stop_after[1:]) if (stop_after or "").startswith("g") else NG)

                  xt = [sbuf(pa, f"xt{i}", [128, D], F32) for i in range(2)]
                  xs = [sbuf(pa, f"xs{i}", [128, D], F32) for i in range(4)]
                  junk = sbuf(pa, "junk", [128, D], BF16)
                  st1 = sbuf(pa, "st1", [128, 8], F32)
                  hT = [sbuf(pa, f"hT{i}", [128, 8, 512], BF16) for i in range(2)]
                  qlT = sbuf(pa, "qlT", [128, 3, 512], BF16)
                  kvT = sbuf(pa, "kvT", [128, 2, 512], BF16)
                  sq = [sbuf(pa, f"sq{i}", [128, 512], F32) for i in range(3)]
                  rq = sbuf(pa, "rq", [128, 512], F32)
                  rkv = sbuf(pa, "rkv", [128, 512], F32)
                  rkvt = sbuf(pa, "rkvt", [128, 4], F32)
                  cst = sbuf(pa, "cst", [64, 512], F32)
                  sst = sbuf(pa, "sst", [64, 512], F32)
                  crr = sbuf(pa, "crr", [64, 512], F32)
                  srr = sbuf(pa, "srr", [64, 512], F32)
                  t1 = [sbuf(pa, f"t1_{i}", [64, 512], F32) for i in range(1)] * 2
                  t2 = [sbuf(pa, f"t2_{i}", [64, 512], F32) for i in range(1)] * 2
                  ost = [sbuf(pa, f"ost{i}", [128, 512], BF16) for i in range(6)]
                  vst = [sbuf(pa, f"vst{i}", [128, 512], BF16) for i in range(2)]
                  zst = [sbuf(pa, f"zst{i}", [128, 4, 512], BF16) for i in range(2)]
                  convd = sbuf(pa, "convd", [128, 48, 128], BF16)
                  cvb = [sbuf(pa, f"cvb{i}", [128, 515], BF16) for i in range(3)]
                  cvc = sbuf(pa, "cvc", [128, 12, 3], BF16)
                  sact = [sbuf(pa, f"sact{i}", [128, 512], F32) for i in range(8)]
                  rn = [sbuf(pa, f"rn{i}", [128, 512], F32) for i in range(2)]
                  abw = [sbuf(pa, f"abw{i}", [8, 512], F32) for i in range(4)]
                  memset("pool", cvc[:], 0.0, ["cvc"])
                  for jj in range(4):
                      for ch in range(12):
                          tsc("dve", convd[:, jj * 12 + ch, :], identb[:], cols[:, l, 30 + jj * 12 + ch:31 + jj * 12 + ch], None,
                              ALU.mult, None, ["identb", ("cols", l)], ["convd"])
                  octr = [0]
                  pbr = [2]

                  def nextpb():
                      b = pbr[0]
                      pbr[0] = 2 + (pbr[0] - 2 + 1) % 6
                      return b

                  def nextost():
                      i = octr[0] % 6
                      octr[0] += 1
                      return i

                  def xstat(g, t):
                      ti = g * 4 + t
                      x_ = xt[ti % 2]
                      xk = f"xt{ti % 2}"
                      dma("sp", xk, x_[:], xsrc[ti * 128:(ti + 1) * 128, :], r=[(xkey, ti)], w=[xk])
                      act(junk[:], x_[:], AF.Square, [xk], ["junk", "st1"], accum=st1[:, 0:1])
                      rsqrt(st1[:, 1:2], st1[:, 0:1], 1.0 / D, epsc[:, 0:1], ["st1", "epsc"], ["st1"])
                      act(xs[t][:], x_[:], AF.Copy, [xk, "st1"], [f"xs{t}"], scale=st1[:, 1:2])

                  def xtrans(g, t):
                      hh_ = hT[g % 2]
                      hhk = f"hT{g % 2}"
                      for c in range(8):
                          tr(pbs[c // 4][:, (c % 4) * 128:(c % 4 + 1) * 128], xs[t][:, c * 128:(c + 1) * 128], identf[:],
                             [f"xs{t}", "identf"], [("pb", c // 4)])
                      for c in range(8):
                          tsc("dve", hh_[:, c, t * 128:(t + 1) * 128], pbs[c // 4][:, (c % 4) * 128:(c % 4 + 1) * 128],
                              cols[:, l, 8 + c:9 + c], cols[:, l, c:c + 1], ALU.mult, ALU.add,
                              [("pb", c // 4), ("cols", l)], [hhk])

                  for t in range(4):
                      if NGr > 0:
                          xstat(0, t)
                          xtrans(0, t)
                  for g in range(NGr):
                      tok = slice(g * 512, (g + 1) * 512)
                      h_ = hT[g % 2]
                      hk = f"hT{g % 2}"
                      dma("sp", "cst", cst[:], COS[:, tok], r=ROPE_KEYS, w=["cst"])
                      dma("sp", "sst", sst[:], SIN[:, tok], r=ROPE_KEYS, w=["sst"])

                      def proj(col0, ncol):
                          b = nextpb()
                          for c in range(8):
                              mm(pbs[b][0:ncol, :], Wb[:, c, col0:col0 + ncol], h_[:, c, :], c == 0, c == 7, [hk, ("Wb", c)], PBK(b))
                          return b

                      def nxt(t):
                          return

                      def nstat(t):
                          if g + 1 < NGr:
                              xstat(g + 1, t)

                      def ntrans(t):
                          if g + 1 < NGr:
                              xtrans(g + 1, t)

                      for c in range(3):
                          b = proj(c * 128, 128)
                          cp("act", qlT[:, c, :], pbs[b][:], PBK(b), ["qlT"])
                          act(sq[c][:], pbs[b][:], AF.Square, PBK(b), [f"sq{c}"])
                      bsum = nextpb()
                      for c in range(3):
                          mm(pbs[bsum][:], onesf[:], sq[c][:], c == 0, c == 2, [f"sq{c}", "onesf"], PBK(bsum))
                      rsqrt(rq[:], pbs[bsum][:], 1.0 / 384.0, epsc[:, 0:1], PBK(bsum) + ["epsc"], ["rq"])
                      for c in range(2):
                          b = proj(384 + c * 128, 128)
                          cp("act", kvT[:, c, :], pbs[b][:], PBK(b), ["kvT"])
                          act(sq[c][:], pbs[b][:], AF.Square, PBK(b), [f"sq{c}"])
                      bsum = nextpb()
                      for c in range(2):
                          mm(pbs[bsum][:], onesf[:], sq[c][:], c == 0, c == 1, [f"sq{c}", "onesf"], PBK(bsum))
                      bt_ = nextpb()
                      for t in range(4):
                          for c in range(2):
                              mm(pbs[bt_][:, 8 + t:9 + t], sq[c][:, t * 128:(t + 1) * 128], onesf[:, 0:1],
                                 c == 0, c == 1, [f"sq{c}", "onesf"], PBK(bt_))
                      rsqrt(rkv[:], pbs[bsum][:], 1.0 / 256.0, epsc[:, 0:1], PBK(bsum) + ["epsc"], ["rkv"])
                      rsqrt(rkvt[:], pbs[bt_][:, 8:12], 1.0 / 256.0, epsc[:, 0:1], PBK(bt_) + ["epsc"], ["rkvt"])
                      tt("dve", crr[:], cst[:], rq[0:64, :], ALU.mult, ["cst", "rq"], ["crr"])
                      tt("dve", srr[:], sst[:], rq[0:64, :], ALU.mult, ["sst", "rq"], ["srr"])
                      nxt(0)
                      for h in range(4):
                          b = nextpb()
                          for c in range(3):
                              mm(pbs[b][:], Qb[:, c, h, 0:128], qlT[:, c, :], c == 0, c == 2, ["Qb", "qlT"], PBK(b))
                          oi = nextost()
                          tt("dve", ost[oi][:], pbs[b][:], rq[:], ALU.mult, PBK(b) + ["rq"], [f"ost{oi}"])
                          dma("sp", f"ost{oi}", QT[h, 0:128, tok], ost[oi][:], r=[f"ost{oi}"], w=[("QT", h, g)])
                          b = nextpb()
                          for c in range(3):
                              mm(pbs[b][:], Qb[:, c, h, 128:256], qlT[:, c, :], c == 0, c == 2, ["Qb", "qlT"], PBK(b))
                          tt("dve", t1[0][:], pbs[b][0:64, :], crr[:], ALU.mult, PBK(b) + ["crr"], ["t1_0"])
                          tt("dve", t2[0][:], pbs[b][64:128, :], srr[:], ALU.mult, PBK(b) + ["srr"], ["t2_0"])
                          oi = nextost()
                          tt("pool", ost[oi][0:64, :], t1[0][:], t2[0][:], ALU.add, ["t1_0", "t2_0"], [f"ost{oi}"])
                          dma("sp", f"ost{oi}", QT[h, 128:192, tok], ost[oi][0:64, :], r=[f"ost{oi}"], w=[("QT", h, g)])
                      nxt(1)
                      for h in range(4):
                          b = nextpb()
                          for c in range(2):
                              mm(pbs[b][:], KVb[:, c, h * 256:h * 256 + 128], kvT[:, c, :], c == 0, c == 1, ["KVb", "kvT"], PBK(b))
                          oi = nextost()
                          tt("dve", ost[oi][:], pbs[b][:], rkv[:], ALU.mult, PBK(b) + ["rkv"], [f"ost{oi}"])
                          dma("sp", f"ost{oi}", KT[h, :, tok], ost[oi][:], r=[f"ost{oi}"], w=[("KT", h, g)])
                      nxt(2)
                      kvv = KVb[:].rearrange("p c (h f) -> p c h f", h=4)
                      for t in range(4):
                          b = nextpb()
                          for c in range(2):
                              mm(pbs[b][:].rearrange("p (h f) -> p h f", h=4), kvT[:, c, t * 128:(t + 1) * 128], kvv[:, c, :, 128:256],
                                 c == 0, c == 1, ["KVb", "kvT"], PBK(b))
                          vi = (g * 4 + t) % 2
                          tsc("dve", vst[vi][:], pbs[b][:], rkvt[:, t:t + 1], None, ALU.mult, None, PBK(b) + ["rkvt"], [f"vst{vi}"])
                          dma("sp", f"vst{vi}", VV[g * 512 + t * 128:g * 512 + (t + 1) * 128, :], vst[vi][:], r=[f"vst{vi}"], w=[("VV", g)])
                      b = proj(640, 128)
                      tt("dve", t1[0][:], pbs[b][0:64, :], cst[:], ALU.mult, PBK(b) + ["cst"], ["t1_0"])
                      tt("dve", t2[0][:], pbs[b][64:128, :], sst[:], ALU.mult, PBK(b) + ["sst"], ["t2_0"])
                      oi = nextost()
                      tt("pool", ost[oi][0:64, :], t1[0][:], t2[0][:], ALU.add, ["t1_0", "t2_0"], [f"ost{oi}"])
                      dma("sp", f"ost{oi}", KPE[:, tok], ost[oi][0:64, :], r=[f"ost{oi}"], w=[("KPE", g)])
                      nxt(3)
                      for (zi, base, dstD, zk) in ((0, 768, ZM, "ZM"), (1, 2824, ZG, "ZG")):
                          for c in range(4):
                              b = proj(base + c * 128, 128)
                              act(zst[zi][:, c, :], pbs[b][:], AF.Silu, PBK(b), [f"zst{zi}"])
                          dma("sp", f"zst{zi}", dstD[:, :, tok].rearrange("c p t -> p c t"), zst[zi][:], r=[f"zst{zi}"], w=[(zk, g)])
                          nstat(zi)
                      def conv_pe(ch):
                          sl = ch % 3
                          b2 = nextpb()
                          for j in range(4):
                              mm(pbs[b2][:], convd[:, j * 12 + ch, :], cvb[sl][:, j:j + 512], j == 0, j == 3, ["convd", f"cvb{sl}"], PBK(b2))
                          hh = ch % 4
                          if ch >= 8:
                              oi = nextost()
                              act(ost[oi][:], pbs[b2][:], AF.Silu, PBK(b2), [f"ost{oi}"])
                              dma("sp", f"ost{oi}", GV[hh, :, tok], ost[oi][:], r=[f"ost{oi}"], w=[("GV", hh, g)])
                          else:
                              act(sact[ch][:], pbs[b2][:], AF.Silu, PBK(b2), [f"sact{ch}"])

                      for ch in range(12):
                          sl = ch % 3
                          b = proj(1280 + ch * 128, 128)
                          cp("act", cvb[sl][:, 3:515], pbs[b][:], PBK(b), [f"cvb{sl}"])
                          cp("pool", cvb[sl][:, 0:3], cvc[:, ch, :], ["cvc"], [f"cvb{sl}"])
                          cp("pool", cvc[:, ch, :], cvb[sl][:, 512:515], [f"cvb{sl}"], ["cvc"])
                          if ch >= 1:
                              conv_pe(ch - 1)
                          if ch == 3:
                              nstat(2)
                          if ch == 7:
                              nstat(3)
                      conv_pe(11)
                      def l2sq(ch):
                          if ch < 8:
                              act(sq[ch % 3][:], sact[ch][:], AF.Square, [f"sact{ch}"], [f"sq{ch % 3}"])
                      l2sq(0)
                      l2sq(1)
                      for ch in range(8):
                          l2sq(ch + 2)
                          b2 = nextpb()
                          mm(pbs[b2][:], onesf[:], sq[ch % 3][:], True, True, [f"sq{ch % 3}", "onesf"], PBK(b2))
                          r_ = rn[ch % 2]
                          rk_ = f"rn{ch % 2}"
                          rsqrt(r_[:], pbs[b2][:], 1.0, epsc[:, 0:1], PBK(b2) + ["epsc"], [rk_])
                          oi = nextost()
                          sc_ = (128.0 ** -0.5) if ch < 4 else 1.0
                          stt("dve", ost[oi][:], sact[ch][:], sc_, r_[:], ALU.mult, ALU.mult, [f"sact{ch}", rk_], [f"ost{oi}"])
                          hh = ch % 4
                          dma("sp", f"ost{oi}", (GQ if ch < 4 else GK)[hh, :, tok], ost[oi][:], r=[f"ost{oi}"],
                              w=[("GQ" if ch < 4 else "GK", hh, g)])
                          if ch % 2 == 1:
                              ntrans(ch // 2)
                      b = proj(2816, 8)
                      bet, ea, ga, gb2 = abw
                      act(bet[:], pbs[b][0:8, :], AF.Exp, PBK(b), ["abw0"], scale=-1.0)
                      tsc("dve", bet[:], bet[:], 1.0, None, ALU.add, None, ["abw0"], ["abw0"])
                      rcp(bet[:], bet[:], ["abw0"], ["abw0"])
                      act(ea[:], pbs[b][0:8, :], AF.Exp, PBK(b) + ["dtb"], ["abw1"], bias=dtb[:, l:l + 1])
                      act(ea[:], ea[:], AF.Ln, ["abw1"], ["abw1"], bias=1.0)
                      tsc("pool", ga[:], ea[:], nA[:, l:l + 1], None, ALU.mult, None, ["abw1", "nA"], ["abw2"])
                      src_, sk_, dst_, dk2 = ga, "abw2", gb2, "abw3"
                      for s_ in (1, 2, 4, 8, 16, 32):
                          sv = src_[:].rearrange("p (n c) -> p n c", c=64)
                          dv = dst_[:].rearrange("p (n c) -> p n c", c=64)
                          cp("pool", dv[:, :, 0:s_], sv[:, :, 0:s_], [sk_], [dk2])
                          tt("pool", dv[:, :, s_:64], sv[:, :, s_:64], sv[:, :, 0:64 - s_], ALU.add, [sk_], [dk2])
                          src_, sk_, dst_, dk2 = dst_, dk2, src_, sk_
                      stt("dve", bet[:], bet[:], small[0:8, 3:4], src_[:], ALU.mult, ALU.add, ["abw0", sk_, "small"], ["abw0"])
                      dma("sp", "abw0", COMB[:, tok], bet[:], r=["abw0"], w=[("COMB", g)])
                  P.barrier()
              if stop_after == "A" or stop_after == "W" or (stop_after or "").startswith("g"):
                  break

              with ExitStack() as pb_:
                  KTs = [sbuf(pb_, f"KTs{i}", [128, S], BF16) for i in range(2)]
                  KPs = sbuf(pb_, "KPs", [128, S], BF16)
                  Vs = [sbuf(pb_, f"Vs{i}", [128, NT, 128], BF16) for i in range(2)]
                  NQ = 3
                  NPT = 6
                  LAG = 2
                  qn = [sbuf(pb_, f"qn{i}", [128, 512], BF16) for i in range(NQ)]
                  qp = [sbuf(pb_, f"qp{i}", [128, 512], BF16) for i in range(NQ)]
                  zt = [sbuf(pb_, f"zt{i}", [128, 512], BF16) for i in range(NQ)]
                  PT = [sbuf(pb_, f"PT{i}", [128, 512], BF16) for i in range(NPT)]
                  accD = [sbuf(pb_, f"accD{i}", [128, 512], F32) for i in range(2)]
                  accP = [sbuf(pb_, f"accP{i}", [128, 512], F32) for i in range(2)]
                  rcs = [sbuf(pb_, f"rcs{i}", [128, 512], F32) for i in range(2)]
                  yo = [sbuf(pb_, f"yo{i}", [128, 512], BF16) for i in range(2)]
                  SCL = 192.0 ** -0.5
                  allg = list(range(NG))
                  memset("pool", KPs[64:128, :], 0.0, ["KPs"])
                  for i in range(NQ):
                      memset("pool", qp[i][64:128, :], 0.0, [f"qp{i}"])
                  dma("sp", "KPs", KPs[0:64, :], KPE, r=[("KPE", g) for g in allg], w=["KPs"])

                  def load_head(h):
                      i = h % 2
                      dma("sp", f"KTs{i}", KTs[i][:], KT[h], r=[("KT", h, g) for g in allg], w=[f"KTs{i}"])
                      vsrc = VV[:, h * 128:(h + 1) * 128].rearrange("(t p) f -> p t f", p=128)
                      nsp = max(1, NT // 16)
                      for ii in range(nsp):
                          tsl = slice(ii * (NT // nsp), (ii + 1) * (NT // nsp))
                          dma("sp", f"Vs{i}", Vs[i][:, tsl, :], vsrc[:, tsl, :], r=[("VV", g) for g in allg], w=[f"Vs{i}"])

                  units = []
                  groups = []
                  for h in range(4):
                      for g in range(NG):
                          gi = len(groups)
                          groups.append((h, g))
                          for jp in range(2 * (g + 1)):
                              units.append((h, g, jp, gi))
                  nU = len(units)
                  loaded = set()
                  touched = {}
                  NP2 = 4
                  PT2 = [sbuf(pb_, f"PT2_{i}", [128, 2, 512], BF16) for i in range(NP2)]
                  pairb = [sbuf(pb_, f"pairb{i}", [128, 512], BF16) for i in range(2)]

                  def ensure_group(gi):
                      if gi in loaded or gi >= len(groups):
                          return
                      loaded.add(gi)
                      h, g = groups[gi]
                      tok = slice(g * 512, (g + 1) * 512)
                      qi = gi % NQ
                      dma("sp", f"qn{qi}", qn[qi][:], QT[h, 0:128, tok], r=[("QT", h, g)], w=[f"qn{qi}"])
                      dma("sp", f"qp{qi}", qp[qi][0:64, :], QT[h, 128:192, tok], r=[("QT", h, g)], w=[f"qp{qi}"])
                      dma("sp", f"zt{qi}", zt[qi][:], ZM[h, :, tok], r=[("ZM", g)], w=[f"zt{qi}"])

                  load_head(0)
                  LAGP = 1
                  for s_ in range(nU + LAGP):
                      if s_ < nU:
                          h, g, jp, gi = units[s_]
                          if jp == 0:
                              ensure_group(gi)
                              ensure_group(gi + 1)
                          qi = gi % NQ
                          hb = h % 2
                          b0 = 2 * (s_ % 2)
                          pi = s_ % NP2
                          for e_ in range(2):
                              j = 2 * jp + e_
                              ks = slice(j * 128, (j + 1) * 128)
                              mm(pbs[b0 + e_][:], KTs[hb][:, ks], qn[qi][:], True, False, [f"KTs{hb}", f"qn{qi}"], PBK(b0 + e_))
                              mm(pbs[b0 + e_][:], KPs[:, ks], qp[qi][:], False, True, ["KPs", f"qp{qi}"], PBK(b0 + e_))
                      if s_ - LAGP >= 0:
                          uh, ug, ujp, ugi = units[s_ - LAGP]
                          ob = 4 + ugi % 2
                          ppi = (s_ - LAGP) % NP2
                          lastp = ujp == 2 * (ug + 1) - 1
                          for e_ in range(2):
                              uj = 2 * ujp + e_
                              mm(pbs[ob][:], Vs[uh % 2][:, uj, :], PT2[ppi][:, e_, :], uj == 0, lastp and e_ == 1,
                                 [f"Vs{uh % 2}", f"PT2_{ppi}"], PBK(ob))
                          if lastp:
                              a2i = ugi % 2
                              uqi = ugi % NQ
                              mm(pbs[6][:], onesf[:], accD[a2i][:], True, True, [f"accD{a2i}", "onesf"], PBK(6))
                              rcp(rcs[a2i][:], pbs[6][:], PBK(6), [f"rcs{a2i}"])
                              tt("dve", rcs[a2i][:], rcs[a2i][:], pbs[ob][:], ALU.mult, [f"rcs{a2i}"] + PBK(ob), [f"rcs{a2i}"])
                              tt("dve", yo[a2i][:], rcs[a2i][:], zt[uqi][:], ALU.mult, [f"rcs{a2i}", f"zt{uqi}"], [f"yo{a2i}"])
                              dma("sp", f"yo{a2i}", YT[uh, :, ug * 512:(ug + 1) * 512], yo[a2i][:], r=[f"yo{a2i}"], w=[("YT", uh, ug)])
                          if ug == 0 and ujp == 1 and uh + 1 < 4:
                              load_head(uh + 1)
                      if s_ < nU:
                          pk = f"PT2_{pi}"
                          psrc = pbs[b0][:].rearrange("p (o n) -> p o n", o=1)
                          act(PT2[pi][:, 0, :], pbs[b0][:], AF.Exp, PBK(b0), [pk], scale=SCL)
                          act(PT2[pi][:, 1, :], pbs[b0 + 1][:], AF.Exp, PBK(b0 + 1), [pk], scale=SCL)
                          if 2 * jp >= 4 * g:
                              m0 = 2 * jp - 4 * g
                              tt("pool", PT2[pi][:], PT2[pi][:], attm[:, m0:m0 + 2, :], ALU.mult, [pk, "attm"], [pk])
                          a2i = gi % 2
                          pb2 = pairb[s_ % 2]
                          pbk = f"pairb{s_ % 2}"
                          tt("dve", pb2[:], PT2[pi][:, 0, :], PT2[pi][:, 1, :], ALU.add, [pk], [pbk])
                          if not touched.get(gi, False):
                              cp("dve", accD[a2i][:], pb2[:], [pbk], [f"accD{a2i}"])
                          else:
                              tt("dve", accD[a2i][:], accD[a2i][:], pb2[:], ALU.add, [pbk, f"accD{a2i}"], [f"accD{a2i}"])
                          touched[gi] = True
                  P.barrier()
              if stop_after == "B":
                  break

              with ExitStack() as pc:
                  combs = sbuf(pc, "combs", [128, 512], F32)
                  combT = sbuf(pc, "combT", [128, 4, 8], F32)
                  tok4 = sbuf(pc, "tok4", [128, 4, 20], F32)
                  gcb = [sbuf(pc, f"gcb{h}", [128, 512], F32) for h in range(4)]
                  btb = [sbuf(pc, f"btb{h}", [128, 512], F32) for h in range(4)]
                  egb = [sbuf(pc, f"egb{h}", [128, 512], F32) for h in range(4)]
                  gqs = [sbuf(pc, f"gqs{h}", [128, 512], BF16) for h in range(4)]
                  gks = [sbuf(pc, f"gks{h}", [128, 512], BF16) for h in range(4)]
                  gvs = [sbuf(pc, f"gvs{h}", [128, 512], BF16) for h in range(4)]
                  zgs = [sbuf(pc, f"zgs{h}", [128, 512], BF16) for h in range(4)]
                  kbT = [sbuf(pc, f"kbT{h}", [128, 512], BF16) for h in range(4)]
                  qdT = [sbuf(pc, f"qdT{h}", [128, 512], BF16) for h in range(4)]
                  ygs = [sbuf(pc, f"ygs{h}", [128, 512], BF16) for h in range(4)]
                  Sf = [sbuf(pc, f"Sf{h}", [128, 128], F32) for h in range(4)]
                  Sb = [sbuf(pc, f"Sb{h}", [128, 128], BF16) for h in range(4)]

                  def mk(name, dt, n=1):
                      return [[sbuf(pc, f"{name}{h}_{i}", [128, 128], dt) for i in range(n)] for h in range(4)]

                  Dm = mk("Dm", F32)
                  DTs = mk("DTs", F32)
                  DTi = mk("DTi", F32)
                  Ab = mk("Ab", F32, 2)
                  Bb = mk("Bb", F32, 2)
                  Pb = mk("Pb", F32, 2)
                  QKT = mk("QKT", BF16)
                  TTb = mk("TTb", BF16)
                  kbg = mk("kbg", BF16)
                  kd = mk("kd", BF16, 2)
                  vb = mk("vb", BF16)
                  usb = mk("usb", F32)
                  wT = mk("wT", BF16)
                  vn = mk("vn", BF16)
                  on = mk("on", BF16)
                  ost2 = sbuf(pc, "ost2", [128, 4, 4], F32)
                  junk2 = sbuf(pc, "junk2", [64, 128], BF16)
                  memset("pool", combs[:], 0.0, ["combs"])
                  for h in range(4):
                      memset("pool", vn[h][0][:], 0.0, [f"vn{h}"])
                      memset("pool", on[h][0][:], 0.0, [f"on{h}"])
                  for h in range(4):
                      memset("pool", Sf[h][:], 0.0, [f"Sf{h}"])
                      memset("pool", Sb[h][:], 0.0, [f"Sb{h}"])
                  qctr = [0]

                  def nq():
                      i = qctr[0] % 24
                      qctr[0] += 1
                      b, q = i % 6, i // 6
                      nq.last = (b, q)
                      return pbs[b][:, q * 128:(q + 1) * 128], [("pb", b)]

                  def nqb():
                      ap, k = nq()
                      b, q = nq.last
                      return pbs[b][:].bitcast(BF16)[:, q * 256:q * 256 + 128], [("pb", b)]

                  for g in range(NG):
                      tok = slice(g * 512, (g + 1) * 512)
                      dma("sp", "combs", combs[0:8, :], COMB[:, tok], r=[("COMB", g)], w=["combs"])
                      for h in range(4):
                          dma("sp", f"gqs{h}", gqs[h][:], GQ[h, :, tok], r=[("GQ", h, g)], w=[f"gqs{h}"])
                          dma("sp", f"gks{h}", gks[h][:], GK[h, :, tok], r=[("GK", h, g)], w=[f"gks{h}"])
                          dma("sp", f"gvs{h}", gvs[h][:], GV[h, :, tok], r=[("GV", h, g)], w=[f"gvs{h}"])
                          dma("sp", f"zgs{h}", zgs[h][:], ZG[h, :, tok], r=[("ZG", g)], w=[f"zgs{h}"])
                      for h in range(4):
                          mm(pbs[6][:], sel[:, h, :], combs[:], True, True, ["sel", "combs"], PBK(6))
                          cp("act", gcb[h][:], pbs[6][:], PBK(6), [f"gcb{h}"])
                          act(egb[h][:], pbs[6][:], AF.Exp, PBK(6), [f"egb{h}"])
                          mm(pbs[7][:], sel[:, 4 + h, :], combs[:], True, True, ["sel", "combs"], PBK(7))
                          cp("act", btb[h][:], pbs[7][:], PBK(7), [f"btb{h}"])
                          tt("dve", kbT[h][:], gks[h][:], btb[h][:], ALU.mult, [f"gks{h}", f"btb{h}"], [f"kbT{h}"])
                          tt("pool", qdT[h][:], gqs[h][:], egb[h][:], ALU.mult, [f"gqs{h}", f"egb{h}"], [f"qdT{h}"])
                      for t in range(4):
                          ap, k = nq()
                          tr(ap, combs[:, t * 128:(t + 1) * 128], identf[:], ["combs", "identf"], k)
                          cp("dve", combT[:, t, :], ap[:, 0:8], k, [("combT", t)])
                          act(tok4[:, t, 0:4], combT[:, t, 0:4], AF.Exp, [("combT", t)], [("tok4", t)])
                          tt("dve", tok4[:, t, 4:8], combT[:, t, 4:8], tok4[:, t, 0:4], ALU.mult, [("combT", t), ("tok4", t)], [("tok4", t)])
                          for h in range(4):
                              for c2 in range(2):
                                  rs = slice(c2 * 64, (c2 + 1) * 64)
                                  lc = t * 128 + c2 * 64 + 63
                                  tt("dve", tok4[rs, t, 8 + h:9 + h], gcb[h][rs, lc:lc + 1], combT[rs, t, h:h + 1], ALU.subtract,
                                     [f"gcb{h}", ("combT", t)], [("tok4", t)])
                          act(tok4[:, t, 8:12], tok4[:, t, 8:12], AF.Exp, [("tok4", t)], [("tok4", t)])
                          tsc("dve", tok4[:, t, 12:16], tok4[:, t, 8:12], pmask[:, 0:1], None, ALU.mult, None, [("tok4", t), "pmask"], [("tok4", t)])
                          tsc("dve", tok4[:, t, 16:20], tok4[:, t, 8:12], pmask[:, 1:2], None, ALU.mult, None, [("tok4", t), "pmask"], [("tok4", t)])

                      for t in range(4):
                          ts_ = slice(t * 128, (t + 1) * 128)
                          HH = range(4)
                          for h in HH:
                              tsc("dve", Dm[h][0][:], gcb[h][:, ts_], combT[:, t, h:h + 1], 0.0, ALU.subtract, ALU.max,
                                  [f"gcb{h}", ("combT", t)], [f"Dm{h}"])
                              act(Dm[h][0][:], Dm[h][0][:], AF.Exp, [f"Dm{h}"], [f"Dm{h}"], scale=-1.0)
                              tsc("dve", DTs[h][0][:], gcb[h][:, ts_], combT[:, t, h:h + 1], 0.0, ALU.subtract, ALU.min,
                                  [f"gcb{h}", ("combT", t)], [f"DTs{h}"])
                              act(DTs[h][0][:], DTs[h][0][:], AF.Exp, [f"DTs{h}"], [f"DTs{h}"])
                              tt("pool", Dm[h][0][:], Dm[h][0][:], gmask[:, 0, :], ALU.mult, [f"Dm{h}", "gmask"], [f"Dm{h}"])
                              tt("pool", DTi[h][0][:], DTs[h][0][:], gmask[:, 2, :], ALU.mult, [f"DTs{h}", "gmask"], [f"DTi{h}"])
                              tt("pool", DTs[h][0][:], DTs[h][0][:], gmask[:, 1, :], ALU.mult, [f"DTs{h}", "gmask"], [f"DTs{h}"])
                          for h in HH:
                              ap, k = nq()
                              mm(ap, kbT[h][:, ts_], gks[h][:, ts_], True, True, [f"kbT{h}", f"gks{h}"], k)
                              tt("dve", Ab[h][0][:], ap, Dm[h][0][:], ALU.mult, k + [f"Dm{h}"], [f"Ab{h}_0"])
                              ap, k = nq()
                              mm(ap, gks[h][:, ts_], kbT[h][:, ts_], True, True, [f"kbT{h}", f"gks{h}"], k)
                              tt("dve", Bb[h][0][:], ap, DTs[h][0][:], ALU.mult, k + [f"DTs{h}"], [f"Bb{h}_0"])
                              ap, k = nq()
                              mm(ap, gks[h][:, ts_], gqs[h][:, ts_], True, True, [f"gqs{h}", f"gks{h}"], k)
                              tt("dve", QKT[h][0][:], ap, DTi[h][0][:], ALU.mult, k + [f"DTi{h}"], [f"QKT{h}"])
                              tt("pool", Pb[h][0][:], identf[:], Bb[h][0][:], ALU.subtract, ["identf", f"Bb{h}_0"], [f"Pb{h}_0"])
                          for kk in range(1, 6):
                              ci, co = (kk - 1) % 2, kk % 2
                              for h in HH:
                                  ap, k = nq()
                                  mm(ap, Bb[h][ci][:], Ab[h][ci][:], True, True, [f"Bb{h}_{ci}", f"Ab{h}_{ci}"], k)
                                  cp("act", Ab[h][co][:], ap, k, [f"Ab{h}_{co}"])
                                  if kk < 5:
                                      ap2, k2 = nq()
                                      mm(ap2, Ab[h][ci][:], Bb[h][ci][:], True, True, [f"Bb{h}_{ci}", f"Ab{h}_{ci}"], k2)
                                      cp("act", Bb[h][co][:], ap2, k2, [f"Bb{h}_{co}"])
                              for h in HH:
                                  ap, k = nq()
                                  mm(ap, Ab[h][co][:], Pb[h][ci][:], True, True, [f"Ab{h}_{co}", f"Pb{h}_{ci}"], k)
                                  tt("dve", Pb[h][co][:], ap, Pb[h][ci][:], ALU.add, k + [f"Pb{h}_{ci}"], [f"Pb{h}_{co}"])
                          PF = 5 % 2
                          for h in HH:
                              cp("act", TTb[h][0][:], Pb[h][PF][:], [f"Pb{h}_{PF}"], [f"TTb{h}"])
                              ap, k = nqb()
                              tr(ap, gks[h][:, ts_], identb[:], [f"gks{h}", "identb"], k)
                              tsc("dve", kbg[h][0][:], ap, tok4[:, t, 4 + h:5 + h], None, ALU.mult, None, k + [("tok4", t)], [f"kbg{h}"])
                              tsc("dve", kd[h][0][:], ap, tok4[:, t, 12 + h:13 + h], None, ALU.mult, None, k + [("tok4", t)], [f"kd{h}"])
                              tsc("dve", kd[h][1][:], ap, tok4[:, t, 16 + h:17 + h], None, ALU.mult, None, k + [("tok4", t)], [f"kd{h}"])
                              ap, k = nqb()
                              tr(ap, gvs[h][:, ts_], identb[:], [f"gvs{h}", "identb"], k)
                              tsc("dve", vb[h][0][:], ap, combT[:, t, 4 + h:5 + h], None, ALU.mult, None, k + [("combT", t)], [f"vb{h}"])
                          for h in HH:
                              ap, k = nq()
                              mm(ap, TTb[h][0][:], vb[h][0][:], True, True, [f"TTb{h}", f"vb{h}"], k)
                              cp("act", usb[h][0][:], ap, k, [f"usb{h}"])
                              ap, k = nq()
                              mm(ap, kbg[h][0][:], TTb[h][0][:], True, True, [f"TTb{h}", f"kbg{h}"], k)
                              cp("act", wT[h][0][:], ap, k, [f"wT{h}"])
                          yps = {}
                          for h in HH:
                              yps[h] = (pbs[7][:].bitcast(BF16)[:, h * 256:(h + 1) * 256].rearrange("p (c t) -> p c t", c=2), [("pb", 7)])
                          for c2 in range(2):
                              rs = slice(c2 * 64, (c2 + 1) * 64)
                              cs_ = slice(t * 128 + c2 * 64, t * 128 + (c2 + 1) * 64)
                              lc = t * 128 + c2 * 64 + 63
                              wsp = {}
                              for h in HH:
                                  ap, k = nq()
                                  mm(ap[0:64, :], wT[h][0][:, rs], Sb[h][:], True, True, [f"wT{h}", f"Sb{h}"], k)
                                  wsp[h] = (ap, k)
                              for h in HH:
                                  ap, k = wsp[h]
                                  tt("dve", vn[h][0][rs, :], usb[h][0][rs, :], ap[0:64, :], ALU.subtract, k + [f"usb{h}"], [f"vn{h}"])
                              osp = {}
                              for h in HH:
                                  ap, k = nq()
                                  mm(ap[0:64, :], qdT[h][:, cs_], Sb[h][:], True, False, [f"qdT{h}", f"Sb{h}"], k)
                                  mm(ap[0:64, :], QKT[h][0][:, rs], vn[h][0][:, :], False, True, [f"QKT{h}", f"vn{h}"], k)
                                  osp[h] = (ap, k)
                                  ap2, k2 = nq()
                                  mm(ap2, kd[h][c2][:, :], vn[h][0][:, :], True, True, [f"kd{h}", f"vn{h}"], k2)
                                  stt("dve", Sf[h][:], Sf[h][:], egb[h][:, lc:lc + 1], ap2, ALU.mult, ALU.add, k2 + [f"Sf{h}", f"egb{h}"], [f"Sf{h}"])
                                  cp("act", Sb[h][:], Sf[h][:], [f"Sf{h}"], [f"Sb{h}"])
                              for h in HH:
                                  ap, k = osp[h]
                                  act(junk2[:], ap[0:64, :], AF.Square, k, ["junk2", ("ost2", h)], accum=ost2[0:64, h, 0:1])
                                  rsqrt(ost2[0:64, h, 1:2], ost2[0:64, h, 0:1], 1.0 / 128.0, epsc[0:64, 0:1], [("ost2", h), "epsc"], [("ost2", h)])
                                  tsc("dve", on[h][0][0:64, :], ap[0:64, :], ost2[0:64, h, 1:2], None, ALU.mult, None, k + [("ost2", h)], [f"on{h}"])
                                  yap, yk = yps[h]
                                  tr(yap[:, c2, :], on[h][0][:, :], identb[:], [f"on{h}", "identb"], yk)
                          for h in HH:
                              yap, yk = yps[h]
                              stt("dve", ygs[h][:, ts_].rearrange("p (c t) -> p c t", c=2), yap[:, :, 0:64], cols[:, l, 29:30],
                                  zgs[h][:, ts_].rearrange("p (c t) -> p c t", c=2), ALU.mult, ALU.mult,
                                  yk + [("cols", l), f"zgs{h}"], [f"ygs{h}"])
                      for h in range(4):
                          dma("sp", f"ygs{h}", YT[4 + h, :, tok], ygs[h][:], r=[f"ygs{h}"], w=[("YT", 4 + h, g)])
                  P.barrier()
              if stop_after == "C":
                  break

              with ExitStack() as pd:
                  WOb = sbuf(pd, "WOb", [128, 8, D], BF16)
                  stg = [sbuf(pd, f"stgd{i}", [128, D], F32) for i in range(2)]
                  for c in range(8):
                      sk = f"stgd{c % 2}"
                      dma("sp", sk, stg[c % 2][:], w_out[l, c * 128:(c + 1) * 128, :], w=[sk])
                      cp("dve" if c % 2 == 0 else "pool", WOb[:, c, :], stg[c % 2][:], [sk], ["WOb"])
                  NS = 3
                  yt = [sbuf(pd, f"yt{i}", [128, 8, 128], BF16) for i in range(NS)]
                  xd = [sbuf(pd, f"xd{i}", [128, D], F32) for i in range(NS)]
                  xo = [sbuf(pd, f"xo{i}", [128, D], F32) for i in range(NS)]
                  junk3 = sbuf(pd, "junk3", [128, D], BF16)
                  st3 = sbuf(pd, "st3", [128, NS, 4], F32)

                  def d_load(ti):
                      if ti >= NT:
                          return
                      g = ti // 4
                      i2 = ti % NS
                      dma("sp", f"yt{i2}", yt[i2][:], YT[:, :, ti * 128:(ti + 1) * 128].rearrange("c p t -> p c t"),
                          r=[("YT", c, g) for c in range(8)], w=[f"yt{i2}"])
                      dma("sp", f"xd{i2}", xd[i2][:], xsrc[ti * 128:(ti + 1) * 128, :], r=[(xkey, ti)], w=[f"xd{i2}"])

                  def d_mm(ti):
                      i2 = ti % NS
                      b0 = 2 * (ti % 4)
                      for half in range(2):
                          for c in range(8):
                              mm(pbs[b0 + half][:], yt[i2][:, c, :], WOb[:, c, half * 512:(half + 1) * 512], c == 0, c == 7,
                                 [f"yt{i2}", "WOb"], PBK(b0 + half))

                  def d_epi(ti):
                      i2 = ti % NS
                      b0 = 2 * (ti % 4)
                      sk3 = ("st3", i2)
                      for half in range(2):
                          act(junk3[:, half * 512:(half + 1) * 512], pbs[b0 + half][:], AF.Square, PBK(b0 + half), ["junk3", sk3],
                              accum=st3[:, i2, half:half + 1])
                      tt("dve", st3[:, i2, 2:3], st3[:, i2, 0:1], st3[:, i2, 1:2], ALU.add, [sk3], [sk3])
                      rsqrt(st3[:, i2, 3:4], st3[:, i2, 2:3], 1.0 / D, epsc[:, 0:1], [sk3, "epsc"], [sk3])
                      for half in range(2):
                          hs = slice(half * 512, (half + 1) * 512)
                          stt("dve", xo[i2][:, hs], pbs[b0 + half][:], st3[:, i2, 3:4], gpb[:, hs], ALU.mult, ALU.mult,
                              PBK(b0 + half) + [sk3, "gpb"], [f"xo{i2}"])
                      tt("dve", xo[i2][:], xo[i2][:], xd[i2][:], ALU.add, [f"xo{i2}", f"xd{i2}"], [f"xo{i2}"])
                      dma("sp", f"xo{i2}", xdst[ti * 128:(ti + 1) * 128, :], xo[i2][:], r=[f"xo{i2}"], w=[(xdkey, ti)])

                  d_load(0)
                  d_load(1)
                  for ti in range(NT + 1):
                      if ti < NT:
                          d_mm(ti)
                      if ti >= 1:
                          d_epi(ti - 1)
                      d_load(ti + 2)
                  P.barrier()
        except _Stop:
            pass
        P.final_wait("sp", [("OUT", ti) for ti in range(NT)])
        build.stats = (dict(P.n_inst), dict(P.n_wait), P.nsem)
    return nc


def make_consts():
    ident = np.eye(128, dtype=np.float32)
    attm = np.zeros((128, 4, 512), np.float32)
    k = np.arange(128)[:, None]
    q = np.arange(512)[None, :]
    for j in range(4):
        attm[:, j, :] = (q >= 128 * j + k).astype(np.float32)
    i = np.arange(128)[:, None]
    jj = np.arange(128)[None, :]
    same = (i // 64) == (jj // 64)
    gm = np.zeros((128, 3, 128), np.float32)
    gm[:, 0, :] = (same & (i > jj))
    gm[:, 1, :] = (same & (i < jj))
    gm[:, 2, :] = (same & (i <= jj))
    sel = np.zeros((128, 8, 128), np.float32)
    for r in range(8):
        sel[r, r, :] = 1.0
    small = np.zeros((64, 4), np.float32)
    half = 32
    invf = np.power(np.float32(10000.0), -np.arange(half, dtype=np.float32) * np.float32(2.0) / np.float32(64)).astype(np.float32)
    small[:, 0] = np.concatenate([invf, invf])
    small[:32, 1] = -1.0
    small[32:, 1] = 1.0
    small[0:4, 2] = -1.0
    small[4:8, 3] = 1.0
    return dict(c_ident=ident, c_attm=attm.reshape(128, 2048), c_gmask=gm.reshape(128, 384), c_sel=sel.reshape(128, 1024), c_small=small,
                c_pm=np.stack([(np.arange(128) < 64), (np.arange(128) >= 64)], axis=1).astype(np.float32))


def make_in_map(inputs, b, S, NL):
    f = lambda a: np.ascontiguousarray(np.asarray(a))
    m = {}
    m["x"] = f(inputs["x"][b, :S])
    m["ccol"] = f(np.asarray(inputs["c"][b]).reshape(8, 128).T)
    m["posrep"] = f(np.broadcast_to(np.asarray(inputs["positions"][b, :S])[None, :], (64, S))).astype(np.int32)
    m["w_mod"] = f(inputs["w_mod"][:NL])
    m["b_mod"] = f(inputs["b_mod"][:NL])
    m["prew"] = f(inputs["pre_norm_w"][:NL])
    m["postw"] = f(inputs["post_norm_w"][:NL])
    m["w_in"] = f(inputs["w_in"][:NL])
    m["qnw"] = f(inputs["mla_q_norm_w"][:NL])
    m["q_up"] = f(inputs["mla_q_up"][:NL])
    m["kvnw"] = f(inputs["mla_kv_norm_w"][:NL])
    m["kv_up"] = f(inputs["mla_kv_up"][:NL])
    m["convw"] = f(np.asarray(inputs["gdn_conv_w"][:NL]).reshape(NL, 4 * 1536))
    a8 = np.zeros((8, NL), np.float32)
    a8[0:4, :] = np.asarray(inputs["gdn_a_log"][:NL]).T
    d8 = np.zeros((8, NL), np.float32)
    d8[0:4, :] = np.asarray(inputs["gdn_dt_bias"][:NL]).T
    m["alog8"] = a8
    m["dtb8"] = d8
    m["onw"] = f(inputs["gdn_o_norm_w"][:NL])
    m["w_out"] = f(inputs["w_out"][:NL])
    m.update(make_consts())
    return m


_NC_CACHE = {}


def kernel(**inputs):
    B, S, _ = inputs["x"].shape
    NL = inputs["w_in"].shape[0]
    key = (S, NL)
    if key not in _NC_CACHE:
        _NC_CACHE[key] = build(S, NL)
    nc = _NC_CACHE[key]
    in_maps = [make_in_map(inputs, c % B, S, NL) for c in range(8)]
    res = run_bass_kernel_spmd(nc, in_maps, core_ids=list(range(8)))
    outs = [np.asarray(res.results[b]["out"]) for b in range(B)]
    return np.stack(outs, axis=0).astype(np.float32)
```

```python
import math
import os
CUT = int(os.environ.get('KCUT', '99'))
import numpy as np
from contextlib import ExitStack
import concourse.bass as bass
import concourse.mybir as mybir
from concourse.bass_utils import run_bass_kernel_spmd

F32 = mybir.dt.float32
BF16 = mybir.dt.bfloat16
I32 = mybir.dt.int32
ALU = mybir.AluOpType
AF = mybir.ActivationFunctionType

D = 1024
NIN = 3272
NWB = 3336
EPS = 1e-6
ENGS = ("pe", "act", "dve", "pool", "sp")
SEM_ROT = 30000
HMAP = {"pe": "tensor", "act": "scalar", "dve": "vector", "pool": "gpsimd", "sp": "sync"}


class _Stop(Exception):
    pass


class Prog:
    def __init__(self, nc, stack):
        self.nc = nc
        self.stack = stack
        self.cur_sem = {}
        self.cur_cnt = {e: 0 for e in ENGS}
        self.nsem = 0
        self.all_eng_sems = {e: [] for e in ENGS}
        for e in ENGS:
            self._new_eng_sem(e)
        self.known = {e: {} for e in ENGS}
        self.last_write = {}
        self.readers = {}
        self.dma_sems = {}
        self.n_inst = {e: 0 for e in ENGS}
        self.n_wait = {e: 0 for e in ENGS}

    def _sem(self, name):
        s = self.stack.enter_context(self.nc.semaphore(name))
        self.nsem += 1
        return s

    def _new_eng_sem(self, e):
        self.cur_sem[e] = self._sem(f"c_{e}_{self.nsem}")
        self.cur_cnt[e] = 0

    def _deps(self, eng, reads, writes):
        deps = {}

        def add(ev):
            if ev is None:
                return
            s, v = ev
            k = id(s)
            if k not in deps or deps[k][1] < v:
                deps[k] = (s, v)

        for r in reads:
            add(self.last_write.get(r))
        for w in writes:
            add(self.last_write.get(w))
            rd = self.readers.get(w)
            if rd:
                for ev in rd.values():
                    add(ev)
        out = []
        kn = self.known[eng]
        own = id(self.cur_sem[eng])
        for k, (s, v) in deps.items():
            if k == own and eng == "pe":
                continue
            if kn.get(k, 0) >= v:
                continue
            kn[k] = v
            out.append((s, v))
        return out

    def _record(self, ev, reads, writes):
        for r in reads:
            self.readers.setdefault(r, {})[id(ev[0])] = ev
        for w in writes:
            self.last_write[w] = ev
            self.readers[w] = {}

    def _emit_now(self, eng, waits, fn, sem, inc):
        e = getattr(self.nc, HMAP[eng])
        for (s, v) in waits:
            e.wait_ge(s, v)
        self.n_wait[eng] += len(waits)
        if fn is not None:
            fn(e).then_inc(sem, inc)
            self.n_inst[eng] += 1

    def op(self, eng, fn, reads=(), writes=()):
        waits = self._deps(eng, reads, writes)
        if self.cur_cnt[eng] >= SEM_ROT:
            self._new_eng_sem(eng)
        sem = self.cur_sem[eng]
        self.cur_cnt[eng] += 1
        val = self.cur_cnt[eng]
        self._emit_now(eng, waits, fn, sem, 1)
        self._record((sem, val), reads, writes)

    def dma(self, q, semname, out, in_, reads=(), writes=(), **kw):
        if semname not in self.dma_sems:
            self.dma_sems[semname] = [self._sem("d_" + semname), 0]
        ent = self.dma_sems[semname]
        waits = self._deps(q, reads, writes)
        ent[1] += 16
        sem, val = ent[0], ent[1]
        self._emit_now(q, waits, lambda e: e.dma_start(out=out, in_=in_, **kw), sem, 16)
        self._record((sem, val), reads, writes)

    def barrier(self):
        evs = [(self.cur_sem[e], self.cur_cnt[e]) for e in ENGS if self.cur_cnt[e] > 0]
        evs += [(s, v) for (s, v) in self.dma_sems.values() if v > 0]
        for eng in ENGS:
            kn = self.known[eng]
            own = id(self.cur_sem[eng])
            waits = []
            for (s, v) in evs:
                if id(s) == own or kn.get(id(s), 0) >= v:
                    continue
                kn[id(s)] = v
                waits.append((s, v))
            self._emit_now(eng, waits, None, None, 0)

    def final_wait(self, eng, keys):
        waits = self._deps(eng, keys, ())
        self._emit_now(eng, waits, None, None, 0)


def build(S, NL, dbg=False, stop_after=None):
    nc = bass.Bass("TRN2", target_bir_lowering=False)
    NG = S // 512
    NT = S // 128

    def din(name, shape, dt=F32):
        return nc.dram_tensor(name, list(shape), dt, kind="ExternalInput").ap()

    def dscr(name, shape, dt):
        return nc.dram_tensor(name, list(shape), dt, kind="ExternalOutput" if dbg else "Internal").ap()

    x_in = din("x", [S, D])
    ccol = din("ccol", [128, 8])
    posrep = din("posrep", [64, S], I32)
    w_mod = din("w_mod", [NL, D, 3 * D])
    b_mod = din("b_mod", [NL, 3 * D])
    prew = din("prew", [NL, D])
    postw = din("postw", [NL, D])
    w_in = din("w_in", [NL, D, NIN])
    qnw = din("qnw", [NL, 384])
    q_up = din("q_up", [NL, 384, 768])
    kvnw = din("kvnw", [NL, 256])
    kv_up = din("kv_up", [NL, 256, 1024])
    convw = din("convw", [NL, 4 * 1536])
    alog8 = din("alog8", [8, NL])
    dtb8 = din("dtb8", [8, NL])
    onw = din("onw", [NL, 128])
    w_out = din("w_out", [NL, D, D])
    c_ident = din("c_ident", [128, 128])
    c_attm = din("c_attm", [128, 4 * 512])
    c_gmask = din("c_gmask", [128, 3 * 128])
    c_sel = din("c_sel", [128, 8 * 128])
    c_pm = din("c_pm", [128, 2])
    c_small = din("c_small", [64, 4])
    out = nc.dram_tensor("out", [S, D], F32, kind="ExternalOutput").ap()

    XR = dscr("XR", [S, D], F32)
    QT = dscr("QT", [4, 192, S], BF16)
    KT = dscr("KT", [4, 128, S], BF16)
    KPE = dscr("KPE", [64, S], BF16)
    VV = dscr("VV", [S, 512], BF16)
    ZM = dscr("ZM", [4, 128, S], BF16)
    ZG = dscr("ZG", [4, 128, S], BF16)
    GQ = dscr("GQ", [4, 128, S], BF16)
    GK = dscr("GK", [4, 128, S], BF16)
    GV = dscr("GV", [4, 128, S], BF16)
    COMB = dscr("COMB", [8, S], F32)
    YT = dscr("YT", [8, 128, S], BF16)
    COS = dscr("COS", [64, S], F32)
    SIN = dscr("SIN", [64, S], F32)
    GP = dscr("GP", [NL, D], F32)

    with ExitStack() as st:
        P = Prog(nc, st)

        uid = [0]

        def sbuf(stack, name, shape, dt):
            uid[0] += 1
            return stack.enter_context(nc.sbuf_tensor(f"{name}_u{uid[0]}", list(shape), dt))

        def mm(o, lhsT, rhs, start, stop, r, w):
            P.op("pe", lambda e: e.matmul(o, lhsT=lhsT, rhs=rhs, start=start, stop=stop), r, w)

        def tr(o, i, ident, r, w):
            P.op("pe", lambda e: e.transpose(o, i, ident), r, w)

        def act(o, i, func, r, w, bias=None, scale=None, accum=None):
            kw = {}
            if bias is not None:
                kw["bias"] = bias
            if scale is not None:
                kw["scale"] = scale
            if accum is not None:
                kw["accum_out"] = accum
            P.op("act", lambda e: e.activation(out=o, in_=i, func=func, **kw), r, w)

        def cp(eng, o, i, r, w):
            if eng == "act":
                P.op("act", lambda e: e.activation(out=o, in_=i, func=AF.Copy), r, w)
            else:
                P.op(eng, lambda e: e.tensor_copy(out=o, in_=i), r, w)

        def tsc(eng, o, i, s1, s2, op0, op1, r, w):
            if op1 is None:
                P.op(eng, lambda e: e.tensor_scalar(out=o, in0=i, scalar1=s1, scalar2=None, op0=op0), r, w)
            else:
                P.op(eng, lambda e: e.tensor_scalar(out=o, in0=i, scalar1=s1, scalar2=s2, op0=op0, op1=op1), r, w)

        def tt(eng, o, a, b, op, r, w):
            P.op(eng, lambda e: e.tensor_tensor(out=o, in0=a, in1=b, op=op), r, w)

        def stt(eng, o, a, sc, b, op0, op1, r, w):
            eng = "dve"
            P.op(eng, lambda e: e.scalar_tensor_tensor(out=o, in0=a, scalar=sc, in1=b, op0=op0, op1=op1), r, w)

        def rcp(o, i, r, w):
            P.op("dve", lambda e: e.reciprocal(out=o, in_=i), r, w)

        def memset(eng, o, val, w):
            P.op(eng, lambda e: e.memset(o, val), (), w)

        def dma(q, sem, o, i, r=(), w=(), **kw):
            P.dma(q, sem, o, i, reads=r, writes=w, **kw)

        def rsqrt(o, i, scale, bias_ap, r, w):
            act(o, i, AF.Ln, r, w, bias=bias_ap, scale=scale)
            act(o, o, AF.Exp, w, w, scale=-0.5)

        pbs = [st.enter_context(nc.psum_tensor(f"pb{i}", [128, 512], F32)) for i in range(8)]

        def PBK(i):
            return [("pb", i)]

        identf = sbuf(st, "identf", [128, 128], F32)
        identb = sbuf(st, "identb", [128, 128], BF16)
        onesf = sbuf(st, "onesf", [128, 128], F32)
        attm = sbuf(st, "attm", [128, 4, 512], BF16)
        gmask = sbuf(st, "gmask", [128, 3, 128], F32)
        sel = sbuf(st, "sel", [128, 8, 128], F32)
        pmask = sbuf(st, "pmask", [128, 2], F32)
        small = sbuf(st, "small", [64, 4], F32)
        epsc = sbuf(st, "epsc", [128, 1], F32)
        cols = sbuf(st, "cols", [128, NL, 80], F32)
        nA = sbuf(st, "nA", [8, NL], F32)
        dtb = sbuf(st, "dtb", [8, NL], F32)
        gpb = sbuf(st, "gpb", [128, D], F32)

        dma("sp", "c0", identf[:], c_ident, w=["identf"])
        dma("sp", "c1", gmask[:].rearrange("p m j -> p (m j)"), c_gmask, w=["gmask"])
        dma("sp", "c2", sel[:].rearrange("p m j -> p (m j)"), c_sel, w=["sel"])
        dma("sp", "c3", small[:], c_small, w=["small"])
        dma("sp", "c3b", pmask[:], c_pm, w=["pmask"])
        dma("sp", "c4", nA[:], alog8, w=["nA"])
        dma("sp", "c5", dtb[:], dtb8, w=["dtb"])
        cp("dve", identb[:], identf[:], ["identf"], ["identb"])
        memset("pool", onesf[:], 1.0, ["onesf"])
        memset("pool", epsc[:], EPS, ["epsc"])
        act(nA[:], nA[:], AF.Exp, ["nA"], ["nA"])
        tsc("dve", nA[:], nA[:], small[0:8, 2:3], None, ALU.mult, None, ["nA", "small"], ["nA"])

        with ExitStack() as ps_:
            amst = sbuf(ps_, "amst", [128, 4 * 512], F32)
            dma("sp", "c6", amst[:], c_attm, w=["amst"])
            cp("dve", attm[:].rearrange("p m j -> p (m j)"), amst[:], ["amst"], ["attm"])

            cact = sbuf(ps_, "cact", [128, 8], F32)
            dma("sp", "c7", cact[:], ccol, w=["cact"])
            act(cact[:], cact[:], AF.Silu, ["cact"], ["cact"])
            NROW = 3072 + 1792 + 6144
            rowbuf = sbuf(ps_, "rowbuf", [1, NROW], F32)
            bmrow = sbuf(ps_, "bmrow", [1, 3072], F32)
            pwrow = sbuf(ps_, "pwrow", [1, D], F32)
            wmst = [sbuf(ps_, f"wmst{i}", [128, 8, 512], F32) for i in range(2)]
            for l in range(NL):
                dma("sp", "r0", bmrow[:], b_mod[l:l + 1, :], w=["bmrow"])
                dma("sp", "r1", rowbuf[0:1, 3072:4096], prew[l:l + 1, :], w=["rowbuf_s"])
                dma("sp", "r1", rowbuf[0:1, 4096:4480], qnw[l:l + 1, :], w=["rowbuf_s"])
                dma("sp", "r1", rowbuf[0:1, 4480:4736], kvnw[l:l + 1, :], w=["rowbuf_s"])
                dma("sp", "r1", rowbuf[0:1, 4736:4864], onw[l:l + 1, :], w=["rowbuf_s"])
                dma("sp", "r1", rowbuf[0:1, 4864:NROW], convw[l:l + 1, :], w=["rowbuf_s"])
                dma("sp", "r2", pwrow[:], postw[l:l + 1, :], w=["pwrow"])
                for cg in range(6):
                    ws_ = wmst[cg % 2]
                    wk = f"wmst{cg % 2}"
                    dma("sp", wk, ws_[:], w_mod[l, :, cg * 512:(cg + 1) * 512].rearrange("(c p) n -> p c n", p=128), w=[wk])
                    pb = pbs[cg % 2]
                    for c in range(8):
                        mm(pb[0:1, :], cact[:, c:c + 1], ws_[:, c, :], c == 0, c == 7, [wk, "cact"], PBK(cg % 2))
                    tt("dve", rowbuf[0:1, cg * 512:(cg + 1) * 512], pb[0:1, :], bmrow[0:1, cg * 512:(cg + 1) * 512], ALU.add,
                       PBK(cg % 2) + ["bmrow"], ["rowbuf_m"])
                nchunk = 16 + 62
                for j in range(nchunk):
                    off = j * 128 if j < 16 else 3072 + (j - 16) * 128
                    mm(pbs[2][:, j:j + 1], rowbuf[0:1, off:off + 128], onesf[0:1, 0:1], True, True,
                       ["rowbuf_m", "rowbuf_s", "onesf"], PBK(2))
                cp("dve", cols[:, l, 0:nchunk], pbs[2][:, 0:nchunk], PBK(2), [("cols", l)])
                stt("dve", cols[:, l, 8:16], cols[:, l, 8:16], 1.0, cols[:, l, 16:24], ALU.add, ALU.mult, [("cols", l)], [("cols", l)])
                tt("dve", pwrow[:], pwrow[:], rowbuf[0:1, 2048:3072], ALU.mult, ["pwrow", "rowbuf_m"], ["pwrow"])
                dma("sp", "r3", GP[l:l + 1, :], pwrow[:], r=["pwrow"], w=[("GP", l)])

            CH = min(2048, S)
            posi = sbuf(ps_, "posi", [64, CH], I32)
            ang = sbuf(ps_, "ang", [64, CH], F32)
            a2 = sbuf(ps_, "a2", [64, CH], F32)
            kf = sbuf(ps_, "kf", [64, CH], F32)
            ki = sbuf(ps_, "ki", [64, CH], I32)
            fx = sbuf(ps_, "fx", [64, CH], F32)
            tab = [sbuf(ps_, f"tab{i}", [64, CH], F32) for i in range(2)]
            C1 = 6.28125
            C2 = 2.0 * math.pi - C1
            for ch in range(S // CH):
                cs = slice(ch * CH, (ch + 1) * CH)
                dma("sp", "rp0", posi[:], posrep[:, cs], w=["posi"])
                cp("dve", ang[:], posi[:], ["posi"], ["ang"])
                tsc("dve", ang[:], ang[:], small[:, 0:1], None, ALU.mult, None, ["ang", "small"], ["ang"])
                for ti, offv in ((0, math.pi / 2), (1, 0.0)):
                    tsc("dve", a2[:], ang[:], offv, None, ALU.add, None, ["ang"], ["a2"])
                    tsc("dve", ki[:], a2[:], 1.0 / (2 * math.pi), None, ALU.mult, None, ["a2"], ["ki"])
                    cp("dve", kf[:], ki[:], ["ki"], ["kf"])
                    stt("dve", a2[:], kf[:], -C1, a2[:], ALU.mult, ALU.add, ["kf", "a2"], ["a2"])
                    stt("dve", a2[:], kf[:], -C2, a2[:], ALU.mult, ALU.add, ["kf", "a2"], ["a2"])
                    tsc("dve", fx[:], a2[:], math.pi, 2 * math.pi, ALU.is_gt, ALU.mult, ["a2"], ["fx"])
                    tt("dve", a2[:], a2[:], fx[:], ALU.subtract, ["a2", "fx"], ["a2"])
                    tsc("dve", fx[:], a2[:], -math.pi, 2 * math.pi, ALU.is_lt, ALU.mult, ["a2"], ["fx"])
                    tt("dve", a2[:], a2[:], fx[:], ALU.add, ["a2", "fx"], ["a2"])
                    tsc("dve", a2[:], a2[:], math.pi, -math.pi, ALU.min, ALU.max, ["a2"], ["a2"])
                    tk = f"tab{ti}"
                    act(tab[ti][:], a2[:], AF.Sin, ["a2"], [tk])
                    if ti == 1:
                        tsc("dve", tab[ti][:], tab[ti][:], small[:, 1:2], None, ALU.mult, None, [tk, "small"], [tk])
                    dma("sp", "rp" + tk, (COS if ti == 0 else SIN)[:, cs], tab[ti][:], r=[tk], w=[("ROPE", ti, ch)])
            P.barrier()
        ROPE_KEYS = [("ROPE", ti, ch) for ti in range(2) for ch in range(S // min(2048, S))]

        def chk(tag):
            if stop_after == tag:
                P.barrier()
                raise _Stop()

        try:
          for l in range(NL if stop_after != "P" else 0):
              xsrc = x_in if l == 0 else XR
              xdst = out if l == NL - 1 else XR
              xkey = "XIN" if l == 0 else "XR"
              xdkey = "OUT" if l == NL - 1 else "XR"
              dma("sp", "gpb", gpb[:], GP[l:l + 1, :].partition_broadcast(128), r=[("GP", l)], w=["gpb"])
              chk("G")

              with ExitStack() as pa:
                  Wb = sbuf(pa, "Wb", [128, 8, NWB], BF16)
                  Qb = sbuf(pa, "Qb", [128, 3, 4, 256], BF16)
                  KVb = sbuf(pa, "KVb", [128, 2, 1024], BF16)
                  HS = NIN // 2
                  pa_w = ExitStack()
                  stg = [sbuf(pa_w, f"stg{i}", [128, HS], F32) for i in range(2)]
                  si = 0
                  ceng = ["dve", "pool"]
                  for c in range(8):
                      for hf in range(2):
                          s_ = stg[si % 2]
                          sk = f"stg{si % 2}"
                          dma("sp", sk, s_[:], w_in[l, c * 128:(c + 1) * 128, hf * HS:(hf + 1) * HS], w=[sk])
                          lo, hi = hf * HS, (hf + 1) * HS
                          for (a, b, dst) in ((0, 704, 0), (672, 704, 704), (640, 672, 736), (704, NIN, 768)):
                              a2_, b2_ = max(a, lo), min(b, hi)
                              if a2_ >= b2_:
                                  continue
                              d0 = dst + (a2_ - a)
                              cp(ceng[si % 2], Wb[:, c, d0:d0 + (b2_ - a2_)], s_[:, a2_ - lo:b2_ - lo], [sk], [("Wb", c)])
                          si += 1
                  for c in range(3):
                      s_ = stg[si % 2]
                      sk = f"stg{si % 2}"
                      dma("sp", sk, s_[:, 0:768], q_up[l, c * 128:(c + 1) * 128, :], w=[sk])
                      sv = s_[:, 0:768].rearrange("p (h f) -> p h f", h=4)
                      qs = cols[:, l, 24 + c:25 + c]
                      for (a, b, dst) in ((0, 128, 0), (128, 192, 128), (160, 192, 192), (128, 160, 224)):
                          tsc("dve", Qb[:, c, :, dst:dst + (b - a)], sv[:, :, a:b], qs, None, ALU.mult, None, [sk, ("cols", l)], ["Qb"])
                      si += 1
                  for c in range(2):
                      s_ = stg[si % 2]
                      sk = f"stg{si % 2}"
                      dma("sp", sk, s_[:, 0:1024], kv_up[l, c * 128:(c + 1) * 128, :], w=[sk])
                      tsc("dve", KVb[:, c, :], s_[:, 0:1024], cols[:, l, 27 + c:28 + c], None, ALU.mult, None, [sk, ("cols", l)], ["KVb"])
                      si += 1
                  WBK = [("Wb", c) for c in range(8)]
                  P.barrier()
                  pa_w.close()
                  NGr = 0 if stop_after == "W" else (int(stop_after[1:]) if (stop_after or "").startswith("g") else NG)

                  xt = [sbuf(pa, f"xt{i}", [128, D], F32) for i in range(2)]
                  xs = [sbuf(pa, f"xs{i}", [128, D], F32) for i in range(4)]
                  junk = sbuf(pa, "junk", [128, D], BF16)
                  st1 = sbuf(pa, "st1", [128, 8], F32)
                  hT = [sbuf(pa, f"hT{i}", [128, 8, 512], BF16) for i in range(2)]
                  qlT = sbuf(pa, "qlT", [128, 3, 512], BF16)
                  kvT = sbuf(pa, "kvT", [128, 2, 512], BF16)
                  sq = [sbuf(pa, f"sq{i}", [128, 512], F32) for i in range(3)]
                  rq = sbuf(pa, "rq", [128, 512], F32)
                  rkv = sbuf(pa, "rkv", [128, 512], F32)
                  rkvt = sbuf(pa, "rkvt", [128, 4], F32)
                  cst = sbuf(pa, "cst", [64, 512], F32)
                  sst = sbuf(pa, "sst", [64, 512], F32)
                  crr = sbuf(pa, "crr", [64, 512], F32)
                  srr = sbuf(pa, "srr", [64, 512], F32)
                  t1 = [sbuf(pa, f"t1_{i}", [64, 512], F32) for i in range(1)] * 2
                  t2 = [sbuf(pa, f"t2_{i}", [64, 512], F32) for i in range(1)] * 2
                  ost = [sbuf(pa, f"ost{i}", [128, 512], BF16) for i in range(6)]
                  vst = [sbuf(pa, f"vst{i}", [128, 512], BF16) for i in range(2)]
                  zst = [sbuf(pa, f"zst{i}", [128, 4, 512], BF16) for i in range(2)]
                  convd = sbuf(pa, "convd", [128, 48, 128], BF16)
                  cvb = [sbuf(pa, f"cvb{i}", [128, 515], BF16) for i in range(3)]
                  cvc = sbuf(pa, "cvc", [128, 12, 3], BF16)
                  sact = [sbuf(pa, f"sact{i}", [128, 512], F32) for i in range(8)]
                  rn = [sbuf(pa, f"rn{i}", [128, 512], F32) for i in range(2)]
                  abw = [sbuf(pa, f"abw{i}", [8, 512], F32) for i in range(4)]
                  memset("pool", cvc[:], 0.0, ["cvc"])
                  for jj in range(4):
                      for ch in range(12):
                          tsc("dve", convd[:, jj * 12 + ch, :], identb[:], cols[:, l, 30 + jj * 12 + ch:31 + jj * 12 + ch], None,
                              ALU.mult, None, ["identb", ("cols", l)], ["convd"])
                  octr = [0]
                  pbr = [2]

                  def nextpb():
                      b = pbr[0]
                      pbr[0] = 2 + (pbr[0] - 2 + 1) % 6
                      return b

                  def nextost():
                      i = octr[0] % 6
                      octr[0] += 1
                      return i

                  def xstat(g, t):
                      ti = g * 4 + t
                      x_ = xt[ti % 2]
                      xk = f"xt{ti % 2}"
                      dma("sp", xk, x_[:], xsrc[ti * 128:(ti + 1) * 128, :], r=[(xkey, ti)], w=[xk])
                      act(junk[:], x_[:], AF.Square, [xk], ["junk", "st1"], accum=st1[:, 0:1])
                      rsqrt(st1[:, 1:2], st1[:, 0:1], 1.0 / D, epsc[:, 0:1], ["st1", "epsc"], ["st1"])
                      act(xs[t][:], x_[:], AF.Copy, [xk, "st1"], [f"xs{t}"], scale=st1[:, 1:2])

                  def xtrans(g, t):
                      hh_ = hT[g % 2]
                      hhk = f"hT{g % 2}"
                      for c in range(8):
                          tr(pbs[c // 4][:, (c % 4) * 128:(c % 4 + 1) * 128], xs[t][:, c * 128:(c + 1) * 128], identf[:],
                             [f"xs{t}", "identf"], [("pb", c // 4)])
                      for c in range(8):
                          tsc("dve", hh_[:, c, t * 128:(t + 1) * 128], pbs[c // 4][:, (c % 4) * 128:(c % 4 + 1) * 128],
                              cols[:, l, 8 + c:9 + c], cols[:, l, c:c + 1], ALU.mult, ALU.add,
                              [("pb", c // 4), ("cols", l)], [hhk])

                  for t in range(4):
                      if NGr > 0:
                          xstat(0, t)
                          xtrans(0, t)
                  for g in range(NGr):
                      tok = slice(g * 512, (g + 1) * 512)
                      h_ = hT[g % 2]
                      hk = f"hT{g % 2}"
                      dma("sp", "cst", cst[:], COS[:, tok], r=ROPE_KEYS, w=["cst"])
                      dma("sp", "sst", sst[:], SIN[:, tok], r=ROPE_KEYS, w=["sst"])

                      def proj(col0, ncol):
                          b = nextpb()
                          for c in range(8):
                              mm(pbs[b][0:ncol, :], Wb[:, c, col0:col0 + ncol], h_[:, c, :], c == 0, c == 7, [hk, ("Wb", c)], PBK(b))
                          return b

                      def nxt(t):
                          return

                      def nstat(t):
                          if g + 1 < NGr:
                              xstat(g + 1, t)

                      def ntrans(t):
                          if g + 1 < NGr:
                              xtrans(g + 1, t)

                      for c in range(3):
                          b = proj(c * 128, 128)
                          cp("act", qlT[:, c, :], pbs[b][:], PBK(b), ["qlT"])
                          act(sq[c][:], pbs[b][:], AF.Square, PBK(b), [f"sq{c}"])
                      bsum = nextpb()
                      for c in range(3):
                          mm(pbs[bsum][:], onesf[:], sq[c][:], c == 0, c == 2, [f"sq{c}", "onesf"], PBK(bsum))
                      rsqrt(rq[:], pbs[bsum][:], 1.0 / 384.0, epsc[:, 0:1], PBK(bsum) + ["epsc"], ["rq"])
                      for c in range(2):
                          b = proj(384 + c * 128, 128)
                          cp("act", kvT[:, c, :], pbs[b][:], PBK(b), ["kvT"])
                          act(sq[c][:], pbs[b][:], AF.Square, PBK(b), [f"sq{c}"])
                      bsum = nextpb()
                      for c in range(2):
                          mm(pbs[bsum][:], onesf[:], sq[c][:], c == 0, c == 1, [f"sq{c}", "onesf"], PBK(bsum))
                      bt_ = nextpb()
                      for t in range(4):
                          for c in range(2):
                              mm(pbs[bt_][:, 8 + t:9 + t], sq[c][:, t * 128:(t + 1) * 128], onesf[:, 0:1],
                                 c == 0, c == 1, [f"sq{c}", "onesf"], PBK(bt_))
                      rsqrt(rkv[:], pbs[bsum][:], 1.0 / 256.0, epsc[:, 0:1], PBK(bsum) + ["epsc"], ["rkv"])
                      rsqrt(rkvt[:], pbs[bt_][:, 8:12], 1.0 / 256.0, epsc[:, 0:1], PBK(bt_) + ["epsc"], ["rkvt"])
                      tt("dve", crr[:], cst[:], rq[0:64, :], ALU.mult, ["cst", "rq"], ["crr"])
                      tt("dve", srr[:], sst[:], rq[0:64, :], ALU.mult, ["sst", "rq"], ["srr"])
                      nxt(0)
                      for h in range(4):
                          b = nextpb()
                          for c in range(3):
                              mm(pbs[b][:], Qb[:, c, h, 0:128], qlT[:, c, :], c == 0, c == 2, ["Qb", "qlT"], PBK(b))
                          oi = nextost()
                          tt("dve", ost[oi][:], pbs[b][:], rq[:], ALU.mult, PBK(b) + ["rq"], [f"ost{oi}"])
                          dma("sp", f"ost{oi}", QT[h, 0:128, tok], ost[oi][:], r=[f"ost{oi}"], w=[("QT", h, g)])
                          b = nextpb()
                          for c in range(3):
                              mm(pbs[b][:], Qb[:, c, h, 128:256], qlT[:, c, :], c == 0, c == 2, ["Qb", "qlT"], PBK(b))
                          tt("dve", t1[0][:], pbs[b][0:64, :], crr[:], ALU.mult, PBK(b) + ["crr"], ["t1_0"])
                          tt("dve", t2[0][:], pbs[b][64:128, :], srr[:], ALU.mult, PBK(b) + ["srr"], ["t2_0"])
                          oi = nextost()
                          tt("pool", ost[oi][0:64, :], t1[0][:], t2[0][:], ALU.add, ["t1_0", "t2_0"], [f"ost{oi}"])
                          dma("sp", f"ost{oi}", QT[h, 128:192, tok], ost[oi][0:64, :], r=[f"ost{oi}"], w=[("QT", h, g)])
                      nxt(1)
                      for h in range(4):
                          b = nextpb()
                          for c in range(2):
                              mm(pbs[b][:], KVb[:, c, h * 256:h * 256 + 128], kvT[:, c, :], c == 0, c == 1, ["KVb", "kvT"], PBK(b))
                          oi = nextost()
                          tt("dve", ost[oi][:], pbs[b][:], rkv[:], ALU.mult, PBK(b) + ["rkv"], [f"ost{oi}"])
                          dma("sp", f"ost{oi}", KT[h, :, tok], ost[oi][:], r=[f"ost{oi}"], w=[("KT", h, g)])
                      nxt(2)
                      kvv = KVb[:].rearrange("p c (h f) -> p c h f", h=4)
                      for t in range(4):
                          b = nextpb()
                          for c in range(2):
                              mm(pbs[b][:].rearrange("p (h f) -> p h f", h=4), kvT[:, c, t * 128:(t + 1) * 128], kvv[:, c, :, 128:256],
                                 c == 0, c == 1, ["KVb", "kvT"], PBK(b))
                          vi = (g * 4 + t) % 2
                          tsc("dve", vst[vi][:], pbs[b][:], rkvt[:, t:t + 1], None, ALU.mult, None, PBK(b) + ["rkvt"], [f"vst{vi}"])
                          dma("sp", f"vst{vi}", VV[g * 512 + t * 128:g * 512 + (t + 1) * 128, :], vst[vi][:], r=[f"vst{vi}"], w=[("VV", g)])
                      b = proj(640, 128)
                      tt("dve", t1[0][:], pbs[b][0:64, :], cst[:], ALU.mult, PBK(b) + ["cst"], ["t1_0"])
                      tt("dve", t2[0][:], pbs[b][64:128, :], sst[:], ALU.mult, PBK(b) + ["sst"], ["t2_0"])
                      oi = nextost()
                      tt("pool", ost[oi][0:64, :], t1[0][:], t2[0][:], ALU.add, ["t1_0", "t2_0"], [f"ost{oi}"])
                      dma("sp", f"ost{oi}", KPE[:, tok], ost[oi][0:64, :], r=[f"ost{oi}"], w=[("KPE", g)])
                      nxt(3)
                      for (zi, base, dstD, zk) in ((0, 768, ZM, "ZM"), (1, 2824, ZG, "ZG")):
                          for c in range(4):
                              b = proj(base + c * 128, 128)
                              act(zst[zi][:, c, :], pbs[b][:], AF.Silu, PBK(b), [f"zst{zi}"])
                          dma("sp", f"zst{zi}", dstD[:, :, tok].rearrange("c p t -> p c t"), zst[zi][:], r=[f"zst{zi}"], w=[(zk, g)])
                          nstat(zi)
                      def conv_pe(ch):
                          sl = ch % 3
                          b2 = nextpb()
                          for j in range(4):
                              mm(pbs[b2][:], convd[:, j * 12 + ch, :], cvb[sl][:, j:j + 512], j == 0, j == 3, ["convd", f"cvb{sl}"], PBK(b2))
                          hh = ch % 4
                          if ch >= 8:
                              oi = nextost()
                              act(ost[oi][:], pbs[b2][:], AF.Silu, PBK(b2), [f"ost{oi}"])
                              dma("sp", f"ost{oi}", GV[hh, :, tok], ost[oi][:], r=[f"ost{oi}"], w=[("GV", hh, g)])
                          else:
                              act(sact[ch][:], pbs[b2][:], AF.Silu, PBK(b2), [f"sact{ch}"])

                      for ch in range(12):
                          sl = ch % 3
                          b = proj(1280 + ch * 128, 128)
                          cp("act", cvb[sl][:, 3:515], pbs[b][:], PBK(b), [f"cvb{sl}"])
                          cp("pool", cvb[sl][:, 0:3], cvc[:, ch, :], ["cvc"], [f"cvb{sl}"])
                          cp("pool", cvc[:, ch, :], cvb[sl][:, 512:515], [f"cvb{sl}"], ["cvc"])
                          if ch >= 1:
                              conv_pe(ch - 1)
                          if ch == 3:
                              nstat(2)
                          if ch == 7:
                              nstat(3)
                      conv_pe(11)
                      def l2sq(ch):
                          if ch < 8:
                              act(sq[ch % 3][:], sact[ch][:], AF.Square, [f"sact{ch}"], [f"sq{ch % 3}"])
                      l2sq(0)
                      l2sq(1)
                      for ch in range(8):
                          l2sq(ch + 2)
                          b2 = nextpb()
                          mm(pbs[b2][:], onesf[:], sq[ch % 3][:], True, True, [f"sq{ch % 3}", "onesf"], PBK(b2))
                          r_ = rn[ch % 2]
                          rk_ = f"rn{ch % 2}"
                          rsqrt(r_[:], pbs[b2][:], 1.0, epsc[:, 0:1], PBK(b2) + ["epsc"], [rk_])
                          oi = nextost()
                          sc_ = (128.0 ** -0.5) if ch < 4 else 1.0
                          stt("dve", ost[oi][:], sact[ch][:], sc_, r_[:], ALU.mult, ALU.mult, [f"sact{ch}", rk_], [f"ost{oi}"])
                          hh = ch % 4
                          dma("sp", f"ost{oi}", (GQ if ch < 4 else GK)[hh, :, tok], ost[oi][:], r=[f"ost{oi}"],
                              w=[("GQ" if ch < 4 else "GK", hh, g)])
                          if ch % 2 == 1:
                              ntrans(ch // 2)
                      b = proj(2816, 8)
                      bet, ea, ga, gb2 = abw
                      act(bet[:], pbs[b][0:8, :], AF.Exp, PBK(b), ["abw0"], scale=-1.0)
                      tsc("dve", bet[:], bet[:], 1.0, None, ALU.add, None, ["abw0"], ["abw0"])
                      rcp(bet[:], bet[:], ["abw0"], ["abw0"])
                      act(ea[:], pbs[b][0:8, :], AF.Exp, PBK(b) + ["dtb"], ["abw1"], bias=dtb[:, l:l + 1])
                      act(ea[:], ea[:], AF.Ln, ["abw1"], ["abw1"], bias=1.0)
                      tsc("pool", ga[:], ea[:], nA[:, l:l + 1], None, ALU.mult, None, ["abw1", "nA"], ["abw2"])
                      src_, sk_, dst_, dk2 = ga, "abw2", gb2, "abw3"
                      for s_ in (1, 2, 4, 8, 16, 32):
                          sv = src_[:].rearrange("p (n c) -> p n c", c=64)
                          dv = dst_[:].rearrange("p (n c) -> p n c", c=64)
                          cp("pool", dv[:, :, 0:s_], sv[:, :, 0:s_], [sk_], [dk2])
                          tt("pool", dv[:, :, s_:64], sv[:, :, s_:64], sv[:, :, 0:64 - s_], ALU.add, [sk_], [dk2])
                          src_, sk_, dst_, dk2 = dst_, dk2, src_, sk_
                      stt("dve", bet[:], bet[:], small[0:8, 3:4], src_[:], ALU.mult, ALU.add, ["abw0", sk_, "small"], ["abw0"])
                      dma("sp", "abw0", COMB[:, tok], bet[:], r=["abw0"], w=[("COMB", g)])
                  P.barrier()
              if stop_after == "A" or stop_after == "W" or (stop_after or "").startswith("g"):
                  break

              with ExitStack() as pb_:
                  KTs = [sbuf(pb_, f"KTs{i}", [128, S], BF16) for i in range(2)]
                  KPs = sbuf(pb_, "KPs", [128, S], BF16)
                  Vs = [sbuf(pb_, f"Vs{i}", [128, NT, 128], BF16) for i in range(2)]
                  NQ = 3
                  NPT = 6
                  LAG = 2
                  qn = [sbuf(pb_, f"qn{i}", [128, 512], BF16) for i in range(NQ)]
                  qp = [sbuf(pb_, f"qp{i}", [128, 512], BF16) for i in range(NQ)]
                  zt = [sbuf(pb_, f"zt{i}", [128, 512], BF16) for i in range(NQ)]
                  PT = [sbuf(pb_, f"PT{i}", [128, 512], BF16) for i in range(NPT)]
                  accD = [sbuf(pb_, f"accD{i}", [128, 512], F32) for i in range(2)]
                  accP = [sbuf(pb_, f"accP{i}", [128, 512], F32) for i in range(2)]
                  rcs = [sbuf(pb_, f"rcs{i}", [128, 512], F32) for i in range(2)]
                  yo = [sbuf(pb_, f"yo{i}", [128, 512], BF16) for i in range(2)]
                  SCL = 192.0 ** -0.5
                  allg = list(range(NG))
                  memset("pool", KPs[64:128, :], 0.0, ["KPs"])
                  for i in range(NQ):
                      memset("pool", qp[i][64:128, :], 0.0, [f"qp{i}"])
                  dma("sp", "KPs", KPs[0:64, :], KPE, r=[("KPE", g) for g in allg], w=["KPs"])

                  def load_head(h):
                      i = h % 2
                      dma("sp", f"KTs{i}", KTs[i][:], KT[h], r=[("KT", h, g) for g in allg], w=[f"KTs{i}"])
                      vsrc = VV[:, h * 128:(h + 1) * 128].rearrange("(t p) f -> p t f", p=128)
                      nsp = max(1, NT // 16)
                      for ii in range(nsp):
                          tsl = slice(ii * (NT // nsp), (ii + 1) * (NT // nsp))
                          dma("sp", f"Vs{i}", Vs[i][:, tsl, :], vsrc[:, tsl, :], r=[("VV", g) for g in allg], w=[f"Vs{i}"])

                  units = []
                  groups = []
                  for h in range(4):
                      for g in range(NG):
                          gi = len(groups)
                          groups.append((h, g))
                          for jp in range(2 * (g + 1)):
                              units.append((h, g, jp, gi))
                  nU = len(units)
                  loaded = set()
                  touched = {}
                  NP2 = 4
                  PT2 = [sbuf(pb_, f"PT2_{i}", [128, 2, 512], BF16) for i in range(NP2)]
                  pairb = [sbuf(pb_, f"pairb{i}", [128, 512], BF16) for i in range(2)]

                  def ensure_group(gi):
                      if gi in loaded or gi >= len(groups):
                          return
                      loaded.add(gi)
                      h, g = groups[gi]
                      tok = slice(g * 512, (g + 1) * 512)
                      qi = gi % NQ
                      dma("sp", f"qn{qi}", qn[qi][:], QT[h, 0:128, tok], r=[("QT", h, g)], w=[f"qn{qi}"])
                      dma("sp", f"qp{qi}", qp[qi][0:64, :], QT[h, 128:192, tok], r=[("QT", h, g)], w=[f"qp{qi}"])
                      dma("sp", f"zt{qi}", zt[qi][:], ZM[h, :, tok], r=[("ZM", g)], w=[f"zt{qi}"])

                  load_head(0)
                  LAGP = 1
                  for s_ in range(nU + LAGP):
                      if s_ < nU:
                          h, g, jp, gi = units[s_]
                          if jp == 0:
                              ensure_group(gi)
                              ensure_group(gi + 1)
                          qi = gi % NQ
                          hb = h % 2
                          b0 = 2 * (s_ % 2)
                          pi = s_ % NP2
                          for e_ in range(2):
                              j = 2 * jp + e_
                              ks = slice(j * 128, (j + 1) * 128)
                              mm(pbs[b0 + e_][:], KTs[hb][:, ks], qn[qi][:], True, False, [f"KTs{hb}", f"qn{qi}"], PBK(b0 + e_))
                              mm(pbs[b0 + e_][:], KPs[:, ks], qp[qi][:], False, True, ["KPs", f"qp{qi}"], PBK(b0 + e_))
                      if s_ - LAGP >= 0:
                          uh, ug, ujp, ugi = units[s_ - LAGP]
                          ob = 4 + ugi % 2
                          ppi = (s_ - LAGP) % NP2
                          lastp = ujp == 2 * (ug + 1) - 1
                          for e_ in range(2):
                              uj = 2 * ujp + e_
                              mm(pbs[ob][:], Vs[uh % 2][:, uj, :], PT2[ppi][:, e_, :], uj == 0, lastp and e_ == 1,
                                 [f"Vs{uh % 2}", f"PT2_{ppi}"], PBK(ob))
                          if lastp:
                              a2i = ugi % 2
                              uqi = ugi % NQ
                              mm(pbs[6][:], onesf[:], accD[a2i][:], True, True, [f"accD{a2i}", "onesf"], PBK(6))
                              rcp(rcs[a2i][:], pbs[6][:], PBK(6), [f"rcs{a2i}"])
                              tt("dve", rcs[a2i][:], rcs[a2i][:], pbs[ob][:], ALU.mult, [f"rcs{a2i}"] + PBK(ob), [f"rcs{a2i}"])
                              tt("dve", yo[a2i][:], rcs[a2i][:], zt[uqi][:], ALU.mult, [f"rcs{a2i}", f"zt{uqi}"], [f"yo{a2i}"])
                              dma("sp", f"yo{a2i}", YT[uh, :, ug * 512:(ug + 1) * 512], yo[a2i][:], r=[f"yo{a2i}"], w=[("YT", uh, ug)])
                          if ug == 0 and ujp == 1 and uh + 1 < 4:
                              load_head(uh + 1)
                      if s_ < nU:
                          pk = f"PT2_{pi}"
                          psrc = pbs[b0][:].rearrange("p (o n) -> p o n", o=1)
                          act(PT2[pi][:, 0, :], pbs[b0][:], AF.Exp, PBK(b0), [pk], scale=SCL)
                          act(PT2[pi][:, 1, :], pbs[b0 + 1][:], AF.Exp, PBK(b0 + 1), [pk], scale=SCL)
                          if 2 * jp >= 4 * g:
                              m0 = 2 * jp - 4 * g
                              tt("pool", PT2[pi][:], PT2[pi][:], attm[:, m0:m0 + 2, :], ALU.mult, [pk, "attm"], [pk])
                          a2i = gi % 2
                          pb2 = pairb[s_ % 2]
                          pbk = f"pairb{s_ % 2}"
                          tt("dve", pb2[:], PT2[pi][:, 0, :], PT2[pi][:, 1, :], ALU.add, [pk], [pbk])
                          if not touched.get(gi, False):
                              cp("dve", accD[a2i][:], pb2[:], [pbk], [f"accD{a2i}"])
                          else:
                              tt("dve", accD[a2i][:], accD[a2i][:], pb2[:], ALU.add, [pbk, f"accD{a2i}"], [f"accD{a2i}"])
                          touched[gi] = True
                  P.barrier()
              if stop_after == "B":
                  break

              with ExitStack() as pc:
                  combs = sbuf(pc, "combs", [128, 512], F32)
                  combT = sbuf(pc, "combT", [128, 4, 8], F32)
                  tok4 = sbuf(pc, "tok4", [128, 4, 20], F32)
                  gcb = [sbuf(pc, f"gcb{h}", [128, 512], F32) for h in range(4)]
                  btb = [sbuf(pc, f"btb{h}", [128, 512], F32) for h in range(4)]
                  gqs = [sbuf(pc, f"gqs{h}", [128, 512], BF16) for h in range(4)]
                  gks = [sbuf(pc, f"gks{h}", [128, 512], BF16) for h in range(4)]
                  gvs = [sbuf(pc, f"gvs{h}", [128, 512], BF16) for h in range(4)]
                  kbT = [sbuf(pc, f"kbT{h}", [128, 512], BF16) for h in range(4)]
                  egb = [[sbuf(pc, f"egb{h}_{p}", [128, 512], F32) for p in range(2)] for h in range(4)]
                  zgs = [[sbuf(pc, f"zgs{h}_{p}", [128, 512], BF16) for p in range(2)] for h in range(4)]
                  qdT = [[sbuf(pc, f"qdT{h}_{p}", [128, 512], BF16) for p in range(2)] for h in range(4)]
                  ygs = [[sbuf(pc, f"ygs{h}_{p}", [128, 512], BF16) for p in range(2)] for h in range(4)]
                  Sf = [sbuf(pc, f"Sf{h}", [128, 128], F32) for h in range(4)]
                  Sb = [sbuf(pc, f"Sb{h}", [128, 128], BF16) for h in range(4)]

                  def mk(name, dt, n=1):
                      return [[sbuf(pc, f"{name}{h}_{i}", [128, 128], dt) for i in range(n)] for h in range(4)]

                  Dm = mk("Dm", F32)
                  DTs = mk("DTs", F32)
                  DTi = mk("DTi", F32)
                  Ab = mk("Ab", F32, 2)
                  Bb = mk("Bb", F32, 2)
                  Pb = mk("Pb", F32, 2)
                  TTb = mk("TTb", BF16)
                  kbg = mk("kbg", BF16)
                  vb = mk("vb", BF16)
                  QKT = mk("QKT", BF16, 2)
                  kd = mk("kd", BF16, 4)
                  usb = mk("usb", F32, 2)
                  wT = mk("wT", BF16, 2)
                  vn = mk("vn", BF16)
                  on = mk("on", BF16)
                  ost2 = sbuf(pc, "ost2", [128, 4, 4], F32)
                  junk2 = sbuf(pc, "junk2", [64, 128], BF16)
                  memset("pool", combs[:], 0.0, ["combs"])
                  for h in range(4):
                      memset("pool", vn[h][0][:], 0.0, [f"vn{h}"])
                      memset("pool", on[h][0][:], 0.0, [f"on{h}"])
                      memset("pool", Sf[h][:], 0.0, [f"Sf{h}"])
                      memset("pool", Sb[h][:], 0.0, [f"Sb{h}"])
                  pctr = [0]
                  sctr = [0]

                  def nq():
                      i = pctr[0] % 16
                      pctr[0] += 1
                      b, q = i % 4, i // 4
                      nq.last = (b, q)
                      return pbs[b][:, q * 128:(q + 1) * 128], [("pb", b)]

                  def nqb():
                      ap, k = nq()
                      b, q = nq.last
                      return pbs[b][:].bitcast(BF16)[:, q * 256:q * 256 + 128], [("pb", b)]

                  def sq_():
                      i = sctr[0] % 12
                      sctr[0] += 1
                      b, q = 4 + i % 3, i // 3
                      return pbs[b][:, q * 128:(q + 1) * 128], [("pb", b)]

                  HH = range(4)

                  def setup(g):
                      gp = g % 2
                      tok = slice(g * 512, (g + 1) * 512)
                      dma("sp", "combs", combs[0:8, :], COMB[:, tok], r=[("COMB", g)], w=["combs"])
                      for h in HH:
                          dma("sp", f"gqs{h}", gqs[h][:], GQ[h, :, tok], r=[("GQ", h, g)], w=[f"gqs{h}"])
                          dma("sp", f"gks{h}", gks[h][:], GK[h, :, tok], r=[("GK", h, g)], w=[f"gks{h}"])
                          dma("sp", f"gvs{h}", gvs[h][:], GV[h, :, tok], r=[("GV", h, g)], w=[f"gvs{h}"])
                          dma("sp", f"zgs{h}_{gp}", zgs[h][gp][:], ZG[h, :, tok], r=[("ZG", g)], w=[f"zgs{h}_{gp}"])
                      for h in HH:
                          b1, b2 = (2 * h) % 4, (2 * h + 1) % 4
                          mm(pbs[b1][:], sel[:, h, :], combs[:], True, True, ["sel", "combs"], PBK(b1))
                          cp("act", gcb[h][:], pbs[b1][:], PBK(b1), [f"gcb{h}"])
                          act(egb[h][gp][:], pbs[b1][:], AF.Exp, PBK(b1), [f"egb{h}_{gp}"])
                          mm(pbs[b2][:], sel[:, 4 + h, :], combs[:], True, True, ["sel", "combs"], PBK(b2))
                          cp("act", btb[h][:], pbs[b2][:], PBK(b2), [f"btb{h}"])
                          tt("dve", kbT[h][:], gks[h][:], btb[h][:], ALU.mult, [f"gks{h}", f"btb{h}"], [f"kbT{h}"])
                          tt("pool", qdT[h][gp][:], gqs[h][:], egb[h][gp][:], ALU.mult, [f"gqs{h}", f"egb{h}_{gp}"], [f"qdT{h}_{gp}"])
                      for t in range(4):
                          ap, k = nq()
                          tr(ap, combs[:, t * 128:(t + 1) * 128], identf[:], ["combs", "identf"], k)
                          cp("dve", combT[:, t, :], ap[:, 0:8], k, [("combT", t)])
                          act(tok4[:, t, 0:4], combT[:, t, 0:4], AF.Exp, [("combT", t)], [("tok4", t)])
                          tt("dve", tok4[:, t, 4:8], combT[:, t, 4:8], tok4[:, t, 0:4], ALU.mult, [("combT", t), ("tok4", t)], [("tok4", t)])
                          for h in HH:
                              for c2 in range(2):
                                  rs = slice(c2 * 64, (c2 + 1) * 64)
                                  lc = t * 128 + c2 * 64 + 63
                                  tt("dve", tok4[rs, t, 8 + h:9 + h], gcb[h][rs, lc:lc + 1], combT[rs, t, h:h + 1], ALU.subtract,
                                     [f"gcb{h}", ("combT", t)], [("tok4", t)])
                          act(tok4[:, t, 8:12], tok4[:, t, 8:12], AF.Exp, [("tok4", t)], [("tok4", t)])
                          tsc("dve", tok4[:, t, 12:16], tok4[:, t, 8:12], pmask[:, 0:1], None, ALU.mult, None, [("tok4", t), "pmask"], [("tok4", t)])
                          tsc("dve", tok4[:, t, 16:20], tok4[:, t, 8:12], pmask[:, 1:2], None, ALU.mult, None, [("tok4", t), "pmask"], [("tok4", t)])

                  def par_steps(g, t):
                      tp = (g * 4 + t) % 2
                      ts_ = slice(t * 128, (t + 1) * 128)
                      steps = []

                      def p0():
                          for h in HH:
                              tsc("dve", Dm[h][0][:], gcb[h][:, ts_], combT[:, t, h:h + 1], 0.0, ALU.subtract, ALU.max,
                                  [f"gcb{h}", ("combT", t)], [f"Dm{h}"])
                              act(Dm[h][0][:], Dm[h][0][:], AF.Exp, [f"Dm{h}"], [f"Dm{h}"], scale=-1.0)
                              tsc("dve", DTs[h][0][:], gcb[h][:, ts_], combT[:, t, h:h + 1], 0.0, ALU.subtract, ALU.min,
                                  [f"gcb{h}", ("combT", t)], [f"DTs{h}"])
                              act(DTs[h][0][:], DTs[h][0][:], AF.Exp, [f"DTs{h}"], [f"DTs{h}"])
                              tt("pool", Dm[h][0][:], Dm[h][0][:], gmask[:, 0, :], ALU.mult, [f"Dm{h}", "gmask"], [f"Dm{h}"])
                              tt("pool", DTi[h][0][:], DTs[h][0][:], gmask[:, 2, :], ALU.mult, [f"DTs{h}", "gmask"], [f"DTi{h}"])
                              tt("pool", DTs[h][0][:], DTs[h][0][:], gmask[:, 1, :], ALU.mult, [f"DTs{h}", "gmask"], [f"DTs{h}"])
                          for h in HH:
                              ap, k = nq()
                              mm(ap, kbT[h][:, ts_], gks[h][:, ts_], True, True, [f"kbT{h}", f"gks{h}"], k)
                              tt("dve", Ab[h][0][:], ap, Dm[h][0][:], ALU.mult, k + [f"Dm{h}"], [f"Ab{h}_0"])
                              ap, k = nq()
                              mm(ap, gks[h][:, ts_], kbT[h][:, ts_], True, True, [f"kbT{h}", f"gks{h}"], k)
                              tt("dve", Bb[h][0][:], ap, DTs[h][0][:], ALU.mult, k + [f"DTs{h}"], [f"Bb{h}_0"])
                              ap, k = nq()
                              mm(ap, gks[h][:, ts_], gqs[h][:, ts_], True, True, [f"gqs{h}", f"gks{h}"], k)
                              tt("dve", QKT[h][tp][:], ap, DTi[h][0][:], ALU.mult, k + [f"DTi{h}"], [f"QKT{h}_{tp}"])
                              tt("pool", Pb[h][0][:], identf[:], Bb[h][0][:], ALU.subtract, ["identf", f"Bb{h}_0"], [f"Pb{h}_0"])
                      steps.append(p0)

                      def mklevel(kk):
                          def lev():
                              ci, co = (kk - 1) % 2, kk % 2
                              for h in HH:
                                  ap, k = nq()
                                  mm(ap, Bb[h][ci][:], Ab[h][ci][:], True, True, [f"Bb{h}_{ci}", f"Ab{h}_{ci}"], k)
                                  cp("act", Ab[h][co][:], ap, k, [f"Ab{h}_{co}"])
                                  if kk < 5:
                                      ap2, k2 = nq()
                                      mm(ap2, Ab[h][ci][:], Bb[h][ci][:], True, True, [f"Bb{h}_{ci}", f"Ab{h}_{ci}"], k2)
                                      cp("act", Bb[h][co][:], ap2, k2, [f"Bb{h}_{co}"])
                              for h in HH:
                                  ap, k = nq()
                                  mm(ap, Ab[h][co][:], Pb[h][ci][:], True, True, [f"Ab{h}_{co}", f"Pb{h}_{ci}"], k)
                                  tt("dve", Pb[h][co][:], ap, Pb[h][ci][:], ALU.add, k + [f"Pb{h}_{ci}"], [f"Pb{h}_{co}"])
                          return lev
                      for kk in range(1, 6):
                          steps.append(mklevel(kk))

                      def p6():
                          PF = 5 % 2
                          for h in HH:
                              cp("act", TTb[h][0][:], Pb[h][PF][:], [f"Pb{h}_{PF}"], [f"TTb{h}"])
                              ap, k = nqb()
                              tr(ap, gks[h][:, ts_], identb[:], [f"gks{h}", "identb"], k)
                              tsc("dve", kbg[h][0][:], ap, tok4[:, t, 4 + h:5 + h], None, ALU.mult, None, k + [("tok4", t)], [f"kbg{h}"])
                              tsc("dve", kd[h][2 * tp][:], ap, tok4[:, t, 12 + h:13 + h], None, ALU.mult, None, k + [("tok4", t)], [f"kd{h}_{tp}"])
                              tsc("dve", kd[h][2 * tp + 1][:], ap, tok4[:, t, 16 + h:17 + h], None, ALU.mult, None, k + [("tok4", t)], [f"kd{h}_{tp}"])
                              ap, k = nqb()
                              tr(ap, gvs[h][:, ts_], identb[:], [f"gvs{h}", "identb"], k)
                              tsc("dve", vb[h][0][:], ap, combT[:, t, 4 + h:5 + h], None, ALU.mult, None, k + [("combT", t)], [f"vb{h}"])
                          for h in HH:
                              ap, k = nq()
                              mm(ap, TTb[h][0][:], vb[h][0][:], True, True, [f"TTb{h}", f"vb{h}"], k)
                              cp("act", usb[h][tp][:], ap, k, [f"usb{h}_{tp}"])
                              ap, k = nq()
                              mm(ap, kbg[h][0][:], TTb[h][0][:], True, True, [f"TTb{h}", f"kbg{h}"], k)
                              cp("act", wT[h][tp][:], ap, k, [f"wT{h}_{tp}"])
                      steps.append(p6)
                      return steps

                  def seq_steps(g, t):
                      gp = g % 2
                      tp = (g * 4 + t) % 2
                      ts_ = slice(t * 128, (t + 1) * 128)
                      tok = slice(g * 512, (g + 1) * 512)
                      steps = []
                      yps = {h: (pbs[7][:].bitcast(BF16)[:, h * 256:(h + 1) * 256].rearrange("p (c t) -> p c t", c=2), [("pb", 7)]) for h in HH}
                      state = {}
                      for c2 in range(2):
                          rs = slice(c2 * 64, (c2 + 1) * 64)
                          cs_ = slice(t * 128 + c2 * 64, t * 128 + (c2 + 1) * 64)
                          lc = t * 128 + c2 * 64 + 63

                          def sa(c2=c2, rs=rs):
                              wsp = {}
                              for h in HH:
                                  ap, k = sq_()
                                  mm(ap[0:64, :], wT[h][tp][:, rs], Sb[h][:], True, True, [f"wT{h}_{tp}", f"Sb{h}"], k)
                                  wsp[h] = (ap, k)
                              for h in HH:
                                  ap, k = wsp[h]
                                  tt("dve", vn[h][0][rs, :], usb[h][tp][rs, :], ap[0:64, :], ALU.subtract, k + [f"usb{h}_{tp}"], [f"vn{h}"])

                          def sb_(c2=c2, rs=rs, cs_=cs_, lc=lc):
                              osp = {}
                              for h in HH:
                                  ap, k = sq_()
                                  mm(ap[0:64, :], qdT[h][gp][:, cs_], Sb[h][:], True, False, [f"qdT{h}_{gp}", f"Sb{h}"], k)
                                  mm(ap[0:64, :], QKT[h][tp][:, rs], vn[h][0][:, :], False, True, [f"QKT{h}_{tp}", f"vn{h}"], k)
                                  osp[h] = (ap, k)
                                  ap2, k2 = sq_()
                                  mm(ap2, kd[h][2 * tp + c2][:, :], vn[h][0][:, :], True, True, [f"kd{h}_{tp}", f"vn{h}"], k2)
                                  stt("dve", Sf[h][:], Sf[h][:], egb[h][gp][:, lc:lc + 1], ap2, ALU.mult, ALU.add,
                                      k2 + [f"Sf{h}", f"egb{h}_{gp}"], [f"Sf{h}"])
                                  cp("act", Sb[h][:], Sf[h][:], [f"Sf{h}"], [f"Sb{h}"])
                              state[c2] = osp

                          def sc(c2=c2):
                              osp = state[c2]
                              for h in HH:
                                  ap, k = osp[h]
                                  act(junk2[:], ap[0:64, :], AF.Square, k, ["junk2", ("ost2", h)], accum=ost2[0:64, h, 0:1])
                                  rsqrt(ost2[0:64, h, 1:2], ost2[0:64, h, 0:1], 1.0 / 128.0, epsc[0:64, 0:1], [("ost2", h), "epsc"], [("ost2", h)])
                                  tsc("dve", on[h][0][0:64, :], ap[0:64, :], ost2[0:64, h, 1:2], None, ALU.mult, None, k + [("ost2", h)], [f"on{h}"])
                                  yap, yk = yps[h]
                                  tr(yap[:, c2, :], on[h][0][:, :], identb[:], [f"on{h}", "identb"], yk)
                          steps += [sa, sb_, sc]

                      def sf():
                          for h in HH:
                              yap, yk = yps[h]
                              stt("dve", ygs[h][gp][:, ts_].rearrange("p (c t) -> p c t", c=2), yap[:, :, 0:64], cols[:, l, 29:30],
                                  zgs[h][gp][:, ts_].rearrange("p (c t) -> p c t", c=2), ALU.mult, ALU.mult,
                                  yk + [("cols", l), f"zgs{h}_{gp}"], [f"ygs{h}_{gp}"])
                          if t == 3:
                              for h in HH:
                                  dma("sp", f"ygs{h}_{gp}", YT[4 + h, :, tok], ygs[h][gp][:], r=[f"ygs{h}_{gp}"], w=[("YT", 4 + h, g)])
                      steps.append(sf)
                      return steps

                  prev = None
                  for g in range(NG):
                      setup(g)
                      for t in range(4):
                          ps_l = par_steps(g, t)
                          ss_l = seq_steps(*prev) if prev is not None else []
                          for i in range(max(len(ps_l), len(ss_l))):
                              if i < len(ps_l):
                                  ps_l[i]()
                              if i < len(ss_l):
                                  ss_l[i]()
                          prev = (g, t)
                  for f_ in seq_steps(*prev):
                      f_()
                  P.barrier()
              if stop_after == "C":
                  break

              with ExitStack() as pd:
                  WOb = sbuf(pd, "WOb", [128, 8, D], BF16)
                  stg = [sbuf(pd, f"stgd{i}", [128, D], F32) for i in range(2)]
                  for c in range(8):
                      sk = f"stgd{c % 2}"
                      dma("sp", sk, stg[c % 2][:], w_out[l, c * 128:(c + 1) * 128, :], w=[sk])
                      cp("dve" if c % 2 == 0 else "pool", WOb[:, c, :], stg[c % 2][:], [sk], ["WOb"])
                  NS = 3
                  yt = [sbuf(pd, f"yt{i}", [128, 8, 128], BF16) for i in range(NS)]
                  xd = [sbuf(pd, f"xd{i}", [128, D], F32) for i in range(NS)]
                  xo = [sbuf(pd, f"xo{i}", [128, D], F32) for i in range(NS)]
                  junk3 = sbuf(pd, "junk3", [128, D], BF16)
                  st3 = sbuf(pd, "st3", [128, NS, 4], F32)

                  def d_load(ti):
                      if ti >= NT:
                          return
                      g = ti // 4
                      i2 = ti % NS
                      dma("sp", f"yt{i2}", yt[i2][:], YT[:, :, ti * 128:(ti + 1) * 128].rearrange("c p t -> p c t"),
                          r=[("YT", c, g) for c in range(8)], w=[f"yt{i2}"])
                      dma("sp", f"xd{i2}", xd[i2][:], xsrc[ti * 128:(ti + 1) * 128, :], r=[(xkey, ti)], w=[f"xd{i2}"])

                  def d_mm(ti):
                      i2 = ti % NS
                      b0 = 2 * (ti % 4)
                      for half in range(2):
                          for c in range(8):
                              mm(pbs[b0 + half][:], yt[i2][:, c, :], WOb[:, c, half * 512:(half + 1) * 512], c == 0, c == 7,
                                 [f"yt{i2}", "WOb"], PBK(b0 + half))

                  def d_epi(ti):
                      i2 = ti % NS
                      b0 = 2 * (ti % 4)
                      sk3 = ("st3", i2)
                      for half in range(2):
                          act(junk3[:, half * 512:(half + 1) * 512], pbs[b0 + half][:], AF.Square, PBK(b0 + half), ["junk3", sk3],
                              accum=st3[:, i2, half:half + 1])
                      tt("dve", st3[:, i2, 2:3], st3[:, i2, 0:1], st3[:, i2, 1:2], ALU.add, [sk3], [sk3])
                      rsqrt(st3[:, i2, 3:4], st3[:, i2, 2:3], 1.0 / D, epsc[:, 0:1], [sk3, "epsc"], [sk3])
                      for half in range(2):
                          hs = slice(half * 512, (half + 1) * 512)
                          stt("dve", xo[i2][:, hs], pbs[b0 + half][:], st3[:, i2, 3:4], gpb[:, hs], ALU.mult, ALU.mult,
                              PBK(b0 + half) + [sk3, "gpb"], [f"xo{i2}"])
                      tt("dve", xo[i2][:], xo[i2][:], xd[i2][:], ALU.add, [f"xo{i2}", f"xd{i2}"], [f"xo{i2}"])
                      dma("sp", f"xo{i2}", xdst[ti * 128:(ti + 1) * 128, :], xo[i2][:], r=[f"xo{i2}"], w=[(xdkey, ti)])

                  d_load(0)
                  d_load(1)
                  for ti in range(NT + 1):
                      if ti < NT:
                          d_mm(ti)
                      if ti >= 1:
                          d_epi(ti - 1)
                      d_load(ti + 2)
                  P.barrier()
        except _Stop:
            pass
        P.final_wait("sp", [("OUT", ti) for ti in range(NT)])
        build.stats = (dict(P.n_inst), dict(P.n_wait), P.nsem)
    return nc


def make_consts():
    ident = np.eye(128, dtype=np.float32)
    attm = np.zeros((128, 4, 512), np.float32)
    k = np.arange(128)[:, None]
    q = np.arange(512)[None, :]
    for j in range(4):
        attm[:, j, :] = (q >= 128 * j + k).astype(np.float32)
    i = np.arange(128)[:, None]
    jj = np.arange(128)[None, :]
    same = (i // 64) == (jj // 64)
    gm = np.zeros((128, 3, 128), np.float32)
    gm[:, 0, :] = (same & (i > jj))
    gm[:, 1, :] = (same & (i < jj))
    gm[:, 2, :] = (same & (i <= jj))
    sel = np.zeros((128, 8, 128), np.float32)
    for r in range(8):
        sel[r, r, :] = 1.0
    small = np.zeros((64, 4), np.float32)
    half = 32
    invf = np.power(np.float32(10000.0), -np.arange(half, dtype=np.float32) * np.float32(2.0) / np.float32(64)).astype(np.float32)
    small[:, 0] = np.concatenate([invf, invf])
    small[:32, 1] = -1.0
    small[32:, 1] = 1.0
    small[0:4, 2] = -1.0
    small[4:8, 3] = 1.0
    return dict(c_ident=ident, c_attm=attm.reshape(128, 2048), c_gmask=gm.reshape(128, 384), c_sel=sel.reshape(128, 1024), c_small=small,
                c_pm=np.stack([(np.arange(128) < 64), (np.arange(128) >= 64)], axis=1).astype(np.float32))


def make_in_map(inputs, b, S, NL):
    f = lambda a: np.ascontiguousarray(np.asarray(a))
    m = {}
    m["x"] = f(inputs["x"][b, :S])
    m["ccol"] = f(np.asarray(inputs["c"][b]).reshape(8, 128).T)
    m["posrep"] = f(np.broadcast_to(np.asarray(inputs["positions"][b, :S])[None, :], (64, S))).astype(np.int32)
    m["w_mod"] = f(inputs["w_mod"][:NL])
    m["b_mod"] = f(inputs["b_mod"][:NL])
    m["prew"] = f(inputs["pre_norm_w"][:NL])
    m["postw"] = f(inputs["post_norm_w"][:NL])
    m["w_in"] = f(inputs["w_in"][:NL])
    m["qnw"] = f(inputs["mla_q_norm_w"][:NL])
    m["q_up"] = f(inputs["mla_q_up"][:NL])
    m["kvnw"] = f(inputs["mla_kv_norm_w"][:NL])
    m["kv_up"] = f(inputs["mla_kv_up"][:NL])
    m["convw"] = f(np.asarray(inputs["gdn_conv_w"][:NL]).reshape(NL, 4 * 1536))
    a8 = np.zeros((8, NL), np.float32)
    a8[0:4, :] = np.asarray(inputs["gdn_a_log"][:NL]).T
    d8 = np.zeros((8, NL), np.float32)
    d8[0:4, :] = np.asarray(inputs["gdn_dt_bias"][:NL]).T
    m["alog8"] = a8
    m["dtb8"] = d8
    m["onw"] = f(inputs["gdn_o_norm_w"][:NL])
    m["w_out"] = f(inputs["w_out"][:NL])
    m.update(make_consts())
    return m


_NC_CACHE = {}


def kernel(**inputs):
    B, S, _ = inputs["x"].shape
    NL = inputs["w_in"].shape[0]
    key = (S, NL)
    if key not in _NC_CACHE:
        _NC_CACHE[key] = build(S, NL)
    nc = _NC_CACHE[key]
    in_maps = [make_in_map(inputs, c % B, S, NL) for c in range(8)]
    res = run_bass_kernel_spmd(nc, in_maps, core_ids=list(range(8)))
    outs = [np.asarray(res.results[b]["out"]) for b in range(B)]
    return np.stack(outs, axis=0).astype(np.float32)
```

```python
import math
import os
CUT = int(os.environ.get('KCUT', '99'))
import numpy as np
from contextlib import ExitStack
import concourse.bass as bass
import concourse.mybir as mybir
from concourse.bass_utils import run_bass_kernel_spmd

F32 = mybir.dt.float32
BF16 = mybir.dt.bfloat16
I32 = mybir.dt.int32
ALU = mybir.AluOpType
AF = mybir.ActivationFunctionType

D = 1024
NIN = 3272
NWB = 3336
EPS = 1e-6
ENGS = ("pe", "act", "dve", "pool", "sp")
SEM_ROT = 30000
HMAP = {"pe": "tensor", "act": "scalar", "dve": "vector", "pool": "gpsimd", "sp": "sync"}


class _Stop(Exception):
    pass


class Prog:
    def __init__(self, nc, stack):
        self.nc = nc
        self.stack = stack
        self.cur_sem = {}
        self.cur_cnt = {e: 0 for e in ENGS}
        self.nsem = 0
        self.all_eng_sems = {e: [] for e in ENGS}
        for e in ENGS:
            self._new_eng_sem(e)
        self.known = {e: {} for e in ENGS}
        self.free_dma = []
        self.retired = []
        self.last_write = {}
        self.readers = {}
        self.dma_sems = {}
        self.n_inst = {e: 0 for e in ENGS}
        self.n_wait = {e: 0 for e in ENGS}

    def _sem(self, name):
        s = self.stack.enter_context(self.nc.semaphore(name))
        self.nsem += 1
        return s

    def _new_eng_sem(self, e):
        if e in self.cur_sem:
            self.retired.append([self.cur_sem[e], self.cur_cnt[e]])
        self.cur_sem[e] = self._sem(f"c_{e}_{self.nsem}")
        self.cur_cnt[e] = 0

    def _deps(self, eng, reads, writes):
        deps = {}

        def add(ev):
            if ev is None:
                return
            s, v = ev
            k = id(s)
            if k not in deps or deps[k][1] < v:
                deps[k] = (s, v)

        for r in reads:
            add(self.last_write.get(r))
        for w in writes:
            add(self.last_write.get(w))
            rd = self.readers.get(w)
            if rd:
                for ev in rd.values():
                    add(ev)
        out = []
        kn = self.known[eng]
        own = id(self.cur_sem[eng])
        for k, (s, v) in deps.items():
            if k == own and eng == "pe":
                continue
            if kn.get(k, 0) >= v:
                continue
            kn[k] = v
            out.append((s, v))
        return out

    def _record(self, ev, reads, writes):
        for r in reads:
            self.readers.setdefault(r, {})[id(ev[0])] = ev
        for w in writes:
            self.last_write[w] = ev
            self.readers[w] = {}

    def _emit_now(self, eng, waits, fn, sem, inc):
        e = getattr(self.nc, HMAP[eng])
        for (s, v) in waits:
            e.wait_ge(s, v)
        self.n_wait[eng] += len(waits)
        if fn is not None:
            fn(e).then_inc(sem, inc)
            self.n_inst[eng] += 1

    def op(self, eng, fn, reads=(), writes=()):
        waits = self._deps(eng, reads, writes)
        if self.cur_cnt[eng] >= SEM_ROT:
            self._new_eng_sem(eng)
        sem = self.cur_sem[eng]
        self.cur_cnt[eng] += 1
        val = self.cur_cnt[eng]
        self._emit_now(eng, waits, fn, sem, 1)
        self._record((sem, val), reads, writes)

    def dma(self, q, semname, out, in_, reads=(), writes=(), **kw):
        if semname not in self.dma_sems:
            if self.free_dma:
                self.dma_sems[semname] = self.free_dma.pop()
            else:
                self.dma_sems[semname] = [self._sem("d_" + semname), 0]
        ent = self.dma_sems[semname]
        waits = self._deps(q, reads, writes)
        ent[1] += 16
        sem, val = ent[0], ent[1]
        self._emit_now(q, waits, lambda e: e.dma_start(out=out, in_=in_, **kw), sem, 16)
        self._record((sem, val), reads, writes)

    def barrier(self):
        evs = [(self.cur_sem[e], self.cur_cnt[e]) for e in ENGS if self.cur_cnt[e] > 0]
        evs += [(s, v) for (s, v) in self.dma_sems.values() if v > 0]
        evs += [(s, v) for (s, v) in self.retired if v > 0]
        for eng in ENGS:
            kn = self.known[eng]
            own = id(self.cur_sem[eng])
            waits = []
            for (s, v) in evs:
                if id(s) == own or kn.get(id(s), 0) >= v:
                    continue
                kn[id(s)] = v
                waits.append((s, v))
            self._emit_now(eng, waits, None, None, 0)
        self.free_dma.extend(self.dma_sems.values())
        self.dma_sems = {}
        self.retired = []

    def final_wait(self, eng, keys):
        waits = self._deps(eng, keys, ())
        self._emit_now(eng, waits, None, None, 0)


def build(S, NL, dbg=False, stop_after=None):
    nc = bass.Bass("TRN2", target_bir_lowering=False)
    NG = S // 512
    NT = S // 128

    def din(name, shape, dt=F32):
        return nc.dram_tensor(name, list(shape), dt, kind="ExternalInput").ap()

    def dscr(name, shape, dt):
        return nc.dram_tensor(name, list(shape), dt, kind="ExternalOutput" if dbg else "Internal").ap()

    x_in = din("x", [S, D])
    ccol = din("ccol", [128, 8])
    posrep = din("posrep", [64, S], I32)
    w_mod = din("w_mod", [NL, D, 3 * D])
    b_mod = din("b_mod", [NL, 3 * D])
    prew = din("prew", [NL, D])
    postw = din("postw", [NL, D])
    w_in = din("w_in", [NL, D, NIN])
    qnw = din("qnw", [NL, 384])
    q_up = din("q_up", [NL, 384, 768])
    kvnw = din("kvnw", [NL, 256])
    kv_up = din("kv_up", [NL, 256, 1024])
    convw = din("convw", [NL, 4 * 1536])
    alog8 = din("alog8", [8, NL])
    dtb8 = din("dtb8", [8, NL])
    onw = din("onw", [NL, 128])
    w_out = din("w_out", [NL, D, D])
    c_ident = din("c_ident", [128, 128])
    c_attm = din("c_attm", [128, 4 * 512])
    c_gmask = din("c_gmask", [128, 3 * 128])
    c_sel = din("c_sel", [128, 8 * 128])
    c_pm = din("c_pm", [128, 2])
    c_small = din("c_small", [64, 4])
    out = nc.dram_tensor("out", [S, D], F32, kind="ExternalOutput").ap()

    XR = dscr("XR", [S, D], F32)
    QT = dscr("QT", [4, 192, S], BF16)
    KT = dscr("KT", [4, 128, S], BF16)
    KPE = dscr("KPE", [64, S], BF16)
    VV = dscr("VV", [S, 512], BF16)
    ZM = dscr("ZM", [4, 128, S], BF16)
    ZG = dscr("ZG", [4, 128, S], BF16)
    GQ = dscr("GQ", [4, 128, S], BF16)
    GK = dscr("GK", [4, 128, S], BF16)
    GV = dscr("GV", [4, 128, S], BF16)
    COMB = dscr("COMB", [8, S], F32)
    YT = dscr("YT", [8, 128, S], BF16)
    COS = dscr("COS", [64, S], F32)
    SIN = dscr("SIN", [64, S], F32)
    GP = dscr("GP", [NL, D], F32)

    with ExitStack() as st:
        P = Prog(nc, st)

        uid = [0]

        def sbuf(stack, name, shape, dt):
            uid[0] += 1
            return stack.enter_context(nc.sbuf_tensor(f"{name}_u{uid[0]}", list(shape), dt))

        def mm(o, lhsT, rhs, start, stop, r, w):
            P.op("pe", lambda e: e.matmul(o, lhsT=lhsT, rhs=rhs, start=start, stop=stop), r, w)

        def tr(o, i, ident, r, w):
            P.op("pe", lambda e: e.transpose(o, i, ident), r, w)

        def act(o, i, func, r, w, bias=None, scale=None, accum=None):
            kw = {}
            if bias is not None:
                kw["bias"] = bias
            if scale is not None:
                kw["scale"] = scale
            if accum is not None:
                kw["accum_out"] = accum
            P.op("act", lambda e: e.activation(out=o, in_=i, func=func, **kw), r, w)

        def cp(eng, o, i, r, w):
            if eng == "act":
                P.op("act", lambda e: e.activation(out=o, in_=i, func=AF.Copy), r, w)
            else:
                P.op(eng, lambda e: e.tensor_copy(out=o, in_=i), r, w)

        def tsc(eng, o, i, s1, s2, op0, op1, r, w):
            if op1 is None:
                P.op(eng, lambda e: e.tensor_scalar(out=o, in0=i, scalar1=s1, scalar2=None, op0=op0), r, w)
            else:
                P.op(eng, lambda e: e.tensor_scalar(out=o, in0=i, scalar1=s1, scalar2=s2, op0=op0, op1=op1), r, w)

        def tt(eng, o, a, b, op, r, w):
            P.op(eng, lambda e: e.tensor_tensor(out=o, in0=a, in1=b, op=op), r, w)

        def stt(eng, o, a, sc, b, op0, op1, r, w):
            eng = "dve"
            P.op(eng, lambda e: e.scalar_tensor_tensor(out=o, in0=a, scalar=sc, in1=b, op0=op0, op1=op1), r, w)

        def rcp(o, i, r, w):
            P.op("dve", lambda e: e.reciprocal(out=o, in_=i), r, w)

        def memset(eng, o, val, w):
            P.op(eng, lambda e: e.memset(o, val), (), w)

        def dma(q, sem, o, i, r=(), w=(), **kw):
            P.dma(q, sem, o, i, reads=r, writes=w, **kw)

        def rsqrt(o, i, scale, bias_ap, r, w):
            act(o, i, AF.Ln, r, w, bias=bias_ap, scale=scale)
            act(o, o, AF.Exp, w, w, scale=-0.5)

        pbs = [st.enter_context(nc.psum_tensor(f"pb{i}", [128, 512], F32)) for i in range(8)]

        def PBK(i):
            return [("pb", i)]

        identf = sbuf(st, "identf", [128, 128], F32)
        identb = sbuf(st, "identb", [128, 128], BF16)
        onesf = sbuf(st, "onesf", [128, 128], F32)
        attm = sbuf(st, "attm", [128, 4, 512], BF16)
        gmask = sbuf(st, "gmask", [128, 3, 128], F32)
        sel = sbuf(st, "sel", [128, 8, 128], F32)
        pmask = sbuf(st, "pmask", [128, 2], F32)
        small = sbuf(st, "small", [64, 4], F32)
        epsc = sbuf(st, "epsc", [128, 1], F32)
        cols = sbuf(st, "cols", [128, NL, 80], F32)
        nA = sbuf(st, "nA", [8, NL], F32)
        dtb = sbuf(st, "dtb", [8, NL], F32)
        gpb = sbuf(st, "gpb", [128, D], F32)

        dma("sp", "c0", identf[:], c_ident, w=["identf"])
        dma("sp", "c1", gmask[:].rearrange("p m j -> p (m j)"), c_gmask, w=["gmask"])
        dma("sp", "c2", sel[:].rearrange("p m j -> p (m j)"), c_sel, w=["sel"])
        dma("sp", "c3", small[:], c_small, w=["small"])
        dma("sp", "c3b", pmask[:], c_pm, w=["pmask"])
        dma("sp", "c4", nA[:], alog8, w=["nA"])
        dma("sp", "c5", dtb[:], dtb8, w=["dtb"])
        cp("dve", identb[:], identf[:], ["identf"], ["identb"])
        memset("pool", onesf[:], 1.0, ["onesf"])
        memset("pool", epsc[:], EPS, ["epsc"])
        act(nA[:], nA[:], AF.Exp, ["nA"], ["nA"])
        tsc("dve", nA[:], nA[:], small[0:8, 2:3], None, ALU.mult, None, ["nA", "small"], ["nA"])

        with ExitStack() as ps_:
            amst = sbuf(ps_, "amst", [128, 4 * 512], F32)
            dma("sp", "c6", amst[:], c_attm, w=["amst"])
            cp("dve", attm[:].rearrange("p m j -> p (m j)"), amst[:], ["amst"], ["attm"])

            cact = sbuf(ps_, "cact", [128, 8], F32)
            dma("sp", "c7", cact[:], ccol, w=["cact"])
            act(cact[:], cact[:], AF.Silu, ["cact"], ["cact"])
            NROW = 3072 + 1792 + 6144
            rowbuf = sbuf(ps_, "rowbuf", [1, NROW], F32)
            bmrow = sbuf(ps_, "bmrow", [1, 3072], F32)
            pwrow = sbuf(ps_, "pwrow", [1, D], F32)
            wmst = [sbuf(ps_, f"wmst{i}", [128, 8, 512], F32) for i in range(2)]
            for l in range(NL):
                dma("sp", "r0", bmrow[:], b_mod[l:l + 1, :], w=["bmrow"])
                dma("sp", "r1", rowbuf[0:1, 3072:4096], prew[l:l + 1, :], w=["rowbuf_s"])
                dma("sp", "r1", rowbuf[0:1, 4096:4480], qnw[l:l + 1, :], w=["rowbuf_s"])
                dma("sp", "r1", rowbuf[0:1, 4480:4736], kvnw[l:l + 1, :], w=["rowbuf_s"])
                dma("sp", "r1", rowbuf[0:1, 4736:4864], onw[l:l + 1, :], w=["rowbuf_s"])
                dma("sp", "r1", rowbuf[0:1, 4864:NROW], convw[l:l + 1, :], w=["rowbuf_s"])
                dma("sp", "r2", pwrow[:], postw[l:l + 1, :], w=["pwrow"])
                for cg in range(6):
                    ws_ = wmst[cg % 2]
                    wk = f"wmst{cg % 2}"
                    dma("sp", wk, ws_[:], w_mod[l, :, cg * 512:(cg + 1) * 512].rearrange("(c p) n -> p c n", p=128), w=[wk])
                    pb = pbs[cg % 2]
                    for c in range(8):
                        mm(pb[0:1, :], cact[:, c:c + 1], ws_[:, c, :], c == 0, c == 7, [wk, "cact"], PBK(cg % 2))
                    tt("dve", rowbuf[0:1, cg * 512:(cg + 1) * 512], pb[0:1, :], bmrow[0:1, cg * 512:(cg + 1) * 512], ALU.add,
                       PBK(cg % 2) + ["bmrow"], ["rowbuf_m"])
                nchunk = 16 + 62
                for j in range(nchunk):
                    off = j * 128 if j < 16 else 3072 + (j - 16) * 128
                    mm(pbs[2][:, j:j + 1], rowbuf[0:1, off:off + 128], onesf[0:1, 0:1], True, True,
                       ["rowbuf_m", "rowbuf_s", "onesf"], PBK(2))
                cp("dve", cols[:, l, 0:nchunk], pbs[2][:, 0:nchunk], PBK(2), [("cols", l)])
                stt("dve", cols[:, l, 8:16], cols[:, l, 8:16], 1.0, cols[:, l, 16:24], ALU.add, ALU.mult, [("cols", l)], [("cols", l)])
                tt("dve", pwrow[:], pwrow[:], rowbuf[0:1, 2048:3072], ALU.mult, ["pwrow", "rowbuf_m"], ["pwrow"])
                dma("sp", "r3", GP[l:l + 1, :], pwrow[:], r=["pwrow"], w=[("GP", l)])

            CH = min(2048, S)
            posi = sbuf(ps_, "posi", [64, CH], I32)
            ang = sbuf(ps_, "ang", [64, CH], F32)
            a2 = sbuf(ps_, "a2", [64, CH], F32)
            kf = sbuf(ps_, "kf", [64, CH], F32)
            ki = sbuf(ps_, "ki", [64, CH], I32)
            fx = sbuf(ps_, "fx", [64, CH], F32)
            tab = [sbuf(ps_, f"tab{i}", [64, CH], F32) for i in range(2)]
            C1 = 6.28125
            C2 = 2.0 * math.pi - C1
            for ch in range(S // CH):
                cs = slice(ch * CH, (ch + 1) * CH)
                dma("sp", "rp0", posi[:], posrep[:, cs], w=["posi"])
                cp("dve", ang[:], posi[:], ["posi"], ["ang"])
                tsc("dve", ang[:], ang[:], small[:, 0:1], None, ALU.mult, None, ["ang", "small"], ["ang"])
                for ti, offv in ((0, math.pi / 2), (1, 0.0)):
                    tsc("dve", a2[:], ang[:], offv, None, ALU.add, None, ["ang"], ["a2"])
                    tsc("dve", ki[:], a2[:], 1.0 / (2 * math.pi), None, ALU.mult, None, ["a2"], ["ki"])
                    cp("dve", kf[:], ki[:], ["ki"], ["kf"])
                    stt("dve", a2[:], kf[:], -C1, a2[:], ALU.mult, ALU.add, ["kf", "a2"], ["a2"])
                    stt("dve", a2[:], kf[:], -C2, a2[:], ALU.mult, ALU.add, ["kf", "a2"], ["a2"])
                    tsc("dve", fx[:], a2[:], math.pi, 2 * math.pi, ALU.is_gt, ALU.mult, ["a2"], ["fx"])
                    tt("dve", a2[:], a2[:], fx[:], ALU.subtract, ["a2", "fx"], ["a2"])
                    tsc("dve", fx[:], a2[:], -math.pi, 2 * math.pi, ALU.is_lt, ALU.mult, ["a2"], ["fx"])
                    tt("dve", a2[:], a2[:], fx[:], ALU.add, ["a2", "fx"], ["a2"])
                    tsc("dve", a2[:], a2[:], math.pi, -math.pi, ALU.min, ALU.max, ["a2"], ["a2"])
                    tk = f"tab{ti}"
                    act(tab[ti][:], a2[:], AF.Sin, ["a2"], [tk])
                    if ti == 1:
                        tsc("dve", tab[ti][:], tab[ti][:], small[:, 1:2], None, ALU.mult, None, [tk, "small"], [tk])
                    dma("sp", "rp" + tk, (COS if ti == 0 else SIN)[:, cs], tab[ti][:], r=[tk], w=[("ROPE", ti, ch)])
            P.barrier()
        ROPE_KEYS = [("ROPE", ti, ch) for ti in range(2) for ch in range(S // min(2048, S))]

        def chk(tag):
            if stop_after == tag:
                P.barrier()
                raise _Stop()

        try:
          for l in range(NL if stop_after != "P" else 0):
              xsrc = x_in if l == 0 else XR
              xdst = out if l == NL - 1 else XR
              xkey = "XIN" if l == 0 else "XR"
              xdkey = "OUT" if l == NL - 1 else "XR"
              dma("sp", "gpb", gpb[:], GP[l:l + 1, :].partition_broadcast(128), r=[("GP", l)], w=["gpb"])
              chk("G")

              with ExitStack() as pa:
                  Wb = sbuf(pa, "Wb", [128, 8, NWB], BF16)
                  Qb = sbuf(pa, "Qb", [128, 3, 4, 256], BF16)
                  KVb = sbuf(pa, "KVb", [128, 2, 1024], BF16)
                  HS = NIN // 2
                  pa_w = ExitStack()
                  stg = [sbuf(pa_w, f"stg{i}", [128, HS], F32) for i in range(2)]
                  si = 0
                  ceng = ["dve", "pool"]
                  for c in range(8):
                      for hf in range(2):
                          s_ = stg[si % 2]
                          sk = f"stg{si % 2}"
                          dma("sp", sk, s_[:], w_in[l, c * 128:(c + 1) * 128, hf * HS:(hf + 1) * HS], w=[sk])
                          lo, hi = hf * HS, (hf + 1) * HS
                          for (a, b, dst) in ((0, 704, 0), (672, 704, 704), (640, 672, 736), (704, NIN, 768)):
                              a2_, b2_ = max(a, lo), min(b, hi)
                              if a2_ >= b2_:
                                  continue
                              d0 = dst + (a2_ - a)
                              cp(ceng[si % 2], Wb[:, c, d0:d0 + (b2_ - a2_)], s_[:, a2_ - lo:b2_ - lo], [sk], [("Wb", c)])
                          si += 1
                  for c in range(3):
                      s_ = stg[si % 2]
                      sk = f"stg{si % 2}"
                      dma("sp", sk, s_[:, 0:768], q_up[l, c * 128:(c + 1) * 128, :], w=[sk])
                      sv = s_[:, 0:768].rearrange("p (h f) -> p h f", h=4)
                      qs = cols[:, l, 24 + c:25 + c]
                      for (a, b, dst) in ((0, 128, 0), (128, 192, 128), (160, 192, 192), (128, 160, 224)):
                          tsc("dve", Qb[:, c, :, dst:dst + (b - a)], sv[:, :, a:b], qs, None, ALU.mult, None, [sk, ("cols", l)], ["Qb"])
                      si += 1
                  for c in range(2):
                      s_ = stg[si % 2]
                      sk = f"stg{si % 2}"
                      dma("sp", sk, s_[:, 0:1024], kv_up[l, c * 128:(c + 1) * 128, :], w=[sk])
                      tsc("dve", KVb[:, c, :], s_[:, 0:1024], cols[:, l, 27 + c:28 + c], None, ALU.mult, None, [sk, ("cols", l)], ["KVb"])
                      si += 1
                  WBK = [("Wb", c) for c in range(8)]
                  P.barrier()
                  pa_w.close()
                  NGr = 0 if stop_after == "W" else (int(stop_after[1:]) if (stop_after or "").startswith("g") else NG)

                  xt = [sbuf(pa, f"xt{i}", [128, D], F32) for i in range(2)]
                  xs = [sbuf(pa, f"xs{i}", [128, D], F32) for i in range(4)]
                  junk = sbuf(pa, "junk", [128, D], BF16)
                  st1 = sbuf(pa, "st1", [128, 8], F32)
                  hT = [sbuf(pa, f"hT{i}", [128, 8, 512], BF16) for i in range(2)]
                  qlT = sbuf(pa, "qlT", [128, 3, 512], BF16)
                  kvT = sbuf(pa, "kvT", [128, 2, 512], BF16)
                  sq = [sbuf(pa, f"sq{i}", [128, 512], F32) for i in range(3)]
                  rq = sbuf(pa, "rq", [128, 512], F32)
                  rkv = sbuf(pa, "rkv", [128, 512], F32)
                  rkvt = sbuf(pa, "rkvt", [128, 4], F32)
                  cst = sbuf(pa, "cst", [64, 512], F32)
                  sst = sbuf(pa, "sst", [64, 512], F32)
                  crr = sbuf(pa, "crr", [64, 512], F32)
                  srr = sbuf(pa, "srr", [64, 512], F32)
                  t1 = [sbuf(pa, f"t1_{i}", [64, 512], F32) for i in range(1)] * 2
                  t2 = [sbuf(pa, f"t2_{i}", [64, 512], F32) for i in range(1)] * 2
                  ost = [sbuf(pa, f"ost{i}", [128, 512], BF16) for i in range(6)]
                  vst = [sbuf(pa, f"vst{i}", [128, 512], BF16) for i in range(2)]
                  zst = [sbuf(pa, f"zst{i}", [128, 4, 512], BF16) for i in range(2)]
                  convd = sbuf(pa, "convd", [128, 48, 128], BF16)
                  cvb = [sbuf(pa, f"cvb{i}", [128, 515], BF16) for i in range(3)]
                  cvc = sbuf(pa, "cvc", [128, 12, 3], BF16)
                  sact = [sbuf(pa, f"sact{i}", [128, 512], F32) for i in range(8)]
                  rn = [sbuf(pa, f"rn{i}", [128, 512], F32) for i in range(2)]
                  abw = [sbuf(pa, f"abw{i}", [8, 512], F32) for i in range(4)]
                  memset("pool", cvc[:], 0.0, ["cvc"])
                  for jj in range(4):
                      for ch in range(12):
                          tsc("dve", convd[:, jj * 12 + ch, :], identb[:], cols[:, l, 30 + jj * 12 + ch:31 + jj * 12 + ch], None,
                              ALU.mult, None, ["identb", ("cols", l)], ["convd"])
                  octr = [0]
                  pbr = [2]

                  def nextpb():
                      b = pbr[0]
                      pbr[0] = 2 + (pbr[0] - 2 + 1) % 6
                      return b

                  def nextost():
                      i = octr[0] % 6
                      octr[0] += 1
                      return i

                  def xstat_all(g):
                      for t in range(4):
                          ti = g * 4 + t
                          k2 = f"xt{t % 2}"
                          dma("sp", k2, xt[t % 2][:], xsrc[ti * 128:(ti + 1) * 128, :], r=[(xkey, ti)], w=[k2])
                          act(junk[:], xt[t % 2][:], AF.Square, [k2], ["junk", "st1"], accum=st1[:, t:t + 1])
                      rsqrt(st1[:, 4:8], st1[:, 0:4], 1.0 / D, epsc[:, 0:1], ["st1", "epsc"], ["st1"])
                      for t in range(4):
                          ti = g * 4 + t
                          k2 = f"xt{t % 2}"
                          dma("sp", k2, xt[t % 2][:], xsrc[ti * 128:(ti + 1) * 128, :], r=[(xkey, ti)], w=[k2])
                          tsc("dve", xs[t][:], xt[t % 2][:], st1[:, 4 + t:5 + t], None, ALU.mult, None, [k2, "st1"], [f"xs{t}"])

                  def xtrans(g, t):
                      hh_ = hT[g % 2]
                      hhk = f"hT{g % 2}"
                      for c in range(8):
                          tr(pbs[c // 4][:, (c % 4) * 128:(c % 4 + 1) * 128], xs[t][:, c * 128:(c + 1) * 128], identf[:],
                             [f"xs{t}", "identf"], [("pb", c // 4)])
                      for c in range(8):
                          tsc("dve", hh_[:, c, t * 128:(t + 1) * 128], pbs[c // 4][:, (c % 4) * 128:(c % 4 + 1) * 128],
                              cols[:, l, 8 + c:9 + c], cols[:, l, c:c + 1], ALU.mult, ALU.add,
                              [("pb", c // 4), ("cols", l)], [hhk])

                  if NGr > 0:
                      xstat_all(0)
                      for t in range(4):
                          xtrans(0, t)
                  for g in range(NGr):
                      tok = slice(g * 512, (g + 1) * 512)
                      h_ = hT[g % 2]
                      hk = f"hT{g % 2}"
                      dma("sp", "cst", cst[:], COS[:, tok], r=ROPE_KEYS, w=["cst"])
                      dma("sp", "sst", sst[:], SIN[:, tok], r=ROPE_KEYS, w=["sst"])

                      def proj(col0, ncol):
                          b = nextpb()
                          for c in range(8):
                              mm(pbs[b][0:ncol, :], Wb[:, c, col0:col0 + ncol], h_[:, c, :], c == 0, c == 7, [hk, ("Wb", c)], PBK(b))
                          return b

                      def nxt(t):
                          return


                      def ntrans(t):
                          if g + 1 < NGr:
                              xtrans(g + 1, t)

                      for c in range(3):
                          b = proj(c * 128, 128)
                          cp("act", qlT[:, c, :], pbs[b][:], PBK(b), ["qlT"])
                          act(sq[c][:], pbs[b][:], AF.Square, PBK(b), [f"sq{c}"])
                      bsum = nextpb()
                      for c in range(3):
                          mm(pbs[bsum][:], onesf[:], sq[c][:], c == 0, c == 2, [f"sq{c}", "onesf"], PBK(bsum))
                      rsqrt(rq[:], pbs[bsum][:], 1.0 / 384.0, epsc[:, 0:1], PBK(bsum) + ["epsc"], ["rq"])
                      for c in range(2):
                          b = proj(384 + c * 128, 128)
                          cp("act", kvT[:, c, :], pbs[b][:], PBK(b), ["kvT"])
                          act(sq[c][:], pbs[b][:], AF.Square, PBK(b), [f"sq{c}"])
                      bsum = nextpb()
                      for c in range(2):
                          mm(pbs[bsum][:], onesf[:], sq[c][:], c == 0, c == 1, [f"sq{c}", "onesf"], PBK(bsum))
                      bt_ = nextpb()
                      for t in range(4):
                          for c in range(2):
                              mm(pbs[bt_][:, 8 + t:9 + t], sq[c][:, t * 128:(t + 1) * 128], onesf[:, 0:1],
                                 c == 0, c == 1, [f"sq{c}", "onesf"], PBK(bt_))
                      rsqrt(rkv[:], pbs[bsum][:], 1.0 / 256.0, epsc[:, 0:1], PBK(bsum) + ["epsc"], ["rkv"])
                      rsqrt(rkvt[:], pbs[bt_][:, 8:12], 1.0 / 256.0, epsc[:, 0:1], PBK(bt_) + ["epsc"], ["rkvt"])
                      tt("dve", crr[:], cst[:], rq[0:64, :], ALU.mult, ["cst", "rq"], ["crr"])
                      tt("dve", srr[:], sst[:], rq[0:64, :], ALU.mult, ["sst", "rq"], ["srr"])
                      X = []
                      Y = []

                      def xq(h):
                          b = nextpb()
                          for c in range(3):
                              mm(pbs[b][:], Qb[:, c, h, 0:128], qlT[:, c, :], c == 0, c == 2, ["Qb", "qlT"], PBK(b))
                          oi = nextost()
                          tt("dve", ost[oi][:], pbs[b][:], rq[:], ALU.mult, PBK(b) + ["rq"], [f"ost{oi}"])
                          dma("sp", f"ost{oi}", QT[h, 0:128, tok], ost[oi][:], r=[f"ost{oi}"], w=[("QT", h, g)])
                          b = nextpb()
                          for c in range(3):
                              mm(pbs[b][:], Qb[:, c, h, 128:256], qlT[:, c, :], c == 0, c == 2, ["Qb", "qlT"], PBK(b))
                          tt("dve", t1[0][:], pbs[b][0:64, :], crr[:], ALU.mult, PBK(b) + ["crr"], ["t1_0"])
                          tt("dve", t2[0][:], pbs[b][64:128, :], srr[:], ALU.mult, PBK(b) + ["srr"], ["t2_0"])
                          oi = nextost()
                          tt("pool", ost[oi][0:64, :], t1[0][:], t2[0][:], ALU.add, ["t1_0", "t2_0"], [f"ost{oi}"])
                          dma("sp", f"ost{oi}", QT[h, 128:192, tok], ost[oi][0:64, :], r=[f"ost{oi}"], w=[("QT", h, g)])

                      def xk(h):
                          b = nextpb()
                          for c in range(2):
                              mm(pbs[b][:], KVb[:, c, h * 256:h * 256 + 128], kvT[:, c, :], c == 0, c == 1, ["KVb", "kvT"], PBK(b))
                          oi = nextost()
                          tt("dve", ost[oi][:], pbs[b][:], rkv[:], ALU.mult, PBK(b) + ["rkv"], [f"ost{oi}"])
                          dma("sp", f"ost{oi}", KT[h, :, tok], ost[oi][:], r=[f"ost{oi}"], w=[("KT", h, g)])

                      kvv = KVb[:].rearrange("p c (h f) -> p c h f", h=4)

                      def xv(t):
                          b = nextpb()
                          for c in range(2):
                              mm(pbs[b][:].rearrange("p (h f) -> p h f", h=4), kvT[:, c, t * 128:(t + 1) * 128], kvv[:, c, :, 128:256],
                                 c == 0, c == 1, ["KVb", "kvT"], PBK(b))
                          vi = (g * 4 + t) % 2
                          tsc("dve", vst[vi][:], pbs[b][:], rkvt[:, t:t + 1], None, ALU.mult, None, PBK(b) + ["rkvt"], [f"vst{vi}"])
                          dma("sp", f"vst{vi}", VV[g * 512 + t * 128:g * 512 + (t + 1) * 128, :], vst[vi][:], r=[f"vst{vi}"], w=[("VV", g)])

                      def xkpe():
                          b = proj(640, 128)
                          tt("dve", t1[0][:], pbs[b][0:64, :], cst[:], ALU.mult, PBK(b) + ["cst"], ["t1_0"])
                          tt("dve", t2[0][:], pbs[b][64:128, :], sst[:], ALU.mult, PBK(b) + ["sst"], ["t2_0"])
                          oi = nextost()
                          tt("pool", ost[oi][0:64, :], t1[0][:], t2[0][:], ALU.add, ["t1_0", "t2_0"], [f"ost{oi}"])
                          dma("sp", f"ost{oi}", KPE[:, tok], ost[oi][0:64, :], r=[f"ost{oi}"], w=[("KPE", g)])

                      for h in range(4):
                          X.append(lambda h=h: xq(h))
                      for h in range(4):
                          X.append(lambda h=h: xk(h))
                      for t in range(4):
                          X.append(lambda t=t: xv(t))
                      X.append(xkpe)

                      def yz(zi, c):
                          base, dstD, zk = ((768, ZM, "ZM"), (2824, ZG, "ZG"))[zi]
                          b = proj(base + c * 128, 128)
                          act(zst[zi][:, c, :], pbs[b][:], AF.Silu, PBK(b), [f"zst{zi}"])
                          if c == 3:
                              dma("sp", f"zst{zi}", dstD[:, :, tok].rearrange("c p t -> p c t"), zst[zi][:], r=[f"zst{zi}"], w=[(zk, g)])

                      def conv_pe(ch):
                          sl = ch % 3
                          b2 = nextpb()
                          for j in range(4):
                              mm(pbs[b2][:], convd[:, j * 12 + ch, :], cvb[sl][:, j:j + 512], j == 0, j == 3, ["convd", f"cvb{sl}"], PBK(b2))
                          hh = ch % 4
                          if ch >= 8:
                              oi = nextost()
                              act(ost[oi][:], pbs[b2][:], AF.Silu, PBK(b2), [f"ost{oi}"])
                              dma("sp", f"ost{oi}", GV[hh, :, tok], ost[oi][:], r=[f"ost{oi}"], w=[("GV", hh, g)])
                          else:
                              act(sact[ch][:], pbs[b2][:], AF.Silu, PBK(b2), [f"sact{ch}"])

                      def yconv(ch):
                          sl = ch % 3
                          b = proj(1280 + ch * 128, 128)
                          cp("act", cvb[sl][:, 3:515], pbs[b][:], PBK(b), [f"cvb{sl}"])
                          cp("pool", cvb[sl][:, 0:3], cvc[:, ch, :], ["cvc"], [f"cvb{sl}"])
                          cp("pool", cvc[:, ch, :], cvb[sl][:, 512:515], [f"cvb{sl}"], ["cvc"])
                          if ch >= 1:
                              conv_pe(ch - 1)
                          if ch == 11:
                              conv_pe(11)

                      for zi in range(2):
                          for c in range(4):
                              Y.append(lambda zi=zi, c=c: yz(zi, c))
                      for ch in range(12):
                          Y.append(lambda ch=ch: yconv(ch))
                      xi = 0
                      for yi, yf in enumerate(Y):
                          yf()
                          while xi < len(X) and xi < (yi + 1) * len(X) / len(Y):
                              X[xi]()
                              xi += 1
                          if yi == 9 and g + 1 < NGr:
                              xstat_all(g + 1)
                      while xi < len(X):
                          X[xi]()
                          xi += 1
                      def l2sq(ch):
                          if ch < 8:
                              act(sq[ch % 3][:], sact[ch][:], AF.Square, [f"sact{ch}"], [f"sq{ch % 3}"])
                      l2sq(0)
                      l2sq(1)
                      for ch in range(8):
                          l2sq(ch + 2)
                          b2 = nextpb()
                          mm(pbs[b2][:], onesf[:], sq[ch % 3][:], True, True, [f"sq{ch % 3}", "onesf"], PBK(b2))
                          r_ = rn[ch % 2]
                          rk_ = f"rn{ch % 2}"
                          rsqrt(r_[:], pbs[b2][:], 1.0, epsc[:, 0:1], PBK(b2) + ["epsc"], [rk_])
                          oi = nextost()
                          sc_ = (128.0 ** -0.5) if ch < 4 else 1.0
                          stt("dve", ost[oi][:], sact[ch][:], sc_, r_[:], ALU.mult, ALU.mult, [f"sact{ch}", rk_], [f"ost{oi}"])
                          hh = ch % 4
                          dma("sp", f"ost{oi}", (GQ if ch < 4 else GK)[hh, :, tok], ost[oi][:], r=[f"ost{oi}"],
                              w=[("GQ" if ch < 4 else "GK", hh, g)])
                          if ch % 2 == 1:
                              ntrans(ch // 2)
                      b = proj(2816, 8)
                      bet, ea, ga, gb2 = abw
                      act(bet[:], pbs[b][0:8, :], AF.Exp, PBK(b), ["abw0"], scale=-1.0)
                      tsc("dve", bet[:], bet[:], 1.0, None, ALU.add, None, ["abw0"], ["abw0"])
                      rcp(bet[:], bet[:], ["abw0"], ["abw0"])
                      act(ea[:], pbs[b][0:8, :], AF.Exp, PBK(b) + ["dtb"], ["abw1"], bias=dtb[:, l:l + 1])
                      act(ea[:], ea[:], AF.Ln, ["abw1"], ["abw1"], bias=1.0)
                      tsc("pool", ga[:], ea[:], nA[:, l:l + 1], None, ALU.mult, None, ["abw1", "nA"], ["abw2"])
                      src_, sk_, dst_, dk2 = ga, "abw2", gb2, "abw3"
                      for s_ in (1, 2, 4, 8, 16, 32):
                          sv = src_[:].rearrange("p (n c) -> p n c", c=64)
                          dv = dst_[:].rearrange("p (n c) -> p n c", c=64)
                          cp("pool", dv[:, :, 0:s_], sv[:, :, 0:s_], [sk_], [dk2])
                          tt("pool", dv[:, :, s_:64], sv[:, :, s_:64], sv[:, :, 0:64 - s_], ALU.add, [sk_], [dk2])
                          src_, sk_, dst_, dk2 = dst_, dk2, src_, sk_
                      stt("dve", bet[:], bet[:], small[0:8, 3:4], src_[:], ALU.mult, ALU.add, ["abw0", sk_, "small"], ["abw0"])
                      dma("sp", "abw0", COMB[:, tok], bet[:], r=["abw0"], w=[("COMB", g)])
                  P.barrier()
              if stop_after == "A" or stop_after == "W" or (stop_after or "").startswith("g"):
                  break

              with ExitStack() as pb_:
                  KTs = [sbuf(pb_, f"KTs{i}", [128, S], BF16) for i in range(2)]
                  KPs = sbuf(pb_, "KPs", [128, S], BF16)
                  Vs = [sbuf(pb_, f"Vs{i}", [128, NT, 128], BF16) for i in range(2)]
                  NQ = 4
                  NPT = 6
                  LAG = 2
                  qn = [sbuf(pb_, f"qn{i}", [128, 512], BF16) for i in range(NQ)]
                  qp = [sbuf(pb_, f"qp{i}", [128, 512], BF16) for i in range(NQ)]
                  zt = [sbuf(pb_, f"zt{i}", [128, 512], BF16) for i in range(NQ)]
                  PT = [sbuf(pb_, f"PT{i}", [128, 512], BF16) for i in range(NPT)]
                  accD = [sbuf(pb_, f"accD{i}", [128, 512], F32) for i in range(2)]
                  accP = [sbuf(pb_, f"accP{i}", [128, 512], F32) for i in range(2)]
                  rcs = [sbuf(pb_, f"rcs{i}", [128, 512], F32) for i in range(2)]
                  yo = [sbuf(pb_, f"yo{i}", [128, 512], BF16) for i in range(2)]
                  SCL = 192.0 ** -0.5
                  allg = list(range(NG))
                  memset("pool", KPs[64:128, :], 0.0, ["KPs"])
                  for i in range(NQ):
                      memset("pool", qp[i][64:128, :], 0.0, [f"qp{i}"])
                  dma("sp", "KPs", KPs[0:64, :], KPE, r=[("KPE", g) for g in allg], w=["KPs"])

                  def load_head(h):
                      i = h % 2
                      dma("sp", f"KTs{i}", KTs[i][:], KT[h], r=[("KT", h, g) for g in allg], w=[f"KTs{i}"])
                      vsrc = VV[:, h * 128:(h + 1) * 128].rearrange("(t p) f -> p t f", p=128)
                      nsp = max(1, NT // 16)
                      for ii in range(nsp):
                          tsl = slice(ii * (NT // nsp), (ii + 1) * (NT // nsp))
                          dma("sp", f"Vs{i}", Vs[i][:, tsl, :], vsrc[:, tsl, :], r=[("VV", g) for g in allg], w=[f"Vs{i}"])

                  units = []
                  groups = []
                  for h in range(4):
                      for g in range(NG):
                          gi = len(groups)
                          groups.append((h, g))
                          for jp in range(2 * (g + 1)):
                              units.append((h, g, jp, gi))
                  nU = len(units)
                  loaded = set()
                  touched = {}
                  NP2 = 4
                  PT2 = [sbuf(pb_, f"PT2_{i}", [128, 2, 512], BF16) for i in range(NP2)]
                  pairb = [sbuf(pb_, f"pairb{i}", [128, 512], BF16) for i in range(2)]

                  def ensure_group(gi):
                      if gi in loaded or gi >= len(groups):
                          return
                      loaded.add(gi)
                      h, g = groups[gi]
                      tok = slice(g * 512, (g + 1) * 512)
                      qi = gi % NQ
                      dma("sp", f"qn{qi}", qn[qi][:], QT[h, 0:128, tok], r=[("QT", h, g)], w=[f"qn{qi}"])
                      dma("sp", f"qp{qi}", qp[qi][0:64, :], QT[h, 128:192, tok], r=[("QT", h, g)], w=[f"qp{qi}"])
                      dma("sp", f"zt{qi}", zt[qi][:], ZM[h, :, tok], r=[("ZM", g)], w=[f"zt{qi}"])

                  load_head(0)
                  LAGP = 1
                  for s_ in range(nU + LAGP):
                      if s_ < nU:
                          h, g, jp, gi = units[s_]
                          if jp == 0:
                              ensure_group(gi)
                              ensure_group(gi + 1)
                              ensure_group(gi + 2)
                          qi = gi % NQ
                          hb = h % 2
                          b0 = 2 * (s_ % 2)
                          pi = s_ % NP2
                          for e_ in range(2):
                              j = 2 * jp + e_
                              ks = slice(j * 128, (j + 1) * 128)
                              mm(pbs[b0 + e_][:], KTs[hb][:, ks], qn[qi][:], True, False, [f"KTs{hb}", f"qn{qi}"], PBK(b0 + e_))
                              mm(pbs[b0 + e_][:], KPs[:, ks], qp[qi][:], False, True, ["KPs", f"qp{qi}"], PBK(b0 + e_))
                      if s_ - LAGP >= 0:
                          uh, ug, ujp, ugi = units[s_ - LAGP]
                          ob = 4 + ugi % 2
                          ppi = (s_ - LAGP) % NP2
                          lastp = ujp == 2 * (ug + 1) - 1
                          for e_ in range(2):
                              uj = 2 * ujp + e_
                              mm(pbs[ob][:], Vs[uh % 2][:, uj, :], PT2[ppi][:, e_, :], uj == 0, lastp and e_ == 1,
                                 [f"Vs{uh % 2}", f"PT2_{ppi}"], PBK(ob))
                          if lastp:
                              a2i = ugi % 2
                              uqi = ugi % NQ
                              mm(pbs[6][:], onesf[:], accD[a2i][:], True, True, [f"accD{a2i}", "onesf"], PBK(6))
                              rcp(rcs[a2i][:], pbs[6][:], PBK(6), [f"rcs{a2i}"])
                              tt("dve", rcs[a2i][:], rcs[a2i][:], pbs[ob][:], ALU.mult, [f"rcs{a2i}"] + PBK(ob), [f"rcs{a2i}"])
                              tt("dve", yo[a2i][:], rcs[a2i][:], zt[uqi][:], ALU.mult, [f"rcs{a2i}", f"zt{uqi}"], [f"yo{a2i}"])
                              dma("sp", f"yo{a2i}", YT[uh, :, ug * 512:(ug + 1) * 512], yo[a2i][:], r=[f"yo{a2i}"], w=[("YT", uh, ug)])
                          if ug == 0 and ujp == 1 and uh + 1 < 4:
                              load_head(uh + 1)
                      if s_ < nU:
                          pk = f"PT2_{pi}"
                          psrc = pbs[b0][:].rearrange("p (o n) -> p o n", o=1)
                          act(PT2[pi][:, 0, :], pbs[b0][:], AF.Exp, PBK(b0), [pk], scale=SCL)
                          act(PT2[pi][:, 1, :], pbs[b0 + 1][:], AF.Exp, PBK(b0 + 1), [pk], scale=SCL)
                          if 2 * jp >= 4 * g:
                              m0 = 2 * jp - 4 * g
                              tt("pool", PT2[pi][:], PT2[pi][:], attm[:, m0:m0 + 2, :], ALU.mult, [pk, "attm"], [pk])
                          a2i = gi % 2
                          pb2 = pairb[s_ % 2]
                          pbk = f"pairb{s_ % 2}"
                          tt("dve", pb2[:], PT2[pi][:, 0, :], PT2[pi][:, 1, :], ALU.add, [pk], [pbk])
                          if not touched.get(gi, False):
                              cp("dve", accD[a2i][:], pb2[:], [pbk], [f"accD{a2i}"])
                          else:
                              tt("dve", accD[a2i][:], accD[a2i][:], pb2[:], ALU.add, [pbk, f"accD{a2i}"], [f"accD{a2i}"])
                          touched[gi] = True
                  P.barrier()
              if stop_after == "B":
                  break

              with ExitStack() as pc:
                  combs = sbuf(pc, "combs", [128, 512], F32)
                  combT = sbuf(pc, "combT", [128, 4, 8], F32)
                  tok4 = sbuf(pc, "tok4", [128, 4, 20], F32)
                  gcb = [sbuf(pc, f"gcb{h}", [128, 512], F32) for h in range(4)]
                  btb = [sbuf(pc, f"btb{h}", [128, 512], F32) for h in range(4)]
                  gqs = [sbuf(pc, f"gqs{h}", [128, 512], BF16) for h in range(4)]
                  gks = [sbuf(pc, f"gks{h}", [128, 512], BF16) for h in range(4)]
                  gvs = [sbuf(pc, f"gvs{h}", [128, 512], BF16) for h in range(4)]
                  kbT = [sbuf(pc, f"kbT{h}", [128, 512], BF16) for h in range(4)]
                  egb = [[sbuf(pc, f"egb{h}_{p}", [128, 512], F32) for p in range(2)] for h in range(4)]
                  zgs = [[sbuf(pc, f"zgs{h}_{p}", [128, 512], BF16) for p in range(2)] for h in range(4)]
                  qdT = [[sbuf(pc, f"qdT{h}_{p}", [128, 512], BF16) for p in range(2)] for h in range(4)]
                  ygs = [[sbuf(pc, f"ygs{h}_{p}", [128, 512], BF16) for p in range(2)] for h in range(4)]
                  Sf = [sbuf(pc, f"Sf{h}", [128, 128], F32) for h in range(4)]
                  Sb = [sbuf(pc, f"Sb{h}", [128, 128], BF16) for h in range(4)]

                  def mk(name, dt, n=1):
                      return [[sbuf(pc, f"{name}{h}_{i}", [128, 128], dt) for i in range(n)] for h in range(4)]

                  Dm = mk("Dm", F32)
                  DTs = mk("DTs", F32)
                  DTi = mk("DTi", F32)
                  Ab = mk("Ab", F32, 2)
                  Bb = mk("Bb", F32, 2)
                  Pb = mk("Pb", F32, 2)
                  TTb = mk("TTb", BF16)
                  kbg = mk("kbg", BF16)
                  vb = mk("vb", BF16)
                  QKT = mk("QKT", BF16, 2)
                  kd = mk("kd", BF16, 4)
                  usb = mk("usb", F32, 2)
                  wT = mk("wT", BF16, 2)
                  vn = mk("vn", BF16)
                  on = mk("on", BF16)
                  ost2 = sbuf(pc, "ost2", [128, 4, 4], F32)
                  junk2 = sbuf(pc, "junk2", [64, 128], BF16)
                  memset("pool", combs[:], 0.0, ["combs"])
                  for h in range(4):
                      memset("pool", vn[h][0][:], 0.0, [f"vn{h}"])
                      memset("pool", on[h][0][:], 0.0, [f"on{h}"])
                      memset("pool", Sf[h][:], 0.0, [f"Sf{h}"])
                      memset("pool", Sb[h][:], 0.0, [f"Sb{h}"])
                  pctr = [0]
                  sctr = [0]

                  def nq():
                      i = pctr[0] % 16
                      pctr[0] += 1
                      b, q = i % 4, i // 4
                      nq.last = (b, q)
                      return pbs[b][:, q * 128:(q + 1) * 128], [("pb", b)]

                  def nqb():
                      ap, k = nq()
                      b, q = nq.last
                      return pbs[b][:].bitcast(BF16)[:, q * 256:q * 256 + 128], [("pb", b)]

                  def sq_():
                      i = sctr[0] % 12
                      sctr[0] += 1
                      b, q = 4 + i % 3, i // 3
                      return pbs[b][:, q * 128:(q + 1) * 128], [("pb", b)]

                  HH = range(4)

                  def setup(g):
                      gp = g % 2
                      tok = slice(g * 512, (g + 1) * 512)
                      dma("sp", "combs", combs[0:8, :], COMB[:, tok], r=[("COMB", g)], w=["combs"])
                      for h in HH:
                          dma("sp", f"gqs{h}", gqs[h][:], GQ[h, :, tok], r=[("GQ", h, g)], w=[f"gqs{h}"])
                          dma("sp", f"gks{h}", gks[h][:], GK[h, :, tok], r=[("GK", h, g)], w=[f"gks{h}"])
                          dma("sp", f"gvs{h}", gvs[h][:], GV[h, :, tok], r=[("GV", h, g)], w=[f"gvs{h}"])
                          dma("sp", f"zgs{h}_{gp}", zgs[h][gp][:], ZG[h, :, tok], r=[("ZG", g)], w=[f"zgs{h}_{gp}"])
                      for h in HH:
                          b1, b2 = (2 * h) % 4, (2 * h + 1) % 4
                          mm(pbs[b1][:], sel[:, h, :], combs[:], True, True, ["sel", "combs"], PBK(b1))
                          cp("act", gcb[h][:], pbs[b1][:], PBK(b1), [f"gcb{h}"])
                          act(egb[h][gp][:], pbs[b1][:], AF.Exp, PBK(b1), [f"egb{h}_{gp}"])
                          mm(pbs[b2][:], sel[:, 4 + h, :], combs[:], True, True, ["sel", "combs"], PBK(b2))
                          cp("act", btb[h][:], pbs[b2][:], PBK(b2), [f"btb{h}"])
                          tt("dve", kbT[h][:], gks[h][:], btb[h][:], ALU.mult, [f"gks{h}", f"btb{h}"], [f"kbT{h}"])
                          tt("pool", qdT[h][gp][:], gqs[h][:], egb[h][gp][:], ALU.mult, [f"gqs{h}", f"egb{h}_{gp}"], [f"qdT{h}_{gp}"])
                      for t in range(4):
                          ap, k = nq()
                          tr(ap, combs[:, t * 128:(t + 1) * 128], identf[:], ["combs", "identf"], k)
                          cp("dve", combT[:, t, :], ap[:, 0:8], k, [("combT", t)])
                          act(tok4[:, t, 0:4], combT[:, t, 0:4], AF.Exp, [("combT", t)], [("tok4", t)])
                          tt("dve", tok4[:, t, 4:8], combT[:, t, 4:8], tok4[:, t, 0:4], ALU.mult, [("combT", t), ("tok4", t)], [("tok4", t)])
                          for h in HH:
                              for c2 in range(2):
                                  rs = slice(c2 * 64, (c2 + 1) * 64)
                                  lc = t * 128 + c2 * 64 + 63
                                  tt("dve", tok4[rs, t, 8 + h:9 + h], gcb[h][rs, lc:lc + 1], combT[rs, t, h:h + 1], ALU.subtract,
                                     [f"gcb{h}", ("combT", t)], [("tok4", t)])
                          act(tok4[:, t, 8:12], tok4[:, t, 8:12], AF.Exp, [("tok4", t)], [("tok4", t)])
                          tsc("dve", tok4[:, t, 12:16], tok4[:, t, 8:12], pmask[:, 0:1], None, ALU.mult, None, [("tok4", t), "pmask"], [("tok4", t)])
                          tsc("dve", tok4[:, t, 16:20], tok4[:, t, 8:12], pmask[:, 1:2], None, ALU.mult, None, [("tok4", t), "pmask"], [("tok4", t)])

                  def par_steps(g, t):
                      tp = (g * 4 + t) % 2
                      ts_ = slice(t * 128, (t + 1) * 128)
                      steps = []

                      def p0():
                          for h in HH:
                              tsc("dve", Dm[h][0][:], gcb[h][:, ts_], combT[:, t, h:h + 1], 0.0, ALU.subtract, ALU.max,
                                  [f"gcb{h}", ("combT", t)], [f"Dm{h}"])
                              act(Dm[h][0][:], Dm[h][0][:], AF.Exp, [f"Dm{h}"], [f"Dm{h}"], scale=-1.0)
                              tsc("dve", DTs[h][0][:], gcb[h][:, ts_], combT[:, t, h:h + 1], 0.0, ALU.subtract, ALU.min,
                                  [f"gcb{h}", ("combT", t)], [f"DTs{h}"])
                              act(DTs[h][0][:], DTs[h][0][:], AF.Exp, [f"DTs{h}"], [f"DTs{h}"])
                              tt("pool", Dm[h][0][:], Dm[h][0][:], gmask[:, 0, :], ALU.mult, [f"Dm{h}", "gmask"], [f"Dm{h}"])
                              tt("pool", DTi[h][0][:], DTs[h][0][:], gmask[:, 2, :], ALU.mult, [f"DTs{h}", "gmask"], [f"DTi{h}"])
                              tt("pool", DTs[h][0][:], DTs[h][0][:], gmask[:, 1, :], ALU.mult, [f"DTs{h}", "gmask"], [f"DTs{h}"])
                          for h in HH:
                              ap, k = nq()
                              mm(ap, kbT[h][:, ts_], gks[h][:, ts_], True, True, [f"kbT{h}", f"gks{h}"], k)
                              tt("dve", Ab[h][0][:], ap, Dm[h][0][:], ALU.mult, k + [f"Dm{h}"], [f"Ab{h}_0"])
                              ap, k = nq()
                              mm(ap, gks[h][:, ts_], kbT[h][:, ts_], True, True, [f"kbT{h}", f"gks{h}"], k)
                              tt("dve", Bb[h][0][:], ap, DTs[h][0][:], ALU.mult, k + [f"DTs{h}"], [f"Bb{h}_0"])
                              ap, k = nq()
                              mm(ap, gks[h][:, ts_], gqs[h][:, ts_], True, True, [f"gqs{h}", f"gks{h}"], k)
                              tt("dve", QKT[h][tp][:], ap, DTi[h][0][:], ALU.mult, k + [f"DTi{h}"], [f"QKT{h}_{tp}"])
                              tt("pool", Pb[h][0][:], identf[:], Bb[h][0][:], ALU.subtract, ["identf", f"Bb{h}_0"], [f"Pb{h}_0"])
                      steps.append(p0)

                      def mklevel(kk):
                          def lev():
                              ci, co = (kk - 1) % 2, kk % 2
                              for h in HH:
                                  ap, k = nq()
                                  mm(ap, Bb[h][ci][:], Ab[h][ci][:], True, True, [f"Bb{h}_{ci}", f"Ab{h}_{ci}"], k)
                                  cp("act", Ab[h][co][:], ap, k, [f"Ab{h}_{co}"])
                                  if kk < 5:
                                      ap2, k2 = nq()
                                      mm(ap2, Ab[h][ci][:], Bb[h][ci][:], True, True, [f"Bb{h}_{ci}", f"Ab{h}_{ci}"], k2)
                                      cp("act", Bb[h][co][:], ap2, k2, [f"Bb{h}_{co}"])
                              for h in HH:
                                  ap, k = nq()
                                  mm(ap, Ab[h][co][:], Pb[h][ci][:], True, True, [f"Ab{h}_{co}", f"Pb{h}_{ci}"], k)
                                  tt("dve", Pb[h][co][:], ap, Pb[h][ci][:], ALU.add, k + [f"Pb{h}_{ci}"], [f"Pb{h}_{co}"])
                          return lev
                      for kk in range(1, 6):
                          steps.append(mklevel(kk))

                      def p6():
                          PF = 5 % 2
                          for h in HH:
                              cp("act", TTb[h][0][:], Pb[h][PF][:], [f"Pb{h}_{PF}"], [f"TTb{h}"])
                              ap, k = nqb()
                              tr(ap, gks[h][:, ts_], identb[:], [f"gks{h}", "identb"], k)
                              tsc("dve", kbg[h][0][:], ap, tok4[:, t, 4 + h:5 + h], None, ALU.mult, None, k + [("tok4", t)], [f"kbg{h}"])
                              tsc("dve", kd[h][2 * tp][:], ap, tok4[:, t, 12 + h:13 + h], None, ALU.mult, None, k + [("tok4", t)], [f"kd{h}_{tp}"])
                              tsc("dve", kd[h][2 * tp + 1][:], ap, tok4[:, t, 16 + h:17 + h], None, ALU.mult, None, k + [("tok4", t)], [f"kd{h}_{tp}"])
                              ap, k = nqb()
                              tr(ap, gvs[h][:, ts_], identb[:], [f"gvs{h}", "identb"], k)
                              tsc("dve", vb[h][0][:], ap, combT[:, t, 4 + h:5 + h], None, ALU.mult, None, k + [("combT", t)], [f"vb{h}"])
                          for h in HH:
                              ap, k = nq()
                              mm(ap, TTb[h][0][:], vb[h][0][:], True, True, [f"TTb{h}", f"vb{h}"], k)
                              cp("act", usb[h][tp][:], ap, k, [f"usb{h}_{tp}"])
                              ap, k = nq()
                              mm(ap, kbg[h][0][:], TTb[h][0][:], True, True, [f"TTb{h}", f"kbg{h}"], k)
                              cp("act", wT[h][tp][:], ap, k, [f"wT{h}_{tp}"])
                      steps.append(p6)
                      return steps

                  def seq_steps(g, t):
                      gp = g % 2
                      tp = (g * 4 + t) % 2
                      ts_ = slice(t * 128, (t + 1) * 128)
                      tok = slice(g * 512, (g + 1) * 512)
                      steps = []
                      yps = {h: (pbs[7][:].bitcast(BF16)[:, h * 256:(h + 1) * 256].rearrange("p (c t) -> p c t", c=2), [("pb", 7)]) for h in HH}
                      state = {}
                      for c2 in range(2):
                          rs = slice(c2 * 64, (c2 + 1) * 64)
                          cs_ = slice(t * 128 + c2 * 64, t * 128 + (c2 + 1) * 64)
                          lc = t * 128 + c2 * 64 + 63

                          def sa(c2=c2, rs=rs):
                              wsp = {}
                              for h in HH:
                                  ap, k = sq_()
                                  mm(ap[0:64, :], wT[h][tp][:, rs], Sb[h][:], True, True, [f"wT{h}_{tp}", f"Sb{h}"], k)
                                  wsp[h] = (ap, k)
                              for h in HH:
                                  ap, k = wsp[h]
                                  tt("dve", vn[h][0][rs, :], usb[h][tp][rs, :], ap[0:64, :], ALU.subtract, k + [f"usb{h}_{tp}"], [f"vn{h}"])

                          def sb_(c2=c2, rs=rs, cs_=cs_, lc=lc):
                              osp = {}
                              for h in HH:
                                  ap, k = sq_()
                                  mm(ap[0:64, :], qdT[h][gp][:, cs_], Sb[h][:], True, False, [f"qdT{h}_{gp}", f"Sb{h}"], k)
                                  mm(ap[0:64, :], QKT[h][tp][:, rs], vn[h][0][:, :], False, True, [f"QKT{h}_{tp}", f"vn{h}"], k)
                                  osp[h] = (ap, k)
                                  ap2, k2 = sq_()
                                  mm(ap2, kd[h][2 * tp + c2][:, :], vn[h][0][:, :], True, True, [f"kd{h}_{tp}", f"vn{h}"], k2)
                                  stt("dve", Sf[h][:], Sf[h][:], egb[h][gp][:, lc:lc + 1], ap2, ALU.mult, ALU.add,
                                      k2 + [f"Sf{h}", f"egb{h}_{gp}"], [f"Sf{h}"])
                                  cp("act", Sb[h][:], Sf[h][:], [f"Sf{h}"], [f"Sb{h}"])
                              state[c2] = osp

                          def sc(c2=c2):
                              osp = state[c2]
                              for h in HH:
                                  ap, k = osp[h]
                                  act(junk2[:], ap[0:64, :], AF.Square, k, ["junk2", ("ost2", h)], accum=ost2[0:64, h, 0:1])
                                  rsqrt(ost2[0:64, h, 1:2], ost2[0:64, h, 0:1], 1.0 / 128.0, epsc[0:64, 0:1], [("ost2", h), "epsc"], [("ost2", h)])
                                  tsc("dve", on[h][0][0:64, :], ap[0:64, :], ost2[0:64, h, 1:2], None, ALU.mult, None, k + [("ost2", h)], [f"on{h}"])
                                  yap, yk = yps[h]
                                  tr(yap[:, c2, :], on[h][0][:, :], identb[:], [f"on{h}", "identb"], yk)
                          steps += [sa, sb_, sc]

                      def sf():
                          for h in HH:
                              yap, yk = yps[h]
                              stt("dve", ygs[h][gp][:, ts_].rearrange("p (c t) -> p c t", c=2), yap[:, :, 0:64], cols[:, l, 29:30],
                                  zgs[h][gp][:, ts_].rearrange("p (c t) -> p c t", c=2), ALU.mult, ALU.mult,
                                  yk + [("cols", l), f"zgs{h}_{gp}"], [f"ygs{h}_{gp}"])
                          if t == 3:
                              for h in HH:
                                  dma("sp", f"ygs{h}_{gp}", YT[4 + h, :, tok], ygs[h][gp][:], r=[f"ygs{h}_{gp}"], w=[("YT", 4 + h, g)])
                      steps.append(sf)
                      return steps

                  prev = None
                  for g in range(NG):
                      setup(g)
                      for t in range(4):
                          ps_l = par_steps(g, t)
                          ss_l = seq_steps(*prev) if prev is not None else []
                          for i in range(max(len(ps_l), len(ss_l))):
                              if i < len(ps_l):
                                  ps_l[i]()
                              if i < len(ss_l):
                                  ss_l[i]()
                          prev = (g, t)
                  for f_ in seq_steps(*prev):
                      f_()
                  P.barrier()
              if stop_after == "C":
                  break

              with ExitStack() as pd:
                  WOb = sbuf(pd, "WOb", [128, 8, D], BF16)
                  stg = [sbuf(pd, f"stgd{i}", [128, D], F32) for i in range(2)]
                  for c in range(8):
                      sk = f"stgd{c % 2}"
                      dma("sp", sk, stg[c % 2][:], w_out[l, c * 128:(c + 1) * 128, :], w=[sk])
                      cp("dve" if c % 2 == 0 else "pool", WOb[:, c, :], stg[c % 2][:], [sk], ["WOb"])
                  NS = 3
                  yt = [sbuf(pd, f"yt{i}", [128, 8, 128], BF16) for i in range(NS)]
                  xd = [sbuf(pd, f"xd{i}", [128, D], F32) for i in range(NS)]
                  xo = [sbuf(pd, f"xo{i}", [128, D], F32) for i in range(NS)]
                  junk3 = sbuf(pd, "junk3", [128, D], BF16)
                  st3 = sbuf(pd, "st3", [128, NS, 4], F32)

                  def d_load(ti):
                      if ti >= NT:
                          return
                      g = ti // 4
                      i2 = ti % NS
                      dma("sp", f"yt{i2}", yt[i2][:], YT[:, :, ti * 128:(ti + 1) * 128].rearrange("c p t -> p c t"),
                          r=[("YT", c, g) for c in range(8)], w=[f"yt{i2}"])
                      dma("sp", f"xd{i2}", xd[i2][:], xsrc[ti * 128:(ti + 1) * 128, :], r=[(xkey, ti)], w=[f"xd{i2}"])

                  def d_mm(ti):
                      i2 = ti % NS
                      b0 = 2 * (ti % 4)
                      for half in range(2):
                          for c in range(8):
                              mm(pbs[b0 + half][:], yt[i2][:, c, :], WOb[:, c, half * 512:(half + 1) * 512], c == 0, c == 7,
                                 [f"yt{i2}", "WOb"], PBK(b0 + half))

                  def d_epi(ti):
                      i2 = ti % NS
                      b0 = 2 * (ti % 4)
                      sk3 = ("st3", i2)
                      for half in range(2):
                          act(junk3[:, half * 512:(half + 1) * 512], pbs[b0 + half][:], AF.Square, PBK(b0 + half), ["junk3", sk3],
                              accum=st3[:, i2, half:half + 1])
                      tt("dve", st3[:, i2, 2:3], st3[:, i2, 0:1], st3[:, i2, 1:2], ALU.add, [sk3], [sk3])
                      rsqrt(st3[:, i2, 3:4], st3[:, i2, 2:3], 1.0 / D, epsc[:, 0:1], [sk3, "epsc"], [sk3])
                      for half in range(2):
                          hs = slice(half * 512, (half + 1) * 512)
                          stt("dve", xo[i2][:, hs], pbs[b0 + half][:], st3[:, i2, 3:4], gpb[:, hs], ALU.mult, ALU.mult,
                              PBK(b0 + half) + [sk3, "gpb"], [f"xo{i2}"])
                      tt("dve", xo[i2][:], xo[i2][:], xd[i2][:], ALU.add, [f"xo{i2}", f"xd{i2}"], [f"xo{i2}"])
                      dma("sp", f"xo{i2}", xdst[ti * 128:(ti + 1) * 128, :], xo[i2][:], r=[f"xo{i2}"], w=[(xdkey, ti)])

                  d_load(0)
                  d_load(1)
                  for ti in range(NT + 1):
                      if ti < NT:
                          d_mm(ti)
                      if ti >= 1:
                          d_epi(ti - 1)
                      d_load(ti + 2)
                  P.barrier()
        except _Stop:
            pass
        P.final_wait("sp", [("OUT", ti) for ti in range(NT)])
        build.stats = (dict(P.n_inst), dict(P.n_wait), P.nsem)
    return nc


def make_consts():
    ident = np.eye(128, dtype=np.float32)
    attm = np.zeros((128, 4, 512), np.float32)
    k = np.arange(128)[:, None]
    q = np.arange(512)[None, :]
    for j in range(4):
        attm[:, j, :] = (q >= 128 * j + k).astype(np.float32)
    i = np.arange(128)[:, None]
    jj = np.arange(128)[None, :]
    same = (i // 64) == (jj // 64)
    gm = np.zeros((128, 3, 128), np.float32)
    gm[:, 0, :] = (same & (i > jj))
    gm[:, 1, :] = (same & (i < jj))
    gm[:, 2, :] = (same & (i <= jj))
    sel = np.zeros((128, 8, 128), np.float32)
    for r in range(8):
        sel[r, r, :] = 1.0
    small = np.zeros((64, 4), np.float32)
    half = 32
    invf = np.power(np.float32(10000.0), -np.arange(half, dtype=np.float32) * np.float32(2.0) / np.float32(64)).astype(np.float32)
    small[:, 0] = np.concatenate([invf, invf])
    small[:32, 1] = -1.0
    small[32:, 1] = 1.0
    small[0:4, 2] = -1.0
    small[4:8, 3] = 1.0
    return dict(c_ident=ident, c_attm=attm.reshape(128, 2048), c_gmask=gm.reshape(128, 384), c_sel=sel.reshape(128, 1024), c_small=small,
                c_pm=np.stack([(np.arange(128) < 64), (np.arange(128) >= 64)], axis=1).astype(np.float32))


def make_in_map(inputs, b, S, NL):
    f = lambda a: np.ascontiguousarray(np.asarray(a))
    m = {}
    m["x"] = f(inputs["x"][b, :S])
    m["ccol"] = f(np.asarray(inputs["c"][b]).reshape(8, 128).T)
    m["posrep"] = f(np.broadcast_to(np.asarray(inputs["positions"][b, :S])[None, :], (64, S))).astype(np.int32)
    m["w_mod"] = f(inputs["w_mod"][:NL])
    m["b_mod"] = f(inputs["b_mod"][:NL])
    m["prew"] = f(inputs["pre_norm_w"][:NL])
    m["postw"] = f(inputs["post_norm_w"][:NL])
    m["w_in"] = f(inputs["w_in"][:NL])
    m["qnw"] = f(inputs["mla_q_norm_w"][:NL])
    m["q_up"] = f(inputs["mla_q_up"][:NL])
    m["kvnw"] = f(inputs["mla_kv_norm_w"][:NL])
    m["kv_up"] = f(inputs["mla_kv_up"][:NL])
    m["convw"] = f(np.asarray(inputs["gdn_conv_w"][:NL]).reshape(NL, 4 * 1536))
    a8 = np.zeros((8, NL), np.float32)
    a8[0:4, :] = np.asarray(inputs["gdn_a_log"][:NL]).T
    d8 = np.zeros((8, NL), np.float32)
    d8[0:4, :] = np.asarray(inputs["gdn_dt_bias"][:NL]).T
    m["alog8"] = a8
    m["dtb8"] = d8
    m["onw"] = f(inputs["gdn_o_norm_w"][:NL])
    m["w_out"] = f(inputs["w_out"][:NL])
    m.update(make_consts())
    return m


_NC_CACHE = {}


def kernel(**inputs):
    B, S, _ = inputs["x"].shape
    NL = inputs["w_in"].shape[0]
    key = (S, NL)
    if key not in _NC_CACHE:
        _NC_CACHE[key] = build(S, NL)
    nc = _NC_CACHE[key]
    in_maps = [make_in_map(inputs, c % B, S, NL) for c in range(8)]
    res = run_bass_kernel_spmd(nc, in_maps, core_ids=list(range(8)))
    outs = [np.asarray(res.results[b]["out"]) for b in range(B)]
    return np.stack(outs, axis=0).astype(np.float32)
```

```python
import math
import os
CUT = int(os.environ.get('KCUT', '99'))
import numpy as np
from contextlib import ExitStack
import concourse.bass as bass
import concourse.mybir as mybir
from concourse.bass_utils import run_bass_kernel_spmd

F32 = mybir.dt.float32
BF16 = mybir.dt.bfloat16
I32 = mybir.dt.int32
ALU = mybir.AluOpType
AF = mybir.ActivationFunctionType

D = 1024
NIN = 3272
NWB = 3336
EPS = 1e-6
ENGS = ("pe", "act", "dve", "pool", "sp")
SEM_ROT = 30000
HMAP = {"pe": "tensor", "act": "scalar", "dve": "vector", "pool": "gpsimd", "sp": "sync"}


class _Stop(Exception):
    pass


class Prog:
    def __init__(self, nc, stack):
        self.nc = nc
        self.stack = stack
        self.cur_sem = {}
        self.cur_cnt = {e: 0 for e in ENGS}
        self.nsem = 0
        self.all_eng_sems = {e: [] for e in ENGS}
        for e in ENGS:
            self._new_eng_sem(e)
        self.known = {e: {} for e in ENGS}
        self.free_dma = []
        self.retired = []
        self.last_write = {}
        self.readers = {}
        self.dma_sems = {}
        self.n_inst = {e: 0 for e in ENGS}
        self.n_wait = {e: 0 for e in ENGS}

    def _sem(self, name):
        s = self.stack.enter_context(self.nc.semaphore(name))
        self.nsem += 1
        return s

    def _new_eng_sem(self, e):
        if e in self.cur_sem:
            self.retired.append([self.cur_sem[e], self.cur_cnt[e]])
        self.cur_sem[e] = self._sem(f"c_{e}_{self.nsem}")
        self.cur_cnt[e] = 0

    def _deps(self, eng, reads, writes):
        deps = {}

        def add(ev):
            if ev is None:
                return
            s, v = ev
            k = id(s)
            if k not in deps or deps[k][1] < v:
                deps[k] = (s, v)

        for r in reads:
            add(self.last_write.get(r))
        for w in writes:
            add(self.last_write.get(w))
            rd = self.readers.get(w)
            if rd:
                for ev in rd.values():
                    add(ev)
        out = []
        kn = self.known[eng]
        own = id(self.cur_sem[eng])
        for k, (s, v) in deps.items():
            if k == own and eng == "pe":
                continue
            if kn.get(k, 0) >= v:
                continue
            kn[k] = v
            out.append((s, v))
        return out

    def _record(self, ev, reads, writes):
        for r in reads:
            self.readers.setdefault(r, {})[id(ev[0])] = ev
        for w in writes:
            self.last_write[w] = ev
            self.readers[w] = {}

    def _emit_now(self, eng, waits, fn, sem, inc):
        e = getattr(self.nc, HMAP[eng])
        for (s, v) in waits:
            e.wait_ge(s, v)
        self.n_wait[eng] += len(waits)
        if fn is not None:
            fn(e).then_inc(sem, inc)
            self.n_inst[eng] += 1

    def op(self, eng, fn, reads=(), writes=()):
        waits = self._deps(eng, reads, writes)
        if self.cur_cnt[eng] >= SEM_ROT:
            self._new_eng_sem(eng)
        sem = self.cur_sem[eng]
        self.cur_cnt[eng] += 1
        val = self.cur_cnt[eng]
        self._emit_now(eng, waits, fn, sem, 1)
        self._record((sem, val), reads, writes)

    def dma(self, q, semname, out, in_, reads=(), writes=(), **kw):
        if semname not in self.dma_sems:
            if self.free_dma:
                self.dma_sems[semname] = self.free_dma.pop()
            else:
                self.dma_sems[semname] = [self._sem("d_" + semname), 0]
        ent = self.dma_sems[semname]
        waits = self._deps(q, reads, writes)
        ent[1] += 16
        sem, val = ent[0], ent[1]
        self._emit_now(q, waits, lambda e: e.dma_start(out=out, in_=in_, **kw), sem, 16)
        self._record((sem, val), reads, writes)

    def barrier(self):
        evs = [(self.cur_sem[e], self.cur_cnt[e]) for e in ENGS if self.cur_cnt[e] > 0]
        evs += [(s, v) for (s, v) in self.dma_sems.values() if v > 0]
        evs += [(s, v) for (s, v) in self.retired if v > 0]
        for eng in ENGS:
            kn = self.known[eng]
            own = id(self.cur_sem[eng])
            waits = []
            for (s, v) in evs:
                if id(s) == own or kn.get(id(s), 0) >= v:
                    continue
                kn[id(s)] = v
                waits.append((s, v))
            self._emit_now(eng, waits, None, None, 0)
        self.free_dma.extend(self.dma_sems.values())
        self.dma_sems = {}
        self.retired = []

    def final_wait(self, eng, keys):
        waits = self._deps(eng, keys, ())
        self._emit_now(eng, waits, None, None, 0)


def build(S, NL, dbg=False, stop_after=None):
    nc = bass.Bass("TRN2", target_bir_lowering=False)
    NG = S // 512
    NT = S // 128

    def din(name, shape, dt=F32):
        return nc.dram_tensor(name, list(shape), dt, kind="ExternalInput").ap()

    def dscr(name, shape, dt):
        return nc.dram_tensor(name, list(shape), dt, kind="ExternalOutput" if dbg else "Internal").ap()

    x_in = din("x", [S, D])
    ccol = din("ccol", [128, 8])
    posrep = din("posrep", [64, S], I32)
    w_mod = din("w_mod", [NL, D, 3 * D])
    b_mod = din("b_mod", [NL, 3 * D])
    prew = din("prew", [NL, D])
    postw = din("postw", [NL, D])
    w_in = din("w_in", [NL, D, NIN])
    qnw = din("qnw", [NL, 384])
    q_up = din("q_up", [NL, 384, 768])
    kvnw = din("kvnw", [NL, 256])
    kv_up = din("kv_up", [NL, 256, 1024])
    convw = din("convw", [NL, 4 * 1536])
    alog8 = din("alog8", [8, NL])
    dtb8 = din("dtb8", [8, NL])
    onw = din("onw", [NL, 128])
    w_out = din("w_out", [NL, D, D])
    c_ident = din("c_ident", [128, 128])
    c_attm = din("c_attm", [128, 4 * 512])
    c_gmask = din("c_gmask", [128, 3 * 128])
    c_sel = din("c_sel", [128, 8 * 128])
    c_pm = din("c_pm", [128, 2])
    c_small = din("c_small", [64, 4])
    out = nc.dram_tensor("out", [S, D], F32, kind="ExternalOutput").ap()

    XR = dscr("XR", [S, D], F32)
    QT = dscr("QT", [4, 192, S], BF16)
    KT = dscr("KT", [4, 128, S], BF16)
    KPE = dscr("KPE", [64, S], BF16)
    VV = dscr("VV", [S, 512], BF16)
    ZM = dscr("ZM", [4, 128, S], BF16)
    ZG = dscr("ZG", [4, 128, S], BF16)
    GQ = dscr("GQ", [4, 128, S], BF16)
    GK = dscr("GK", [4, 128, S], BF16)
    GV = dscr("GV", [4, 128, S], BF16)
    COMB = dscr("COMB", [8, S], F32)
    YT = dscr("YT", [8, 128, S], BF16)
    COS = dscr("COS", [64, S], F32)
    SIN = dscr("SIN", [64, S], F32)
    GP = dscr("GP", [NL, D], F32)

    with ExitStack() as st:
        P = Prog(nc, st)

        uid = [0]

        def sbuf(stack, name, shape, dt):
            uid[0] += 1
            return stack.enter_context(nc.sbuf_tensor(f"{name}_u{uid[0]}", list(shape), dt))

        def mm(o, lhsT, rhs, start, stop, r, w):
            P.op("pe", lambda e: e.matmul(o, lhsT=lhsT, rhs=rhs, start=start, stop=stop), r, w)

        def tr(o, i, ident, r, w):
            P.op("pe", lambda e: e.transpose(o, i, ident), r, w)

        def act(o, i, func, r, w, bias=None, scale=None, accum=None):
            kw = {}
            if bias is not None:
                kw["bias"] = bias
            if scale is not None:
                kw["scale"] = scale
            if accum is not None:
                kw["accum_out"] = accum
            P.op("act", lambda e: e.activation(out=o, in_=i, func=func, **kw), r, w)

        def cp(eng, o, i, r, w):
            if eng == "act":
                P.op("act", lambda e: e.activation(out=o, in_=i, func=AF.Copy), r, w)
            else:
                P.op(eng, lambda e: e.tensor_copy(out=o, in_=i), r, w)

        def tsc(eng, o, i, s1, s2, op0, op1, r, w):
            if op1 is None:
                P.op(eng, lambda e: e.tensor_scalar(out=o, in0=i, scalar1=s1, scalar2=None, op0=op0), r, w)
            else:
                P.op(eng, lambda e: e.tensor_scalar(out=o, in0=i, scalar1=s1, scalar2=s2, op0=op0, op1=op1), r, w)

        def tt(eng, o, a, b, op, r, w):
            P.op(eng, lambda e: e.tensor_tensor(out=o, in0=a, in1=b, op=op), r, w)

        def stt(eng, o, a, sc, b, op0, op1, r, w):
            eng = "dve"
            P.op(eng, lambda e: e.scalar_tensor_tensor(out=o, in0=a, scalar=sc, in1=b, op0=op0, op1=op1), r, w)

        def rcp(o, i, r, w):
            P.op("dve", lambda e: e.reciprocal(out=o, in_=i), r, w)

        def memset(eng, o, val, w):
            P.op(eng, lambda e: e.memset(o, val), (), w)

        def dma(q, sem, o, i, r=(), w=(), **kw):
            P.dma(q, sem, o, i, reads=r, writes=w, **kw)

        def rsqrt(o, i, scale, bias_ap, r, w):
            act(o, i, AF.Ln, r, w, bias=bias_ap, scale=scale)
            act(o, o, AF.Exp, w, w, scale=-0.5)

        pbs = [st.enter_context(nc.psum_tensor(f"pb{i}", [128, 512], F32)) for i in range(8)]

        def PBK(i):
            return [("pb", i)]

        identf = sbuf(st, "identf", [128, 128], F32)
        identb = sbuf(st, "identb", [128, 128], BF16)
        onesf = sbuf(st, "onesf", [128, 128], F32)
        attm = sbuf(st, "attm", [128, 4, 512], BF16)
        gmask = sbuf(st, "gmask", [128, 3, 128], F32)
        sel = sbuf(st, "sel", [128, 8, 128], F32)
        pmask = sbuf(st, "pmask", [128, 2], F32)
        small = sbuf(st, "small", [64, 4], F32)
        epsc = sbuf(st, "epsc", [128, 1], F32)
        cols = sbuf(st, "cols", [128, NL, 80], F32)
        nA = sbuf(st, "nA", [8, NL], F32)
        dtb = sbuf(st, "dtb", [8, NL], F32)
        gpb = sbuf(st, "gpb", [128, D], F32)

        dma("sp", "c0", identf[:], c_ident, w=["identf"])
        dma("sp", "c1", gmask[:].rearrange("p m j -> p (m j)"), c_gmask, w=["gmask"])
        dma("sp", "c2", sel[:].rearrange("p m j -> p (m j)"), c_sel, w=["sel"])
        dma("sp", "c3", small[:], c_small, w=["small"])
        dma("sp", "c3b", pmask[:], c_pm, w=["pmask"])
        dma("sp", "c4", nA[:], alog8, w=["nA"])
        dma("sp", "c5", dtb[:], dtb8, w=["dtb"])
        cp("dve", identb[:], identf[:], ["identf"], ["identb"])
        memset("pool", onesf[:], 1.0, ["onesf"])
        memset("pool", epsc[:], EPS, ["epsc"])
        act(nA[:], nA[:], AF.Exp, ["nA"], ["nA"])
        tsc("dve", nA[:], nA[:], small[0:8, 2:3], None, ALU.mult, None, ["nA", "small"], ["nA"])

        with ExitStack() as ps_:
            amst = sbuf(ps_, "amst", [128, 4 * 512], F32)
            dma("sp", "c6", amst[:], c_attm, w=["amst"])
            cp("dve", attm[:].rearrange("p m j -> p (m j)"), amst[:], ["amst"], ["attm"])

            cact = sbuf(ps_, "cact", [128, 8], F32)
            dma("sp", "c7", cact[:], ccol, w=["cact"])
            act(cact[:], cact[:], AF.Silu, ["cact"], ["cact"])
            NROW = 3072 + 1792 + 6144
            rowbuf = sbuf(ps_, "rowbuf", [1, NROW], F32)
            bmrow = sbuf(ps_, "bmrow", [1, 3072], F32)
            pwrow = sbuf(ps_, "pwrow", [1, D], F32)
            wmst = [sbuf(ps_, f"wmst{i}", [128, 8, 512], F32) for i in range(2)]
            for l in range(NL):
                dma("sp", "r0", bmrow[:], b_mod[l:l + 1, :], w=["bmrow"])
                dma("sp", "r1", rowbuf[0:1, 3072:4096], prew[l:l + 1, :], w=["rowbuf_s"])
                dma("sp", "r1", rowbuf[0:1, 4096:4480], qnw[l:l + 1, :], w=["rowbuf_s"])
                dma("sp", "r1", rowbuf[0:1, 4480:4736], kvnw[l:l + 1, :], w=["rowbuf_s"])
                dma("sp", "r1", rowbuf[0:1, 4736:4864], onw[l:l + 1, :], w=["rowbuf_s"])
                dma("sp", "r1", rowbuf[0:1, 4864:NROW], convw[l:l + 1, :], w=["rowbuf_s"])
                dma("sp", "r2", pwrow[:], postw[l:l + 1, :], w=["pwrow"])
                for cg in range(6):
                    ws_ = wmst[cg % 2]
                    wk = f"wmst{cg % 2}"
                    dma("sp", wk, ws_[:], w_mod[l, :, cg * 512:(cg + 1) * 512].rearrange("(c p) n -> p c n", p=128), w=[wk])
                    pb = pbs[cg % 2]
                    for c in range(8):
                        mm(pb[0:1, :], cact[:, c:c + 1], ws_[:, c, :], c == 0, c == 7, [wk, "cact"], PBK(cg % 2))
                    tt("dve", rowbuf[0:1, cg * 512:(cg + 1) * 512], pb[0:1, :], bmrow[0:1, cg * 512:(cg + 1) * 512], ALU.add,
                       PBK(cg % 2) + ["bmrow"], ["rowbuf_m"])
                nchunk = 16 + 62
                for j in range(nchunk):
                    off = j * 128 if j < 16 else 3072 + (j - 16) * 128
                    mm(pbs[2][:, j:j + 1], rowbuf[0:1, off:off + 128], onesf[0:1, 0:1], True, True,
                       ["rowbuf_m", "rowbuf_s", "onesf"], PBK(2))
                cp("dve", cols[:, l, 0:nchunk], pbs[2][:, 0:nchunk], PBK(2), [("cols", l)])
                stt("dve", cols[:, l, 8:16], cols[:, l, 8:16], 1.0, cols[:, l, 16:24], ALU.add, ALU.mult, [("cols", l)], [("cols", l)])
                tt("dve", pwrow[:], pwrow[:], rowbuf[0:1, 2048:3072], ALU.mult, ["pwrow", "rowbuf_m"], ["pwrow"])
                dma("sp", "r3", GP[l:l + 1, :], pwrow[:], r=["pwrow"], w=[("GP", l)])

            CH = min(2048, S)
            posi = sbuf(ps_, "posi", [64, CH], I32)
            ang = sbuf(ps_, "ang", [64, CH], F32)
            a2 = sbuf(ps_, "a2", [64, CH], F32)
            kf = sbuf(ps_, "kf", [64, CH], F32)
            ki = sbuf(ps_, "ki", [64, CH], I32)
            fx = sbuf(ps_, "fx", [64, CH], F32)
            tab = [sbuf(ps_, f"tab{i}", [64, CH], F32) for i in range(2)]
            C1 = 6.28125
            C2 = 2.0 * math.pi - C1
            for ch in range(S // CH):
                cs = slice(ch * CH, (ch + 1) * CH)
                dma("sp", "rp0", posi[:], posrep[:, cs], w=["posi"])
                cp("dve", ang[:], posi[:], ["posi"], ["ang"])
                tsc("dve", ang[:], ang[:], small[:, 0:1], None, ALU.mult, None, ["ang", "small"], ["ang"])
                for ti, offv in ((0, math.pi / 2), (1, 0.0)):
                    tsc("dve", a2[:], ang[:], offv, None, ALU.add, None, ["ang"], ["a2"])
                    tsc("dve", ki[:], a2[:], 1.0 / (2 * math.pi), None, ALU.mult, None, ["a2"], ["ki"])
                    cp("dve", kf[:], ki[:], ["ki"], ["kf"])
                    stt("dve", a2[:], kf[:], -C1, a2[:], ALU.mult, ALU.add, ["kf", "a2"], ["a2"])
                    stt("dve", a2[:], kf[:], -C2, a2[:], ALU.mult, ALU.add, ["kf", "a2"], ["a2"])
                    tsc("dve", fx[:], a2[:], math.pi, 2 * math.pi, ALU.is_gt, ALU.mult, ["a2"], ["fx"])
                    tt("dve", a2[:], a2[:], fx[:], ALU.subtract, ["a2", "fx"], ["a2"])
                    tsc("dve", fx[:], a2[:], -math.pi, 2 * math.pi, ALU.is_lt, ALU.mult, ["a2"], ["fx"])
                    tt("dve", a2[:], a2[:], fx[:], ALU.add, ["a2", "fx"], ["a2"])
                    tsc("dve", a2[:], a2[:], math.pi, -math.pi, ALU.min, ALU.max, ["a2"], ["a2"])
                    tk = f"tab{ti}"
                    act(tab[ti][:], a2[:], AF.Sin, ["a2"], [tk])
                    if ti == 1:
                        tsc("dve", tab[ti][:], tab[ti][:], small[:, 1:2], None, ALU.mult, None, [tk, "small"], [tk])
                    dma("sp", "rp" + tk, (COS if ti == 0 else SIN)[:, cs], tab[ti][:], r=[tk], w=[("ROPE", ti, ch)])
            P.barrier()
        ROPE_KEYS = [("ROPE", ti, ch) for ti in range(2) for ch in range(S // min(2048, S))]

        def chk(tag):
            if stop_after == tag:
                P.barrier()
                raise _Stop()

        try:
          for l in range(NL if stop_after != "P" else 0):
              xsrc = x_in if l == 0 else XR
              xdst = out if l == NL - 1 else XR
              xkey = "XIN" if l == 0 else "XR"
              xdkey = "OUT" if l == NL - 1 else "XR"
              dma("sp", "gpb", gpb[:], GP[l:l + 1, :].partition_broadcast(128), r=[("GP", l)], w=["gpb"])
              chk("G")

              with ExitStack() as pa:
                  Wb = sbuf(pa, "Wb", [128, 8, NWB], BF16)
                  Qb = sbuf(pa, "Qb", [128, 3, 4, 256], BF16)
                  KVb = sbuf(pa, "KVb", [128, 2, 1024], BF16)
                  HS = NIN // 2
                  pa_w = ExitStack()
                  stg = [sbuf(pa_w, f"stg{i}", [128, HS], F32) for i in range(2)]
                  si = 0
                  ceng = ["dve", "pool"]
                  for c in range(8):
                      for hf in range(2):
                          s_ = stg[si % 2]
                          sk = f"stg{si % 2}"
                          dma("sp", sk, s_[:], w_in[l, c * 128:(c + 1) * 128, hf * HS:(hf + 1) * HS], w=[sk])
                          lo, hi = hf * HS, (hf + 1) * HS
                          for (a, b, dst) in ((0, 704, 0), (672, 704, 704), (640, 672, 736), (704, NIN, 768)):
                              a2_, b2_ = max(a, lo), min(b, hi)
                              if a2_ >= b2_:
                                  continue
                              d0 = dst + (a2_ - a)
                              cp(ceng[si % 2], Wb[:, c, d0:d0 + (b2_ - a2_)], s_[:, a2_ - lo:b2_ - lo], [sk], [("Wb", c)])
                          si += 1
                  for c in range(3):
                      s_ = stg[si % 2]
                      sk = f"stg{si % 2}"
                      dma("sp", sk, s_[:, 0:768], q_up[l, c * 128:(c + 1) * 128, :], w=[sk])
                      sv = s_[:, 0:768].rearrange("p (h f) -> p h f", h=4)
                      qs = cols[:, l, 24 + c:25 + c]
                      for (a, b, dst) in ((0, 128, 0), (128, 192, 128), (160, 192, 192), (128, 160, 224)):
                          tsc("dve", Qb[:, c, :, dst:dst + (b - a)], sv[:, :, a:b], qs, None, ALU.mult, None, [sk, ("cols", l)], ["Qb"])
                      si += 1
                  for c in range(2):
                      s_ = stg[si % 2]
                      sk = f"stg{si % 2}"
                      dma("sp", sk, s_[:, 0:1024], kv_up[l, c * 128:(c + 1) * 128, :], w=[sk])
                      tsc("dve", KVb[:, c, :], s_[:, 0:1024], cols[:, l, 27 + c:28 + c], None, ALU.mult, None, [sk, ("cols", l)], ["KVb"])
                      si += 1
                  WBK = [("Wb", c) for c in range(8)]
                  P.barrier()
                  pa_w.close()
                  NGr = 0 if stop_after == "W" else (int(stop_after[1:]) if (stop_after or "").startswith("g") else NG)

                  xt = [sbuf(pa, f"xt{i}", [128, D], F32) for i in range(2)]
                  xs = [sbuf(pa, f"xs{i}", [128, D], F32) for i in range(4)]
                  junk = sbuf(pa, "junk", [128, D], BF16)
                  st1 = sbuf(pa, "st1", [128, 8], F32)
                  hT = [sbuf(pa, f"hT{i}", [128, 8, 512], BF16) for i in range(2)]
                  qlT = sbuf(pa, "qlT", [128, 3, 512], BF16)
                  kvT = sbuf(pa, "kvT", [128, 2, 512], BF16)
                  sq = [sbuf(pa, f"sq{i}", [128, 512], F32) for i in range(3)]
                  rq = sbuf(pa, "rq", [128, 512], F32)
                  rkv = sbuf(pa, "rkv", [128, 512], F32)
                  rkvt = sbuf(pa, "rkvt", [128, 4], F32)
                  cst = sbuf(pa, "cst", [64, 512], F32)
                  sst = sbuf(pa, "sst", [64, 512], F32)
                  crr = sbuf(pa, "crr", [64, 512], F32)
                  srr = sbuf(pa, "srr", [64, 512], F32)
                  t1 = [sbuf(pa, f"t1_{i}", [64, 512], F32) for i in range(1)] * 2
                  t2 = [sbuf(pa, f"t2_{i}", [64, 512], F32) for i in range(1)] * 2
                  ost = [sbuf(pa, f"ost{i}", [128, 512], BF16) for i in range(6)]
                  vst = [sbuf(pa, f"vst{i}", [128, 512], BF16) for i in range(2)]
                  zst = [sbuf(pa, f"zst{i}", [128, 4, 512], BF16) for i in range(2)]
                  convd = sbuf(pa, "convd", [128, 48, 128], BF16)
                  cvb = [sbuf(pa, f"cvb{i}", [128, 515], BF16) for i in range(3)]
                  cvc = sbuf(pa, "cvc", [128, 12, 3], BF16)
                  sact = [sbuf(pa, f"sact{i}", [128, 512], F32) for i in range(8)]
                  rn = [sbuf(pa, f"rn{i}", [128, 512], F32) for i in range(2)]
                  abw = [sbuf(pa, f"abw{i}", [8, 512], F32) for i in range(4)]
                  memset("pool", cvc[:], 0.0, ["cvc"])
                  for jj in range(4):
                      for ch in range(12):
                          tsc("dve", convd[:, jj * 12 + ch, :], identb[:], cols[:, l, 30 + jj * 12 + ch:31 + jj * 12 + ch], None,
                              ALU.mult, None, ["identb", ("cols", l)], ["convd"])
                  octr = [0]
                  pbr = [2]

                  def nextpb():
                      b = pbr[0]
                      pbr[0] = 2 + (pbr[0] - 2 + 1) % 6
                      return b

                  def nextost():
                      i = octr[0] % 6
                      octr[0] += 1
                      return i

                  def xstat_all(g):
                      for t in range(4):
                          ti = g * 4 + t
                          k2 = f"xt{t % 2}"
                          dma("sp", k2, xt[t % 2][:], xsrc[ti * 128:(ti + 1) * 128, :], r=[(xkey, ti)], w=[k2])
                          act(junk[:], xt[t % 2][:], AF.Square, [k2], ["junk", "st1"], accum=st1[:, t:t + 1])
                      rsqrt(st1[:, 4:8], st1[:, 0:4], 1.0 / D, epsc[:, 0:1], ["st1", "epsc"], ["st1"])
                      for t in range(4):
                          ti = g * 4 + t
                          k2 = f"xt{t % 2}"
                          dma("sp", k2, xt[t % 2][:], xsrc[ti * 128:(ti + 1) * 128, :], r=[(xkey, ti)], w=[k2])
                          tsc("dve", xs[t][:], xt[t % 2][:], st1[:, 4 + t:5 + t], None, ALU.mult, None, [k2, "st1"], [f"xs{t}"])

                  def xtrans(g, t):
                      hh_ = hT[g % 2]
                      hhk = f"hT{g % 2}"
                      for c in range(8):
                          tr(pbs[c // 4][:, (c % 4) * 128:(c % 4 + 1) * 128], xs[t][:, c * 128:(c + 1) * 128], identf[:],
                             [f"xs{t}", "identf"], [("pb", c // 4)])
                      for c in range(8):
                          tsc("dve", hh_[:, c, t * 128:(t + 1) * 128], pbs[c // 4][:, (c % 4) * 128:(c % 4 + 1) * 128],
                              cols[:, l, 8 + c:9 + c], cols[:, l, c:c + 1], ALU.mult, ALU.add,
                              [("pb", c // 4), ("cols", l)], [hhk])

                  if NGr > 0:
                      xstat_all(0)
                      for t in range(4):
                          xtrans(0, t)
                  for g in range(NGr):
                      tok = slice(g * 512, (g + 1) * 512)
                      h_ = hT[g % 2]
                      hk = f"hT{g % 2}"
                      dma("sp", "cst", cst[:], COS[:, tok], r=ROPE_KEYS, w=["cst"])
                      dma("sp", "sst", sst[:], SIN[:, tok], r=ROPE_KEYS, w=["sst"])

                      def proj(col0, ncol):
                          b = nextpb()
                          for c in range(8):
                              mm(pbs[b][0:ncol, :], Wb[:, c, col0:col0 + ncol], h_[:, c, :], c == 0, c == 7, [hk, ("Wb", c)], PBK(b))
                          return b

                      def nxt(t):
                          return


                      def ntrans(t):
                          if g + 1 < NGr:
                              xtrans(g + 1, t)

                      for c in range(3):
                          b = proj(c * 128, 128)
                          cp("act", qlT[:, c, :], pbs[b][:], PBK(b), ["qlT"])
                          act(sq[c][:], pbs[b][:], AF.Square, PBK(b), [f"sq{c}"])
                      bsum = nextpb()
                      for c in range(3):
                          mm(pbs[bsum][:], onesf[:], sq[c][:], c == 0, c == 2, [f"sq{c}", "onesf"], PBK(bsum))
                      rsqrt(rq[:], pbs[bsum][:], 1.0 / 384.0, epsc[:, 0:1], PBK(bsum) + ["epsc"], ["rq"])
                      for c in range(2):
                          b = proj(384 + c * 128, 128)
                          cp("act", kvT[:, c, :], pbs[b][:], PBK(b), ["kvT"])
                          act(sq[c][:], pbs[b][:], AF.Square, PBK(b), [f"sq{c}"])
                      bsum = nextpb()
                      for c in range(2):
                          mm(pbs[bsum][:], onesf[:], sq[c][:], c == 0, c == 1, [f"sq{c}", "onesf"], PBK(bsum))
                      bt_ = nextpb()
                      for t in range(4):
                          for c in range(2):
                              mm(pbs[bt_][:, 8 + t:9 + t], sq[c][:, t * 128:(t + 1) * 128], onesf[:, 0:1],
                                 c == 0, c == 1, [f"sq{c}", "onesf"], PBK(bt_))
                      rsqrt(rkv[:], pbs[bsum][:], 1.0 / 256.0, epsc[:, 0:1], PBK(bsum) + ["epsc"], ["rkv"])
                      rsqrt(rkvt[:], pbs[bt_][:, 8:12], 1.0 / 256.0, epsc[:, 0:1], PBK(bt_) + ["epsc"], ["rkvt"])
                      tt("dve", crr[:], cst[:], rq[0:64, :], ALU.mult, ["cst", "rq"], ["crr"])
                      tt("dve", srr[:], sst[:], rq[0:64, :], ALU.mult, ["sst", "rq"], ["srr"])
                      X = []
                      Y = []

                      def xq(h):
                          b = nextpb()
                          for c in range(3):
                              mm(pbs[b][:], Qb[:, c, h, 0:128], qlT[:, c, :], c == 0, c == 2, ["Qb", "qlT"], PBK(b))
                          oi = nextost()
                          tt("dve", ost[oi][:], pbs[b][:], rq[:], ALU.mult, PBK(b) + ["rq"], [f"ost{oi}"])
                          dma("sp", f"ost{oi}", QT[h, 0:128, tok], ost[oi][:], r=[f"ost{oi}"], w=[("QT", h, g)])
                          b = nextpb()
                          for c in range(3):
                              mm(pbs[b][:], Qb[:, c, h, 128:256], qlT[:, c, :], c == 0, c == 2, ["Qb", "qlT"], PBK(b))
                          tt("dve", t1[0][:], pbs[b][0:64, :], crr[:], ALU.mult, PBK(b) + ["crr"], ["t1_0"])
                          tt("dve", t2[0][:], pbs[b][64:128, :], srr[:], ALU.mult, PBK(b) + ["srr"], ["t2_0"])
                          oi = nextost()
                          tt("pool", ost[oi][0:64, :], t1[0][:], t2[0][:], ALU.add, ["t1_0", "t2_0"], [f"ost{oi}"])
                          dma("sp", f"ost{oi}", QT[h, 128:192, tok], ost[oi][0:64, :], r=[f"ost{oi}"], w=[("QT", h, g)])

                      def xk(h):
                          b = nextpb()
                          for c in range(2):
                              mm(pbs[b][:], KVb[:, c, h * 256:h * 256 + 128], kvT[:, c, :], c == 0, c == 1, ["KVb", "kvT"], PBK(b))
                          oi = nextost()
                          tt("dve", ost[oi][:], pbs[b][:], rkv[:], ALU.mult, PBK(b) + ["rkv"], [f"ost{oi}"])
                          dma("sp", f"ost{oi}", KT[h, :, tok], ost[oi][:], r=[f"ost{oi}"], w=[("KT", h, g)])

                      kvv = KVb[:].rearrange("p c (h f) -> p c h f", h=4)

                      def xv(t):
                          b = nextpb()
                          for c in range(2):
                              mm(pbs[b][:].rearrange("p (h f) -> p h f", h=4), kvT[:, c, t * 128:(t + 1) * 128], kvv[:, c, :, 128:256],
                                 c == 0, c == 1, ["KVb", "kvT"], PBK(b))
                          vi = (g * 4 + t) % 2
                          tsc("dve", vst[vi][:], pbs[b][:], rkvt[:, t:t + 1], None, ALU.mult, None, PBK(b) + ["rkvt"], [f"vst{vi}"])
                          dma("sp", f"vst{vi}", VV[g * 512 + t * 128:g * 512 + (t + 1) * 128, :], vst[vi][:], r=[f"vst{vi}"], w=[("VV", g)])

                      def xkpe():
                          b = proj(640, 128)
                          tt("dve", t1[0][:], pbs[b][0:64, :], cst[:], ALU.mult, PBK(b) + ["cst"], ["t1_0"])
                          tt("dve", t2[0][:], pbs[b][64:128, :], sst[:], ALU.mult, PBK(b) + ["sst"], ["t2_0"])
                          oi = nextost()
                          tt("pool", ost[oi][0:64, :], t1[0][:], t2[0][:], ALU.add, ["t1_0", "t2_0"], [f"ost{oi}"])
                          dma("sp", f"ost{oi}", KPE[:, tok], ost[oi][0:64, :], r=[f"ost{oi}"], w=[("KPE", g)])

                      for h in range(4):
                          X.append(lambda h=h: xq(h))
                      for h in range(4):
                          X.append(lambda h=h: xk(h))
                      for t in range(4):
                          X.append(lambda t=t: xv(t))
                      X.append(xkpe)

                      def yz(zi, c):
                          base, dstD, zk = ((768, ZM, "ZM"), (2824, ZG, "ZG"))[zi]
                          b = proj(base + c * 128, 128)
                          act(zst[zi][:, c, :], pbs[b][:], AF.Silu, PBK(b), [f"zst{zi}"])
                          if c == 3:
                              dma("sp", f"zst{zi}", dstD[:, :, tok].rearrange("c p t -> p c t"), zst[zi][:], r=[f"zst{zi}"], w=[(zk, g)])

                      def conv_pe(ch):
                          sl = ch % 3
                          b2 = nextpb()
                          for j in range(4):
                              mm(pbs[b2][:], convd[:, j * 12 + ch, :], cvb[sl][:, j:j + 512], j == 0, j == 3, ["convd", f"cvb{sl}"], PBK(b2))
                          hh = ch % 4
                          if ch >= 8:
                              oi = nextost()
                              act(ost[oi][:], pbs[b2][:], AF.Silu, PBK(b2), [f"ost{oi}"])
                              dma("sp", f"ost{oi}", GV[hh, :, tok], ost[oi][:], r=[f"ost{oi}"], w=[("GV", hh, g)])
                          else:
                              act(sact[ch][:], pbs[b2][:], AF.Silu, PBK(b2), [f"sact{ch}"])

                      def yconv(ch):
                          sl = ch % 3
                          b = proj(1280 + ch * 128, 128)
                          cp("act", cvb[sl][:, 3:515], pbs[b][:], PBK(b), [f"cvb{sl}"])
                          cp("pool", cvb[sl][:, 0:3], cvc[:, ch, :], ["cvc"], [f"cvb{sl}"])
                          cp("pool", cvc[:, ch, :], cvb[sl][:, 512:515], [f"cvb{sl}"], ["cvc"])
                          if ch >= 1:
                              conv_pe(ch - 1)
                          if ch == 11:
                              conv_pe(11)

                      for zi in range(2):
                          for c in range(4):
                              Y.append(lambda zi=zi, c=c: yz(zi, c))
                      for ch in range(12):
                          Y.append(lambda ch=ch: yconv(ch))
                      xi = 0
                      for yi, yf in enumerate(Y):
                          yf()
                          while xi < len(X) and xi < (yi + 1) * len(X) / len(Y):
                              X[xi]()
                              xi += 1
                          if yi == 9 and g + 1 < NGr:
                              xstat_all(g + 1)
                      while xi < len(X):
                          X[xi]()
                          xi += 1
                      def l2sq(ch):
                          if ch < 8:
                              act(sq[ch % 3][:], sact[ch][:], AF.Square, [f"sact{ch}"], [f"sq{ch % 3}"])
                      l2sq(0)
                      l2sq(1)
                      for ch in range(8):
                          l2sq(ch + 2)
                          b2 = nextpb()
                          mm(pbs[b2][:], onesf[:], sq[ch % 3][:], True, True, [f"sq{ch % 3}", "onesf"], PBK(b2))
                          r_ = rn[ch % 2]
                          rk_ = f"rn{ch % 2}"
                          rsqrt(r_[:], pbs[b2][:], 1.0, epsc[:, 0:1], PBK(b2) + ["epsc"], [rk_])
                          oi = nextost()
                          sc_ = (128.0 ** -0.5) if ch < 4 else 1.0
                          stt("dve", ost[oi][:], sact[ch][:], sc_, r_[:], ALU.mult, ALU.mult, [f"sact{ch}", rk_], [f"ost{oi}"])
                          hh = ch % 4
                          dma("sp", f"ost{oi}", (GQ if ch < 4 else GK)[hh, :, tok], ost[oi][:], r=[f"ost{oi}"],
                              w=[("GQ" if ch < 4 else "GK", hh, g)])
                          if ch % 2 == 1:
                              ntrans(ch // 2)
                      b = proj(2816, 8)
                      bet, ea, ga, gb2 = abw
                      act(bet[:], pbs[b][0:8, :], AF.Exp, PBK(b), ["abw0"], scale=-1.0)
                      tsc("dve", bet[:], bet[:], 1.0, None, ALU.add, None, ["abw0"], ["abw0"])
                      rcp(bet[:], bet[:], ["abw0"], ["abw0"])
                      act(ea[:], pbs[b][0:8, :], AF.Exp, PBK(b) + ["dtb"], ["abw1"], bias=dtb[:, l:l + 1])
                      act(ea[:], ea[:], AF.Ln, ["abw1"], ["abw1"], bias=1.0)
                      tsc("pool", ga[:], ea[:], nA[:, l:l + 1], None, ALU.mult, None, ["abw1", "nA"], ["abw2"])
                      src_, sk_, dst_, dk2 = ga, "abw2", gb2, "abw3"
                      for s_ in (1, 2, 4, 8, 16, 32):
                          sv = src_[:].rearrange("p (n c) -> p n c", c=64)
                          dv = dst_[:].rearrange("p (n c) -> p n c", c=64)
                          cp("pool", dv[:, :, 0:s_], sv[:, :, 0:s_], [sk_], [dk2])
                          tt("pool", dv[:, :, s_:64], sv[:, :, s_:64], sv[:, :, 0:64 - s_], ALU.add, [sk_], [dk2])
                          src_, sk_, dst_, dk2 = dst_, dk2, src_, sk_
                      stt("dve", bet[:], bet[:], small[0:8, 3:4], src_[:], ALU.mult, ALU.add, ["abw0", sk_, "small"], ["abw0"])
                      dma("sp", "abw0", COMB[:, tok], bet[:], r=["abw0"], w=[("COMB", g)])
                  P.barrier()
              if stop_after == "A" or stop_after == "W" or (stop_after or "").startswith("g"):
                  break

              with ExitStack() as pb_:
                  KTs = [sbuf(pb_, f"KTs{i}", [128, S], BF16) for i in range(2)]
                  KPs = sbuf(pb_, "KPs", [128, S], BF16)
                  Vs = [sbuf(pb_, f"Vs{i}", [128, NT, 128], BF16) for i in range(2)]
                  NQ = 4
                  NPT = 6
                  LAG = 2
                  qn = [sbuf(pb_, f"qn{i}", [128, 512], BF16) for i in range(NQ)]
                  qp = [sbuf(pb_, f"qp{i}", [128, 512], BF16) for i in range(NQ)]
                  zt = [sbuf(pb_, f"zt{i}", [128, 512], BF16) for i in range(NQ)]
                  PT = [sbuf(pb_, f"PT{i}", [128, 512], BF16) for i in range(NPT)]
                  accD = [sbuf(pb_, f"accD{i}", [128, 512], F32) for i in range(2)]
                  accP = [sbuf(pb_, f"accP{i}", [128, 512], F32) for i in range(2)]
                  rcs = [sbuf(pb_, f"rcs{i}", [128, 512], F32) for i in range(2)]
                  yo = [sbuf(pb_, f"yo{i}", [128, 512], BF16) for i in range(2)]
                  SCL = 192.0 ** -0.5
                  allg = list(range(NG))
                  memset("pool", KPs[64:128, :], 0.0, ["KPs"])
                  for i in range(NQ):
                      memset("pool", qp[i][64:128, :], 0.0, [f"qp{i}"])
                  dma("sp", "KPs", KPs[0:64, :], KPE, r=[("KPE", g) for g in allg], w=["KPs"])

                  def load_head(h):
                      i = h % 2
                      dma("sp", f"KTs{i}", KTs[i][:], KT[h], r=[("KT", h, g) for g in allg], w=[f"KTs{i}"])
                      vsrc = VV[:, h * 128:(h + 1) * 128].rearrange("(t p) f -> p t f", p=128)
                      nsp = max(1, NT // 16)
                      for ii in range(nsp):
                          tsl = slice(ii * (NT // nsp), (ii + 1) * (NT // nsp))
                          dma("sp", f"Vs{i}", Vs[i][:, tsl, :], vsrc[:, tsl, :], r=[("VV", g) for g in allg], w=[f"Vs{i}"])

                  units = []
                  groups = []
                  for h in range(4):
                      for g in range(NG):
                          gi = len(groups)
                          groups.append((h, g))
                          for jp in range(2 * (g + 1)):
                              units.append((h, g, jp, gi))
                  nU = len(units)
                  loaded = set()
                  touched = {}
                  NP2 = 4
                  PT2 = [sbuf(pb_, f"PT2_{i}", [128, 2, 512], BF16) for i in range(NP2)]
                  pairb = [sbuf(pb_, f"pairb{i}", [128, 512], BF16) for i in range(2)]

                  def ensure_group(gi):
                      if gi in loaded or gi >= len(groups):
                          return
                      loaded.add(gi)
                      h, g = groups[gi]
                      tok = slice(g * 512, (g + 1) * 512)
                      qi = gi % NQ
                      dma("sp", f"qn{qi}", qn[qi][:], QT[h, 0:128, tok], r=[("QT", h, g)], w=[f"qn{qi}"])
                      dma("sp", f"qp{qi}", qp[qi][0:64, :], QT[h, 128:192, tok], r=[("QT", h, g)], w=[f"qp{qi}"])
                      dma("sp", f"zt{qi}", zt[qi][:], ZM[h, :, tok], r=[("ZM", g)], w=[f"zt{qi}"])

                  load_head(0)
                  LAGP = 1
                  for s_ in range(nU + LAGP):
                      if s_ < nU:
                          h, g, jp, gi = units[s_]
                          if jp == 0:
                              ensure_group(gi)
                              ensure_group(gi + 1)
                              ensure_group(gi + 2)
                          qi = gi % NQ
                          hb = h % 2
                          b0 = 2 * (s_ % 2)
                          pi = s_ % NP2
                          for e_ in range(2):
                              j = 2 * jp + e_
                              ks = slice(j * 128, (j + 1) * 128)
                              mm(pbs[b0 + e_][:], KTs[hb][:, ks], qn[qi][:], True, False, [f"KTs{hb}", f"qn{qi}"], PBK(b0 + e_))
                              mm(pbs[b0 + e_][:], KPs[:, ks], qp[qi][:], False, True, ["KPs", f"qp{qi}"], PBK(b0 + e_))
                      if s_ - LAGP >= 0:
                          uh, ug, ujp, ugi = units[s_ - LAGP]
                          ob = 4 + ugi % 2
                          ppi = (s_ - LAGP) % NP2
                          lastp = ujp == 2 * (ug + 1) - 1
                          for e_ in range(2):
                              uj = 2 * ujp + e_
                              mm(pbs[ob][:], Vs[uh % 2][:, uj, :], PT2[ppi][:, e_, :], uj == 0, lastp and e_ == 1,
                                 [f"Vs{uh % 2}", f"PT2_{ppi}"], PBK(ob))
                          if lastp:
                              a2i = ugi % 2
                              uqi = ugi % NQ
                              mm(pbs[6][:], onesf[:], accD[a2i][:], True, True, [f"accD{a2i}", "onesf"], PBK(6))
                              rcp(rcs[a2i][:], pbs[6][:], PBK(6), [f"rcs{a2i}"])
                              tt("dve", rcs[a2i][:], rcs[a2i][:], pbs[ob][:], ALU.mult, [f"rcs{a2i}"] + PBK(ob), [f"rcs{a2i}"])
                              tt("dve", yo[a2i][:], rcs[a2i][:], zt[uqi][:], ALU.mult, [f"rcs{a2i}", f"zt{uqi}"], [f"yo{a2i}"])
                              dma("sp", f"yo{a2i}", YT[uh, :, ug * 512:(ug + 1) * 512], yo[a2i][:], r=[f"yo{a2i}"], w=[("YT", uh, ug)])
                          if ug == 0 and ujp == 1 and uh + 1 < 4:
                              load_head(uh + 1)
                      if s_ < nU:
                          pk = f"PT2_{pi}"
                          psrc = pbs[b0][:].rearrange("p (o n) -> p o n", o=1)
                          act(PT2[pi][:, 0, :], pbs[b0][:], AF.Exp, PBK(b0), [pk], scale=SCL)
                          act(PT2[pi][:, 1, :], pbs[b0 + 1][:], AF.Exp, PBK(b0 + 1), [pk], scale=SCL)
                          if 2 * jp >= 4 * g:
                              m0 = 2 * jp - 4 * g
                              tt("pool", PT2[pi][:], PT2[pi][:], attm[:, m0:m0 + 2, :], ALU.mult, [pk, "attm"], [pk])
                          a2i = gi % 2
                          pb2 = pairb[s_ % 2]
                          pbk = f"pairb{s_ % 2}"
                          tt("dve", pb2[:], PT2[pi][:, 0, :], PT2[pi][:, 1, :], ALU.add, [pk], [pbk])
                          if not touched.get(gi, False):
                              cp("dve", accD[a2i][:], pb2[:], [pbk], [f"accD{a2i}"])
                          else:
                              tt("dve", accD[a2i][:], accD[a2i][:], pb2[:], ALU.add, [pbk, f"accD{a2i}"], [f"accD{a2i}"])
                          touched[gi] = True
                  P.barrier()
              if stop_after == "B":
                  break

              with ExitStack() as pc:
                  combs = sbuf(pc, "combs", [128, 512], F32)
                  combT = sbuf(pc, "combT", [128, 4, 8], F32)
                  tok4 = sbuf(pc, "tok4", [128, 4, 20], F32)
                  gcb = [sbuf(pc, f"gcb{h}", [128, 512], F32) for h in range(4)]
                  btb = [sbuf(pc, f"btb{h}", [128, 512], F32) for h in range(4)]
                  gqs = [sbuf(pc, f"gqs{h}", [128, 512], BF16) for h in range(4)]
                  gks = [sbuf(pc, f"gks{h}", [128, 512], BF16) for h in range(4)]
                  gvs = [sbuf(pc, f"gvs{h}", [128, 512], BF16) for h in range(4)]
                  kbT = [sbuf(pc, f"kbT{h}", [128, 512], BF16) for h in range(4)]
                  egb = [[sbuf(pc, f"egb{h}_{p}", [128, 512], F32) for p in range(2)] for h in range(4)]
                  zgs = [[sbuf(pc, f"zgs{h}_{p}", [128, 512], BF16) for p in range(2)] for h in range(4)]
                  qdT = [[sbuf(pc, f"qdT{h}_{p}", [128, 512], BF16) for p in range(2)] for h in range(4)]
                  ygs = [[sbuf(pc, f"ygs{h}_{p}", [128, 512], BF16) for p in range(2)] for h in range(4)]
                  Sf = [sbuf(pc, f"Sf{h}", [128, 128], F32) for h in range(4)]
                  Sb = [sbuf(pc, f"Sb{h}", [128, 128], BF16) for h in range(4)]

                  def mk(name, dt, n=1):
                      return [[sbuf(pc, f"{name}{h}_{i}", [128, 128], dt) for i in range(n)] for h in range(4)]

                  Dm = mk("Dm", F32)
                  DTs = mk("DTs", F32)
                  DTi = mk("DTi", F32)
                  Ab = mk("Ab", F32, 2)
                  Bb = mk("Bb", F32, 2)
                  Pb = mk("Pb", F32, 2)
                  TTb = mk("TTb", BF16)
                  kbg = mk("kbg", BF16)
                  vb = mk("vb", BF16)
                  QKT = mk("QKT", BF16, 2)
                  kd = mk("kd", BF16, 4)
                  usb = mk("usb", F32, 2)
                  wT = mk("wT", BF16, 2)
                  vn = mk("vn", BF16)
                  on = mk("on", BF16)
                  ost2 = sbuf(pc, "ost2", [128, 4, 4], F32)
                  junk2 = sbuf(pc, "junk2", [64, 128], BF16)
                  memset("pool", combs[:], 0.0, ["combs"])
                  for h in range(4):
                      memset("pool", vn[h][0][:], 0.0, [f"vn{h}"])
                      memset("pool", on[h][0][:], 0.0, [f"on{h}"])
                      memset("pool", Sf[h][:], 0.0, [f"Sf{h}"])
                      memset("pool", Sb[h][:], 0.0, [f"Sb{h}"])
                  pctr = [0]
                  sctr = [0]

                  def nq():
                      i = pctr[0] % 16
                      pctr[0] += 1
                      b, q = i % 4, i // 4
                      nq.last = (b, q)
                      return pbs[b][:, q * 128:(q + 1) * 128], [("pb", b)]

                  def nqb():
                      ap, k = nq()
                      b, q = nq.last
                      return pbs[b][:].bitcast(BF16)[:, q * 256:q * 256 + 128], [("pb", b)]

                  def sq_():
                      i = sctr[0] % 12
                      sctr[0] += 1
                      b, q = 4 + i % 3, i // 3
                      return pbs[b][:, q * 128:(q + 1) * 128], [("pb", b)]

                  HH = range(4)

                  def setup(g):
                      gp = g % 2
                      tok = slice(g * 512, (g + 1) * 512)
                      dma("sp", "combs", combs[0:8, :], COMB[:, tok], r=[("COMB", g)], w=["combs"])
                      for h in HH:
                          dma("sp", f"gqs{h}", gqs[h][:], GQ[h, :, tok], r=[("GQ", h, g)], w=[f"gqs{h}"])
                          dma("sp", f"gks{h}", gks[h][:], GK[h, :, tok], r=[("GK", h, g)], w=[f"gks{h}"])
                          dma("sp", f"gvs{h}", gvs[h][:], GV[h, :, tok], r=[("GV", h, g)], w=[f"gvs{h}"])
                          dma("sp", f"zgs{h}_{gp}", zgs[h][gp][:], ZG[h, :, tok], r=[("ZG", g)], w=[f"zgs{h}_{gp}"])
                      for h in HH:
                          b1, b2 = (2 * h) % 4, (2 * h + 1) % 4
                          mm(pbs[b1][:], sel[:, h, :], combs[:], True, True, ["sel", "combs"], PBK(b1))
                          cp("act", gcb[h][:], pbs[b1][:], PBK(b1), [f"gcb{h}"])
                          act(egb[h][gp][:], pbs[b1][:], AF.Exp, PBK(b1), [f"egb{h}_{gp}"])
                          mm(pbs[b2][:], sel[:, 4 + h, :], combs[:], True, True, ["sel", "combs"], PBK(b2))
                          cp("act", btb[h][:], pbs[b2][:], PBK(b2), [f"btb{h}"])
                          tt("dve", kbT[h][:], gks[h][:], btb[h][:], ALU.mult, [f"gks{h}", f"btb{h}"], [f"kbT{h}"])
                          tt("pool", qdT[h][gp][:], gqs[h][:], egb[h][gp][:], ALU.mult, [f"gqs{h}", f"egb{h}_{gp}"], [f"qdT{h}_{gp}"])
                      for t in range(4):
                          ap, k = nq()
                          tr(ap, combs[:, t * 128:(t + 1) * 128], identf[:], ["combs", "identf"], k)
                          cp("dve", combT[:, t, :], ap[:, 0:8], k, [("combT", t)])
                          act(tok4[:, t, 0:4], combT[:, t, 0:4], AF.Exp, [("combT", t)], [("tok4", t)])
                          tt("dve", tok4[:, t, 4:8], combT[:, t, 4:8], tok4[:, t, 0:4], ALU.mult, [("combT", t), ("tok4", t)], [("tok4", t)])
                          for h in HH:
                              for c2 in range(2):
                                  rs = slice(c2 * 64, (c2 + 1) * 64)
                                  lc = t * 128 + c2 * 64 + 63
                                  tt("dve", tok4[rs, t, 8 + h:9 + h], gcb[h][rs, lc:lc + 1], combT[rs, t, h:h + 1], ALU.subtract,
                                     [f"gcb{h}", ("combT", t)], [("tok4", t)])
                          act(tok4[:, t, 8:12], tok4[:, t, 8:12], AF.Exp, [("tok4", t)], [("tok4", t)])
                          tsc("dve", tok4[:, t, 12:16], tok4[:, t, 8:12], pmask[:, 0:1], None, ALU.mult, None, [("tok4", t), "pmask"], [("tok4", t)])
                          tsc("dve", tok4[:, t, 16:20], tok4[:, t, 8:12], pmask[:, 1:2], None, ALU.mult, None, [("tok4", t), "pmask"], [("tok4", t)])

                  def par_steps(g, t):
                      tp = (g * 4 + t) % 2
                      ts_ = slice(t * 128, (t + 1) * 128)
                      steps = []

                      def p0():
                          for h in HH:
                              tsc("dve", Dm[h][0][:], gcb[h][:, ts_], combT[:, t, h:h + 1], 0.0, ALU.subtract, ALU.max,
                                  [f"gcb{h}", ("combT", t)], [f"Dm{h}"])
                              act(Dm[h][0][:], Dm[h][0][:], AF.Exp, [f"Dm{h}"], [f"Dm{h}"], scale=-1.0)
                              tsc("dve", DTs[h][0][:], gcb[h][:, ts_], combT[:, t, h:h + 1], 0.0, ALU.subtract, ALU.min,
                                  [f"gcb{h}", ("combT", t)], [f"DTs{h}"])
                              act(DTs[h][0][:], DTs[h][0][:], AF.Exp, [f"DTs{h}"], [f"DTs{h}"])
                              tt("pool", Dm[h][0][:], Dm[h][0][:], gmask[:, 0, :], ALU.mult, [f"Dm{h}", "gmask"], [f"Dm{h}"])
                              tt("pool", DTi[h][0][:], DTs[h][0][:], gmask[:, 2, :], ALU.mult, [f"DTs{h}", "gmask"], [f"DTi{h}"])
                              tt("pool", DTs[h][0][:], DTs[h][0][:], gmask[:, 1, :], ALU.mult, [f"DTs{h}", "gmask"], [f"DTs{h}"])
                          for h in HH:
                              ap, k = nq()
                              mm(ap, kbT[h][:, ts_], gks[h][:, ts_], True, True, [f"kbT{h}", f"gks{h}"], k)
                              tt("dve", Ab[h][0][:], ap, Dm[h][0][:], ALU.mult, k + [f"Dm{h}"], [f"Ab{h}_0"])
                              ap, k = nq()
                              mm(ap, gks[h][:, ts_], kbT[h][:, ts_], True, True, [f"kbT{h}", f"gks{h}"], k)
                              tt("dve", Bb[h][0][:], ap, DTs[h][0][:], ALU.mult, k + [f"DTs{h}"], [f"Bb{h}_0"])
                              ap, k = nq()
                              mm(ap, gks[h][:, ts_], gqs[h][:, ts_], True, True, [f"gqs{h}", f"gks{h}"], k)
                              tt("dve", QKT[h][tp][:], ap, DTi[h][0][:], ALU.mult, k + [f"DTi{h}"], [f"QKT{h}_{tp}"])
                              tt("pool", Pb[h][0][:], identf[:], Bb[h][0][:], ALU.subtract, ["identf", f"Bb{h}_0"], [f"Pb{h}_0"])
                      steps.append(p0)

                      def mklevel(kk):
                          def lev():
                              ci, co = (kk - 1) % 2, kk % 2
                              for h in HH:
                                  ap, k = nq()
                                  mm(ap, Bb[h][ci][:], Ab[h][ci][:], True, True, [f"Bb{h}_{ci}", f"Ab{h}_{ci}"], k)
                                  cp("act", Ab[h][co][:], ap, k, [f"Ab{h}_{co}"])
                                  if kk < 5:
                                      ap2, k2 = nq()
                                      mm(ap2, Ab[h][ci][:], Bb[h][ci][:], True, True, [f"Bb{h}_{ci}", f"Ab{h}_{ci}"], k2)
                                      cp("dve", Bb[h][co][:], ap2, k2, [f"Bb{h}_{co}"])
                              for h in HH:
                                  ap, k = nq()
                                  mm(ap, Ab[h][co][:], Pb[h][ci][:], True, True, [f"Ab{h}_{co}", f"Pb{h}_{ci}"], k)
                                  tt("dve", Pb[h][co][:], ap, Pb[h][ci][:], ALU.add, k + [f"Pb{h}_{ci}"], [f"Pb{h}_{co}"])
                          return lev
                      for kk in range(1, 6):
                          steps.append(mklevel(kk))

                      def p6():
                          PF = 5 % 2
                          for h in HH:
                              cp("act", TTb[h][0][:], Pb[h][PF][:], [f"Pb{h}_{PF}"], [f"TTb{h}"])
                              ap, k = nqb()
                              tr(ap, gks[h][:, ts_], identb[:], [f"gks{h}", "identb"], k)
                              tsc("dve", kbg[h][0][:], ap, tok4[:, t, 4 + h:5 + h], None, ALU.mult, None, k + [("tok4", t)], [f"kbg{h}"])
                              tsc("dve", kd[h][2 * tp][:], ap, tok4[:, t, 12 + h:13 + h], None, ALU.mult, None, k + [("tok4", t)], [f"kd{h}_{tp}"])
                              tsc("dve", kd[h][2 * tp + 1][:], ap, tok4[:, t, 16 + h:17 + h], None, ALU.mult, None, k + [("tok4", t)], [f"kd{h}_{tp}"])
                              ap, k = nqb()
                              tr(ap, gvs[h][:, ts_], identb[:], [f"gvs{h}", "identb"], k)
                              tsc("dve", vb[h][0][:], ap, combT[:, t, 4 + h:5 + h], None, ALU.mult, None, k + [("combT", t)], [f"vb{h}"])
                          for h in HH:
                              ap, k = nq()
                              mm(ap, TTb[h][0][:], vb[h][0][:], True, True, [f"TTb{h}", f"vb{h}"], k)
                              cp("act", usb[h][tp][:], ap, k, [f"usb{h}_{tp}"])
                              ap, k = nq()
                              mm(ap, kbg[h][0][:], TTb[h][0][:], True, True, [f"TTb{h}", f"kbg{h}"], k)
                              cp("act", wT[h][tp][:], ap, k, [f"wT{h}_{tp}"])
                      steps.append(p6)
                      return steps

                  def seq_steps(g, t):
                      gp = g % 2
                      tp = (g * 4 + t) % 2
                      ts_ = slice(t * 128, (t + 1) * 128)
                      tok = slice(g * 512, (g + 1) * 512)
                      steps = []
                      yps = {h: (pbs[7][:].bitcast(BF16)[:, h * 256:(h + 1) * 256].rearrange("p (c t) -> p c t", c=2), [("pb", 7)]) for h in HH}
                      state = {}
                      for c2 in range(2):
                          rs = slice(c2 * 64, (c2 + 1) * 64)
                          cs_ = slice(t * 128 + c2 * 64, t * 128 + (c2 + 1) * 64)
                          lc = t * 128 + c2 * 64 + 63

                          def sa(c2=c2, rs=rs):
                              wsp = {}
                              for h in HH:
                                  ap, k = sq_()
                                  mm(ap[0:64, :], wT[h][tp][:, rs], Sb[h][:], True, True, [f"wT{h}_{tp}", f"Sb{h}"], k)
                                  wsp[h] = (ap, k)
                              for h in HH:
                                  ap, k = wsp[h]
                                  tt("dve", vn[h][0][rs, :], usb[h][tp][rs, :], ap[0:64, :], ALU.subtract, k + [f"usb{h}_{tp}"], [f"vn{h}"])

                          def sb_(c2=c2, rs=rs, cs_=cs_, lc=lc):
                              osp = {}
                              for h in HH:
                                  ap, k = sq_()
                                  mm(ap[0:64, :], qdT[h][gp][:, cs_], Sb[h][:], True, False, [f"qdT{h}_{gp}", f"Sb{h}"], k)
                                  mm(ap[0:64, :], QKT[h][tp][:, rs], vn[h][0][:, :], False, True, [f"QKT{h}_{tp}", f"vn{h}"], k)
                                  osp[h] = (ap, k)
                                  ap2, k2 = sq_()
                                  mm(ap2, kd[h][2 * tp + c2][:, :], vn[h][0][:, :], True, True, [f"kd{h}_{tp}", f"vn{h}"], k2)
                                  stt("dve", Sf[h][:], Sf[h][:], egb[h][gp][:, lc:lc + 1], ap2, ALU.mult, ALU.add,
                                      k2 + [f"Sf{h}", f"egb{h}_{gp}"], [f"Sf{h}"])
                                  cp("pool", Sb[h][:], Sf[h][:], [f"Sf{h}"], [f"Sb{h}"])
                              state[c2] = osp

                          def sc(c2=c2):
                              osp = state[c2]
                              for h in HH:
                                  ap, k = osp[h]
                                  act(junk2[:], ap[0:64, :], AF.Square, k, ["junk2", "ost2"], accum=ost2[0:64, h, 0:1])
                              rsqrt(ost2[0:64, :, 1:2], ost2[0:64, :, 0:1], 1.0 / 128.0, epsc[0:64, 0:1], ["ost2", "epsc"], ["ost2"])
                              for h in HH:
                                  ap, k = osp[h]
                                  tsc("dve", on[h][0][0:64, :], ap[0:64, :], ost2[0:64, h, 1:2], None, ALU.mult, None, k + ["ost2"], [f"on{h}"])
                                  yap, yk = yps[h]
                                  tr(yap[:, c2, :], on[h][0][:, :], identb[:], [f"on{h}", "identb"], yk)
                          steps += [sa, sb_, sc]

                      def sf():
                          for h in HH:
                              yap, yk = yps[h]
                              stt("dve", ygs[h][gp][:, ts_].rearrange("p (c t) -> p c t", c=2), yap[:, :, 0:64], cols[:, l, 29:30],
                                  zgs[h][gp][:, ts_].rearrange("p (c t) -> p c t", c=2), ALU.mult, ALU.mult,
                                  yk + [("cols", l), f"zgs{h}_{gp}"], [f"ygs{h}_{gp}"])
                          if t == 3:
                              for h in HH:
                                  dma("sp", f"ygs{h}_{gp}", YT[4 + h, :, tok], ygs[h][gp][:], r=[f"ygs{h}_{gp}"], w=[("YT", 4 + h, g)])
                      steps.append(sf)
                      return steps

                  prev = None
                  for g in range(NG):
                      setup(g)
                      for t in range(4):
                          ps_l = par_steps(g, t)
                          ss_l = seq_steps(*prev) if prev is not None else []
                          for i in range(max(len(ps_l), len(ss_l))):
                              if i < len(ps_l):
                                  ps_l[i]()
                              if i < len(ss_l):
                                  ss_l[i]()
                          prev = (g, t)
                  for f_ in seq_steps(*prev):
                      f_()
                  P.barrier()
              if stop_after == "C":
                  break

              with ExitStack() as pd:
                  WOb = sbuf(pd, "WOb", [128, 8, D], BF16)
                  stg = [sbuf(pd, f"stgd{i}", [128, D], F32) for i in range(2)]
                  for c in range(8):
                      sk = f"stgd{c % 2}"
                      dma("sp", sk, stg[c % 2][:], w_out[l, c * 128:(c + 1) * 128, :], w=[sk])
                      cp("dve" if c % 2 == 0 else "pool", WOb[:, c, :], stg[c % 2][:], [sk], ["WOb"])
                  NS = 3
                  yt = [sbuf(pd, f"yt{i}", [128, 8, 128], BF16) for i in range(NS)]
                  xd = [sbuf(pd, f"xd{i}", [128, D], F32) for i in range(NS)]
                  xo = [sbuf(pd, f"xo{i}", [128, D], F32) for i in range(NS)]
                  junk3 = sbuf(pd, "junk3", [128, D], BF16)
                  st3 = sbuf(pd, "st3", [128, NS, 4], F32)

                  def d_load(ti):
                      if ti >= NT:
                          return
                      g = ti // 4
                      i2 = ti % NS
                      dma("sp", f"yt{i2}", yt[i2][:], YT[:, :, ti * 128:(ti + 1) * 128].rearrange("c p t -> p c t"),
                          r=[("YT", c, g) for c in range(8)], w=[f"yt{i2}"])
                      dma("sp", f"xd{i2}", xd[i2][:], xsrc[ti * 128:(ti + 1) * 128, :], r=[(xkey, ti)], w=[f"xd{i2}"])

                  def d_mm(ti):
                      i2 = ti % NS
                      b0 = 2 * (ti % 4)
                      for half in range(2):
                          for c in range(8):
                              mm(pbs[b0 + half][:], yt[i2][:, c, :], WOb[:, c, half * 512:(half + 1) * 512], c == 0, c == 7,
                                 [f"yt{i2}", "WOb"], PBK(b0 + half))

                  def d_epi(ti):
                      i2 = ti % NS
                      b0 = 2 * (ti % 4)
                      sk3 = ("st3", i2)
                      for half in range(2):
                          act(junk3[:, half * 512:(half + 1) * 512], pbs[b0 + half][:], AF.Square, PBK(b0 + half), ["junk3", sk3],
                              accum=st3[:, i2, half:half + 1])
                      tt("dve", st3[:, i2, 2:3], st3[:, i2, 0:1], st3[:, i2, 1:2], ALU.add, [sk3], [sk3])
                      rsqrt(st3[:, i2, 3:4], st3[:, i2, 2:3], 1.0 / D, epsc[:, 0:1], [sk3, "epsc"], [sk3])
                      for half in range(2):
                          hs = slice(half * 512, (half + 1) * 512)
                          stt("dve", xo[i2][:, hs], pbs[b0 + half][:], st3[:, i2, 3:4], gpb[:, hs], ALU.mult, ALU.mult,
                              PBK(b0 + half) + [sk3, "gpb"], [f"xo{i2}"])
                      tt("dve", xo[i2][:], xo[i2][:], xd[i2][:], ALU.add, [f"xo{i2}", f"xd{i2}"], [f"xo{i2}"])
                      dma("sp", f"xo{i2}", xdst[ti * 128:(ti + 1) * 128, :], xo[i2][:], r=[f"xo{i2}"], w=[(xdkey, ti)])

                  d_load(0)
                  d_load(1)
                  for ti in range(NT + 1):
                      if ti < NT:
                          d_mm(ti)
                      if ti >= 1:
                          d_epi(ti - 1)
                      d_load(ti + 2)
                  P.barrier()
        except _Stop:
            pass
        P.final_wait("sp", [("OUT", ti) for ti in range(NT)])
        build.stats = (dict(P.n_inst), dict(P.n_wait), P.nsem)
    return nc


def make_consts():
    ident = np.eye(128, dtype=np.float32)
    attm = np.zeros((128, 4, 512), np.float32)
    k = np.arange(128)[:, None]
    q = np.arange(512)[None, :]
    for j in range(4):
        attm[:, j, :] = (q >= 128 * j + k).astype(np.float32)
    i = np.arange(128)[:, None]
    jj = np.arange(128)[None, :]
    same = (i // 64) == (jj // 64)
    gm = np.zeros((128, 3, 128), np.float32)
    gm[:, 0, :] = (same & (i > jj))
    gm[:, 1, :] = (same & (i < jj))
    gm[:, 2, :] = (same & (i <= jj))
    sel = np.zeros((128, 8, 128), np.float32)
    for r in range(8):
        sel[r, r, :] = 1.0
    small = np.zeros((64, 4), np.float32)
    half = 32
    invf = np.power(np.float32(10000.0), -np.arange(half, dtype=np.float32) * np.float32(2.0) / np.float32(64)).astype(np.float32)
    small[:, 0] = np.concatenate([invf, invf])
    small[:32, 1] = -1.0
    small[32:, 1] = 1.0
    small[0:4, 2] = -1.0
    small[4:8, 3] = 1.0
    return dict(c_ident=ident, c_attm=attm.reshape(128, 2048), c_gmask=gm.reshape(128, 384), c_sel=sel.reshape(128, 1024), c_small=small,
                c_pm=np.stack([(np.arange(128) < 64), (np.arange(128) >= 64)], axis=1).astype(np.float32))


def make_in_map(inputs, b, S, NL):
    f = lambda a: np.ascontiguousarray(np.asarray(a))
    m = {}
    m["x"] = f(inputs["x"][b, :S])
    m["ccol"] = f(np.asarray(inputs["c"][b]).reshape(8, 128).T)
    m["posrep"] = f(np.broadcast_to(np.asarray(inputs["positions"][b, :S])[None, :], (64, S))).astype(np.int32)
    m["w_mod"] = f(inputs["w_mod"][:NL])
    m["b_mod"] = f(inputs["b_mod"][:NL])
    m["prew"] = f(inputs["pre_norm_w"][:NL])
    m["postw"] = f(inputs["post_norm_w"][:NL])
    m["w_in"] = f(inputs["w_in"][:NL])
    m["qnw"] = f(inputs["mla_q_norm_w"][:NL])
    m["q_up"] = f(inputs["mla_q_up"][:NL])
    m["kvnw"] = f(inputs["mla_kv_norm_w"][:NL])
    m["kv_up"] = f(inputs["mla_kv_up"][:NL])
    m["convw"] = f(np.asarray(inputs["gdn_conv_w"][:NL]).reshape(NL, 4 * 1536))
    a8 = np.zeros((8, NL), np.float32)
    a8[0:4, :] = np.asarray(inputs["gdn_a_log"][:NL]).T
    d8 = np.zeros((8, NL), np.float32)
    d8[0:4, :] = np.asarray(inputs["gdn_dt_bias"][:NL]).T
    m["alog8"] = a8
    m["dtb8"] = d8
    m["onw"] = f(inputs["gdn_o_norm_w"][:NL])
    m["w_out"] = f(inputs["w_out"][:NL])
    m.update(make_consts())
    return m


_NC_CACHE = {}


def kernel(**inputs):
    B, S, _ = inputs["x"].shape
    NL = inputs["w_in"].shape[0]
    key = (S, NL)
    if key not in _NC_CACHE:
        _NC_CACHE[key] = build(S, NL)
    nc = _NC_CACHE[key]
    in_maps = [make_in_map(inputs, c % B, S, NL) for c in range(8)]
    res = run_bass_kernel_spmd(nc, in_maps, core_ids=list(range(8)))
    outs = [np.asarray(res.results[b]["out"]) for b in range(B)]
    return np.stack(outs, axis=0).astype(np.float32)
```

```python
import math
import os
CUT = int(os.environ.get('KCUT', '99'))
import numpy as np
from contextlib import ExitStack
import concourse.bass as bass
import concourse.mybir as mybir
from concourse.bass_utils import run_bass_kernel_spmd

F32 = mybir.dt.float32
BF16 = mybir.dt.bfloat16
I32 = mybir.dt.int32
ALU = mybir.AluOpType
AF = mybir.ActivationFunctionType

D = 1024
NIN = 3272
NWB = 3336
EPS = 1e-6
ENGS = ("pe", "act", "dve", "pool", "sp")
SEM_ROT = 30000
HMAP = {"pe": "tensor", "act": "scalar", "dve": "vector", "pool": "gpsimd", "sp": "sync"}


class _Stop(Exception):
    pass


class Prog:
    def __init__(self, nc, stack):
        self.nc = nc
        self.stack = stack
        self.cur_sem = {}
        self.cur_cnt = {e: 0 for e in ENGS}
        self.nsem = 0
        self.all_eng_sems = {e: [] for e in ENGS}
        for e in ENGS:
            self._new_eng_sem(e)
        self.known = {e: {} for e in ENGS}
        self.free_dma = []
        self.retired = []
        self.last_write = {}
        self.readers = {}
        self.dma_sems = {}
        self.n_inst = {e: 0 for e in ENGS}
        self.n_wait = {e: 0 for e in ENGS}

    def _sem(self, name):
        s = self.stack.enter_context(self.nc.semaphore(name))
        self.nsem += 1
        return s

    def _new_eng_sem(self, e):
        if e in self.cur_sem:
            self.retired.append([self.cur_sem[e], self.cur_cnt[e]])
        self.cur_sem[e] = self._sem(f"c_{e}_{self.nsem}")
        self.cur_cnt[e] = 0

    def _deps(self, eng, reads, writes):
        deps = {}

        def add(ev):
            if ev is None:
                return
            s, v = ev
            k = id(s)
            if k not in deps or deps[k][1] < v:
                deps[k] = (s, v)

        for r in reads:
            add(self.last_write.get(r))
        for w in writes:
            add(self.last_write.get(w))
            rd = self.readers.get(w)
            if rd:
                for ev in rd.values():
                    add(ev)
        out = []
        kn = self.known[eng]
        own = id(self.cur_sem[eng])
        for k, (s, v) in deps.items():
            if k == own and eng == "pe":
                continue
            if kn.get(k, 0) >= v:
                continue
            kn[k] = v
            out.append((s, v))
        return out

    def _record(self, ev, reads, writes):
        for r in reads:
            self.readers.setdefault(r, {})[id(ev[0])] = ev
        for w in writes:
            self.last_write[w] = ev
            self.readers[w] = {}

    def _emit_now(self, eng, waits, fn, sem, inc):
        e = getattr(self.nc, HMAP[eng])
        for (s, v) in waits:
            e.wait_ge(s, v)
        self.n_wait[eng] += len(waits)
        if fn is not None:
            fn(e).then_inc(sem, inc)
            self.n_inst[eng] += 1

    def op(self, eng, fn, reads=(), writes=()):
        waits = self._deps(eng, reads, writes)
        if self.cur_cnt[eng] >= SEM_ROT:
            self._new_eng_sem(eng)
        sem = self.cur_sem[eng]
        self.cur_cnt[eng] += 1
        val = self.cur_cnt[eng]
        self._emit_now(eng, waits, fn, sem, 1)
        self._record((sem, val), reads, writes)

    def dma(self, q, semname, out, in_, reads=(), writes=(), **kw):
        if semname not in self.dma_sems:
            if self.free_dma:
                self.dma_sems[semname] = self.free_dma.pop()
            else:
                self.dma_sems[semname] = [self._sem("d_" + semname), 0]
        ent = self.dma_sems[semname]
        waits = self._deps(q, reads, writes)
        ent[1] += 16
        sem, val = ent[0], ent[1]
        self._emit_now(q, waits, lambda e: e.dma_start(out=out, in_=in_, **kw), sem, 16)
        self._record((sem, val), reads, writes)

    def barrier(self):
        evs = [(self.cur_sem[e], self.cur_cnt[e]) for e in ENGS if self.cur_cnt[e] > 0]
        evs += [(s, v) for (s, v) in self.dma_sems.values() if v > 0]
        evs += [(s, v) for (s, v) in self.retired if v > 0]
        for eng in ENGS:
            kn = self.known[eng]
            own = id(self.cur_sem[eng])
            waits = []
            for (s, v) in evs:
                if id(s) == own or kn.get(id(s), 0) >= v:
                    continue
                kn[id(s)] = v
                waits.append((s, v))
            self._emit_now(eng, waits, None, None, 0)
        self.free_dma.extend(self.dma_sems.values())
        self.dma_sems = {}
        self.retired = []

    def final_wait(self, eng, keys):
        waits = self._deps(eng, keys, ())
        self._emit_now(eng, waits, None, None, 0)


def build(S, NL, dbg=False, stop_after=None):
    nc = bass.Bass("TRN2", target_bir_lowering=False)
    NG = S // 512
    NT = S // 128

    def din(name, shape, dt=F32):
        return nc.dram_tensor(name, list(shape), dt, kind="ExternalInput").ap()

    def dscr(name, shape, dt):
        return nc.dram_tensor(name, list(shape), dt, kind="ExternalOutput" if dbg else "Internal").ap()

    x_in = din("x", [S, D])
    ccol = din("ccol", [128, 8])
    posrep = din("posrep", [64, S], I32)
    w_mod = din("w_mod", [NL, D, 3 * D])
    b_mod = din("b_mod", [NL, 3 * D])
    prew = din("prew", [NL, D])
    postw = din("postw", [NL, D])
    w_in = din("w_in", [NL, D, NIN])
    qnw = din("qnw", [NL, 384])
    q_up = din("q_up", [NL, 384, 768])
    kvnw = din("kvnw", [NL, 256])
    kv_up = din("kv_up", [NL, 256, 1024])
    convw = din("convw", [NL, 4 * 1536])
    alog8 = din("alog8", [8, NL])
    dtb8 = din("dtb8", [8, NL])
    onw = din("onw", [NL, 128])
    w_out = din("w_out", [NL, D, D])
    c_ident = din("c_ident", [128, 128])
    c_attm = din("c_attm", [128, 4 * 512])
    c_gmask = din("c_gmask", [128, 3 * 128])
    c_sel = din("c_sel", [128, 8 * 128])
    c_pm = din("c_pm", [128, 2])
    c_small = din("c_small", [64, 4])
    out = nc.dram_tensor("out", [S, D], F32, kind="ExternalOutput").ap()

    XR = dscr("XR", [S, D], F32)
    QT = dscr("QT", [4, 192, S], BF16)
    KT = dscr("KT", [4, 128, S], BF16)
    KPE = dscr("KPE", [64, S], BF16)
    VV = dscr("VV", [S, 512], BF16)
    ZM = dscr("ZM", [4, 128, S], BF16)
    ZG = dscr("ZG", [4, 128, S], BF16)
    GQ = dscr("GQ", [4, 128, S], BF16)
    GK = dscr("GK", [4, 128, S], BF16)
    GV = dscr("GV", [4, 128, S], BF16)
    COMB = dscr("COMB", [8, S], F32)
    YT = dscr("YT", [8, 128, S], BF16)
    COS = dscr("COS", [64, S], F32)
    SIN = dscr("SIN", [64, S], F32)
    GP = dscr("GP", [NL, D], F32)

    with ExitStack() as st:
        P = Prog(nc, st)

        uid = [0]

        def sbuf(stack, name, shape, dt):
            uid[0] += 1
            return stack.enter_context(nc.sbuf_tensor(f"{name}_u{uid[0]}", list(shape), dt))

        def mm(o, lhsT, rhs, start, stop, r, w):
            P.op("pe", lambda e: e.matmul(o, lhsT=lhsT, rhs=rhs, start=start, stop=stop), r, w)

        def tr(o, i, ident, r, w):
            P.op("pe", lambda e: e.transpose(o, i, ident), r, w)

        def act(o, i, func, r, w, bias=None, scale=None, accum=None):
            kw = {}
            if bias is not None:
                kw["bias"] = bias
            if scale is not None:
                kw["scale"] = scale
            if accum is not None:
                kw["accum_out"] = accum
            P.op("act", lambda e: e.activation(out=o, in_=i, func=func, **kw), r, w)

        def cp(eng, o, i, r, w):
            if eng == "act":
                P.op("act", lambda e: e.activation(out=o, in_=i, func=AF.Copy), r, w)
            else:
                P.op(eng, lambda e: e.tensor_copy(out=o, in_=i), r, w)

        def tsc(eng, o, i, s1, s2, op0, op1, r, w):
            if op1 is None:
                P.op(eng, lambda e: e.tensor_scalar(out=o, in0=i, scalar1=s1, scalar2=None, op0=op0), r, w)
            else:
                P.op(eng, lambda e: e.tensor_scalar(out=o, in0=i, scalar1=s1, scalar2=s2, op0=op0, op1=op1), r, w)

        def tt(eng, o, a, b, op, r, w):
            P.op(eng, lambda e: e.tensor_tensor(out=o, in0=a, in1=b, op=op), r, w)

        def stt(eng, o, a, sc, b, op0, op1, r, w):
            eng = "dve"
            P.op(eng, lambda e: e.scalar_tensor_tensor(out=o, in0=a, scalar=sc, in1=b, op0=op0, op1=op1), r, w)

        def rcp(o, i, r, w):
            P.op("dve", lambda e: e.reciprocal(out=o, in_=i), r, w)

        def memset(eng, o, val, w):
            P.op(eng, lambda e: e.memset(o, val), (), w)

        def dma(q, sem, o, i, r=(), w=(), **kw):
            P.dma(q, sem, o, i, reads=r, writes=w, **kw)

        def rsqrt(o, i, scale, bias_ap, r, w):
            act(o, i, AF.Ln, r, w, bias=bias_ap, scale=scale)
            act(o, o, AF.Exp, w, w, scale=-0.5)

        pbs = [st.enter_context(nc.psum_tensor(f"pb{i}", [128, 512], F32)) for i in range(8)]

        def PBK(i):
            return [("pb", i)]

        identf = sbuf(st, "identf", [128, 128], F32)
        identb = sbuf(st, "identb", [128, 128], BF16)
        onesf = sbuf(st, "onesf", [128, 128], F32)
        attm = sbuf(st, "attm", [128, 4, 512], BF16)
        gmask = sbuf(st, "gmask", [128, 3, 128], F32)
        sel = sbuf(st, "sel", [128, 8, 128], F32)
        pmask = sbuf(st, "pmask", [128, 2], F32)
        small = sbuf(st, "small", [64, 4], F32)
        epsc = sbuf(st, "epsc", [128, 1], F32)
        cols = sbuf(st, "cols", [128, NL, 80], F32)
        nA = sbuf(st, "nA", [8, NL], F32)
        dtb = sbuf(st, "dtb", [8, NL], F32)
        gpb = sbuf(st, "gpb", [128, D], F32)

        dma("sp", "c0", identf[:], c_ident, w=["identf"])
        dma("sp", "c1", gmask[:].rearrange("p m j -> p (m j)"), c_gmask, w=["gmask"])
        dma("sp", "c2", sel[:].rearrange("p m j -> p (m j)"), c_sel, w=["sel"])
        dma("sp", "c3", small[:], c_small, w=["small"])
        dma("sp", "c3b", pmask[:], c_pm, w=["pmask"])
        dma("sp", "c4", nA[:], alog8, w=["nA"])
        dma("sp", "c5", dtb[:], dtb8, w=["dtb"])
        cp("dve", identb[:], identf[:], ["identf"], ["identb"])
        memset("pool", onesf[:], 1.0, ["onesf"])
        memset("pool", epsc[:], EPS, ["epsc"])
        act(nA[:], nA[:], AF.Exp, ["nA"], ["nA"])
        tsc("dve", nA[:], nA[:], small[0:8, 2:3], None, ALU.mult, None, ["nA", "small"], ["nA"])

        with ExitStack() as ps_:
            amst = sbuf(ps_, "amst", [128, 4 * 512], F32)
            dma("sp", "c6", amst[:], c_attm, w=["amst"])
            cp("dve", attm[:].rearrange("p m j -> p (m j)"), amst[:], ["amst"], ["attm"])

            cact = sbuf(ps_, "cact", [128, 8], F32)
            dma("sp", "c7", cact[:], ccol, w=["cact"])
            act(cact[:], cact[:], AF.Silu, ["cact"], ["cact"])
            NROW = 3072 + 1792 + 6144
            rowbuf = sbuf(ps_, "rowbuf", [1, NROW], F32)
            bmrow = sbuf(ps_, "bmrow", [1, 3072], F32)
            pwrow = sbuf(ps_, "pwrow", [1, D], F32)
            wmst = [sbuf(ps_, f"wmst{i}", [128, 8, 512], F32) for i in range(2)]
            for l in range(NL):
                dma("sp", "r0", bmrow[:], b_mod[l:l + 1, :], w=["bmrow"])
                dma("sp", "r1", rowbuf[0:1, 3072:4096], prew[l:l + 1, :], w=["rowbuf_s"])
                dma("sp", "r1", rowbuf[0:1, 4096:4480], qnw[l:l + 1, :], w=["rowbuf_s"])
                dma("sp", "r1", rowbuf[0:1, 4480:4736], kvnw[l:l + 1, :], w=["rowbuf_s"])
                dma("sp", "r1", rowbuf[0:1, 4736:4864], onw[l:l + 1, :], w=["rowbuf_s"])
                dma("sp", "r1", rowbuf[0:1, 4864:NROW], convw[l:l + 1, :], w=["rowbuf_s"])
                dma("sp", "r2", pwrow[:], postw[l:l + 1, :], w=["pwrow"])
                for cg in range(6):
                    ws_ = wmst[cg % 2]
                    wk = f"wmst{cg % 2}"
                    dma("sp", wk, ws_[:], w_mod[l, :, cg * 512:(cg + 1) * 512].rearrange("(c p) n -> p c n", p=128), w=[wk])
                    pb = pbs[cg % 2]
                    for c in range(8):
                        mm(pb[0:1, :], cact[:, c:c + 1], ws_[:, c, :], c == 0, c == 7, [wk, "cact"], PBK(cg % 2))
                    tt("dve", rowbuf[0:1, cg * 512:(cg + 1) * 512], pb[0:1, :], bmrow[0:1, cg * 512:(cg + 1) * 512], ALU.add,
                       PBK(cg % 2) + ["bmrow"], ["rowbuf_m"])
                nchunk = 16 + 62
                for j in range(nchunk):
                    off = j * 128 if j < 16 else 3072 + (j - 16) * 128
                    mm(pbs[2][:, j:j + 1], rowbuf[0:1, off:off + 128], onesf[0:1, 0:1], True, True,
                       ["rowbuf_m", "rowbuf_s", "onesf"], PBK(2))
                cp("dve", cols[:, l, 0:nchunk], pbs[2][:, 0:nchunk], PBK(2), [("cols", l)])
                stt("dve", cols[:, l, 8:16], cols[:, l, 8:16], 1.0, cols[:, l, 16:24], ALU.add, ALU.mult, [("cols", l)], [("cols", l)])
                tt("dve", pwrow[:], pwrow[:], rowbuf[0:1, 2048:3072], ALU.mult, ["pwrow", "rowbuf_m"], ["pwrow"])
                dma("sp", "r3", GP[l:l + 1, :], pwrow[:], r=["pwrow"], w=[("GP", l)])

            CH = min(2048, S)
            posi = sbuf(ps_, "posi", [64, CH], I32)
            ang = sbuf(ps_, "ang", [64, CH], F32)
            a2 = sbuf(ps_, "a2", [64, CH], F32)
            kf = sbuf(ps_, "kf", [64, CH], F32)
            ki = sbuf(ps_, "ki", [64, CH], I32)
            fx = sbuf(ps_, "fx", [64, CH], F32)
            tab = [sbuf(ps_, f"tab{i}", [64, CH], F32) for i in range(2)]
            C1 = 6.28125
            C2 = 2.0 * math.pi - C1
            for ch in range(S // CH):
                cs = slice(ch * CH, (ch + 1) * CH)
                dma("sp", "rp0", posi[:], posrep[:, cs], w=["posi"])
                cp("dve", ang[:], posi[:], ["posi"], ["ang"])
                tsc("dve", ang[:], ang[:], small[:, 0:1], None, ALU.mult, None, ["ang", "small"], ["ang"])
                for ti, offv in ((0, math.pi / 2), (1, 0.0)):
                    tsc("dve", a2[:], ang[:], offv, None, ALU.add, None, ["ang"], ["a2"])
                    tsc("dve", ki[:], a2[:], 1.0 / (2 * math.pi), None, ALU.mult, None, ["a2"], ["ki"])
                    cp("dve", kf[:], ki[:], ["ki"], ["kf"])
                    stt("dve", a2[:], kf[:], -C1, a2[:], ALU.mult, ALU.add, ["kf", "a2"], ["a2"])
                    stt("dve", a2[:], kf[:], -C2, a2[:], ALU.mult, ALU.add, ["kf", "a2"], ["a2"])
                    tsc("dve", fx[:], a2[:], math.pi, 2 * math.pi, ALU.is_gt, ALU.mult, ["a2"], ["fx"])
                    tt("dve", a2[:], a2[:], fx[:], ALU.subtract, ["a2", "fx"], ["a2"])
                    tsc("dve", fx[:], a2[:], -math.pi, 2 * math.pi, ALU.is_lt, ALU.mult, ["a2"], ["fx"])
                    tt("dve", a2[:], a2[:], fx[:], ALU.add, ["a2", "fx"], ["a2"])
                    tsc("dve", a2[:], a2[:], math.pi, -math.pi, ALU.min, ALU.max, ["a2"], ["a2"])
                    tk = f"tab{ti}"
                    act(tab[ti][:], a2[:], AF.Sin, ["a2"], [tk])
                    if ti == 1:
                        tsc("dve", tab[ti][:], tab[ti][:], small[:, 1:2], None, ALU.mult, None, [tk, "small"], [tk])
                    dma("sp", "rp" + tk, (COS if ti == 0 else SIN)[:, cs], tab[ti][:], r=[tk], w=[("ROPE", ti, ch)])
            P.barrier()
        ROPE_KEYS = [("ROPE", ti, ch) for ti in range(2) for ch in range(S // min(2048, S))]

        def chk(tag):
            if stop_after == tag:
                P.barrier()
                raise _Stop()

        try:
          for l in range(NL if stop_after != "P" else 0):
              xsrc = x_in if l == 0 else XR
              xdst = out if l == NL - 1 else XR
              xkey = "XIN" if l == 0 else "XR"
              xdkey = "OUT" if l == NL - 1 else "XR"
              dma("sp", "gpb", gpb[:], GP[l:l + 1, :].partition_broadcast(128), r=[("GP", l)], w=["gpb"])
              chk("G")

              with ExitStack() as pa:
                  Wb = sbuf(pa, "Wb", [128, 8, NWB], BF16)
                  Qb = sbuf(pa, "Qb", [128, 3, 4, 256], BF16)
                  KVb = sbuf(pa, "KVb", [128, 2, 1024], BF16)
                  HS = NIN // 2
                  pa_w = ExitStack()
                  stg = [sbuf(pa_w, f"stg{i}", [128, HS], F32) for i in range(2)]
                  si = 0
                  ceng = ["dve", "pool"]
                  for c in range(8):
                      for hf in range(2):
                          s_ = stg[si % 2]
                          sk = f"stg{si % 2}"
                          dma("sp", sk, s_[:], w_in[l, c * 128:(c + 1) * 128, hf * HS:(hf + 1) * HS], w=[sk])
                          lo, hi = hf * HS, (hf + 1) * HS
                          for (a, b, dst) in ((0, 704, 0), (672, 704, 704), (640, 672, 736), (704, NIN, 768)):
                              a2_, b2_ = max(a, lo), min(b, hi)
                              if a2_ >= b2_:
                                  continue
                              d0 = dst + (a2_ - a)
                              cp(ceng[si % 2], Wb[:, c, d0:d0 + (b2_ - a2_)], s_[:, a2_ - lo:b2_ - lo], [sk], [("Wb", c)])
                          si += 1
                  for c in range(3):
                      s_ = stg[si % 2]
                      sk = f"stg{si % 2}"
                      dma("sp", sk, s_[:, 0:768], q_up[l, c * 128:(c + 1) * 128, :], w=[sk])
                      sv = s_[:, 0:768].rearrange("p (h f) -> p h f", h=4)
                      qs = cols[:, l, 24 + c:25 + c]
                      for (a, b, dst) in ((0, 128, 0), (128, 192, 128), (160, 192, 192), (128, 160, 224)):
                          tsc("dve", Qb[:, c, :, dst:dst + (b - a)], sv[:, :, a:b], qs, None, ALU.mult, None, [sk, ("cols", l)], ["Qb"])
                      si += 1
                  for c in range(2):
                      s_ = stg[si % 2]
                      sk = f"stg{si % 2}"
                      dma("sp", sk, s_[:, 0:1024], kv_up[l, c * 128:(c + 1) * 128, :], w=[sk])
                      tsc("dve", KVb[:, c, :], s_[:, 0:1024], cols[:, l, 27 + c:28 + c], None, ALU.mult, None, [sk, ("cols", l)], ["KVb"])
                      si += 1
                  WBK = [("Wb", c) for c in range(8)]
                  P.barrier()
                  pa_w.close()
                  NGr = 0 if stop_after == "W" else (int(stop_after[1:]) if (stop_after or "").startswith("g") else NG)

                  xt = [sbuf(pa, f"xt{i}", [128, D], F32) for i in range(2)]
                  xs = [sbuf(pa, f"xs{i}", [128, D], F32) for i in range(4)]
                  junk = sbuf(pa, "junk", [128, D], BF16)
                  st1 = sbuf(pa, "st1", [128, 8], F32)
                  hT = [sbuf(pa, f"hT{i}", [128, 8, 512], BF16) for i in range(2)]
                  qlT = sbuf(pa, "qlT", [128, 3, 512], BF16)
                  kvT = sbuf(pa, "kvT", [128, 2, 512], BF16)
                  sq = [sbuf(pa, f"sq{i}", [128, 512], F32) for i in range(3)]
                  rq = sbuf(pa, "rq", [128, 512], F32)
                  rkv = sbuf(pa, "rkv", [128, 512], F32)
                  rkvt = sbuf(pa, "rkvt", [128, 4], F32)
                  cst = sbuf(pa, "cst", [64, 512], F32)
                  sst = sbuf(pa, "sst", [64, 512], F32)
                  crr = sbuf(pa, "crr", [64, 512], F32)
                  srr = sbuf(pa, "srr", [64, 512], F32)
                  t1 = [sbuf(pa, f"t1_{i}", [64, 512], F32) for i in range(1)] * 2
                  t2 = [sbuf(pa, f"t2_{i}", [64, 512], F32) for i in range(1)] * 2
                  ost = [sbuf(pa, f"ost{i}", [128, 512], BF16) for i in range(6)]
                  vst = [sbuf(pa, f"vst{i}", [128, 512], BF16) for i in range(2)]
                  zst = [sbuf(pa, f"zst{i}", [128, 4, 512], BF16) for i in range(2)]
                  convd = sbuf(pa, "convd", [128, 48, 128], BF16)
                  cvb = [sbuf(pa, f"cvb{i}", [128, 515], BF16) for i in range(3)]
                  cvc = sbuf(pa, "cvc", [128, 12, 3], BF16)
                  sact = [sbuf(pa, f"sact{i}", [128, 512], F32) for i in range(8)]
                  rn = [sbuf(pa, f"rn{i}", [128, 512], F32) for i in range(2)]
                  abw = [sbuf(pa, f"abw{i}", [8, 512], F32) for i in range(4)]
                  memset("pool", cvc[:], 0.0, ["cvc"])
                  for jj in range(4):
                      for ch in range(12):
                          tsc("dve", convd[:, jj * 12 + ch, :], identb[:], cols[:, l, 30 + jj * 12 + ch:31 + jj * 12 + ch], None,
                              ALU.mult, None, ["identb", ("cols", l)], ["convd"])
                  octr = [0]
                  pbr = [2]

                  def nextpb():
                      b = pbr[0]
                      pbr[0] = 2 + (pbr[0] - 2 + 1) % 6
                      return b

                  def nextost():
                      i = octr[0] % 6
                      octr[0] += 1
                      return i

                  def xstat_all(g):
                      for t in range(4):
                          ti = g * 4 + t
                          k2 = f"xt{t % 2}"
                          dma("sp", k2, xt[t % 2][:], xsrc[ti * 128:(ti + 1) * 128, :], r=[(xkey, ti)], w=[k2])
                          act(junk[:], xt[t % 2][:], AF.Square, [k2], ["junk", "st1"], accum=st1[:, t:t + 1])
                      rsqrt(st1[:, 4:8], st1[:, 0:4], 1.0 / D, epsc[:, 0:1], ["st1", "epsc"], ["st1"])
                      for t in range(4):
                          ti = g * 4 + t
                          k2 = f"xt{t % 2}"
                          dma("sp", k2, xt[t % 2][:], xsrc[ti * 128:(ti + 1) * 128, :], r=[(xkey, ti)], w=[k2])
                          tsc("dve", xs[t][:], xt[t % 2][:], st1[:, 4 + t:5 + t], None, ALU.mult, None, [k2, "st1"], [f"xs{t}"])

                  def xtrans(g, t):
                      hh_ = hT[g % 2]
                      hhk = f"hT{g % 2}"
                      for c in range(8):
                          tr(pbs[c // 4][:, (c % 4) * 128:(c % 4 + 1) * 128], xs[t][:, c * 128:(c + 1) * 128], identf[:],
                             [f"xs{t}", "identf"], [("pb", c // 4)])
                      for c in range(8):
                          tsc("dve", hh_[:, c, t * 128:(t + 1) * 128], pbs[c // 4][:, (c % 4) * 128:(c % 4 + 1) * 128],
                              cols[:, l, 8 + c:9 + c], cols[:, l, c:c + 1], ALU.mult, ALU.add,
                              [("pb", c // 4), ("cols", l)], [hhk])

                  if NGr > 0:
                      xstat_all(0)
                      for t in range(4):
                          xtrans(0, t)
                  for g in range(NGr):
                      tok = slice(g * 512, (g + 1) * 512)
                      h_ = hT[g % 2]
                      hk = f"hT{g % 2}"
                      dma("sp", "cst", cst[:], COS[:, tok], r=ROPE_KEYS, w=["cst"])
                      dma("sp", "sst", sst[:], SIN[:, tok], r=ROPE_KEYS, w=["sst"])

                      def proj(col0, ncol):
                          b = nextpb()
                          for c in range(8):
                              mm(pbs[b][0:ncol, :], Wb[:, c, col0:col0 + ncol], h_[:, c, :], c == 0, c == 7, [hk, ("Wb", c)], PBK(b))
                          return b

                      def nxt(t):
                          return


                      def ntrans(t):
                          if g + 1 < NGr:
                              xtrans(g + 1, t)

                      for c in range(3):
                          b = proj(c * 128, 128)
                          cp("act", qlT[:, c, :], pbs[b][:], PBK(b), ["qlT"])
                          act(sq[c][:], pbs[b][:], AF.Square, PBK(b), [f"sq{c}"])
                      bsum = nextpb()
                      for c in range(3):
                          mm(pbs[bsum][:], onesf[:], sq[c][:], c == 0, c == 2, [f"sq{c}", "onesf"], PBK(bsum))
                      rsqrt(rq[:], pbs[bsum][:], 1.0 / 384.0, epsc[:, 0:1], PBK(bsum) + ["epsc"], ["rq"])
                      for c in range(2):
                          b = proj(384 + c * 128, 128)
                          cp("act", kvT[:, c, :], pbs[b][:], PBK(b), ["kvT"])
                          act(sq[c][:], pbs[b][:], AF.Square, PBK(b), [f"sq{c}"])
                      bsum = nextpb()
                      for c in range(2):
                          mm(pbs[bsum][:], onesf[:], sq[c][:], c == 0, c == 1, [f"sq{c}", "onesf"], PBK(bsum))
                      bt_ = nextpb()
                      for t in range(4):
                          for c in range(2):
                              mm(pbs[bt_][:, 8 + t:9 + t], sq[c][:, t * 128:(t + 1) * 128], onesf[:, 0:1],
                                 c == 0, c == 1, [f"sq{c}", "onesf"], PBK(bt_))
                      rsqrt(rkv[:], pbs[bsum][:], 1.0 / 256.0, epsc[:, 0:1], PBK(bsum) + ["epsc"], ["rkv"])
                      rsqrt(rkvt[:], pbs[bt_][:, 8:12], 1.0 / 256.0, epsc[:, 0:1], PBK(bt_) + ["epsc"], ["rkvt"])
                      tt("dve", crr[:], cst[:], rq[0:64, :], ALU.mult, ["cst", "rq"], ["crr"])
                      tt("dve", srr[:], sst[:], rq[0:64, :], ALU.mult, ["sst", "rq"], ["srr"])
                      X = []
                      Y = []

                      def xq(h):
                          b = nextpb()
                          for c in range(3):
                              mm(pbs[b][:], Qb[:, c, h, 0:128], qlT[:, c, :], c == 0, c == 2, ["Qb", "qlT"], PBK(b))
                          oi = nextost()
                          tt("dve", ost[oi][:], pbs[b][:], rq[:], ALU.mult, PBK(b) + ["rq"], [f"ost{oi}"])
                          dma("sp", f"ost{oi}", QT[h, 0:128, tok], ost[oi][:], r=[f"ost{oi}"], w=[("QT", h, g)])
                          b = nextpb()
                          for c in range(3):
                              mm(pbs[b][:], Qb[:, c, h, 128:256], qlT[:, c, :], c == 0, c == 2, ["Qb", "qlT"], PBK(b))
                          tt("dve", t1[0][:], pbs[b][0:64, :], crr[:], ALU.mult, PBK(b) + ["crr"], ["t1_0"])
                          tt("dve", t2[0][:], pbs[b][64:128, :], srr[:], ALU.mult, PBK(b) + ["srr"], ["t2_0"])
                          oi = nextost()
                          tt("pool", ost[oi][0:64, :], t1[0][:], t2[0][:], ALU.add, ["t1_0", "t2_0"], [f"ost{oi}"])
                          dma("sp", f"ost{oi}", QT[h, 128:192, tok], ost[oi][0:64, :], r=[f"ost{oi}"], w=[("QT", h, g)])

                      def xk(h):
                          b = nextpb()
                          for c in range(2):
                              mm(pbs[b][:], KVb[:, c, h * 256:h * 256 + 128], kvT[:, c, :], c == 0, c == 1, ["KVb", "kvT"], PBK(b))
                          oi = nextost()
                          tt("dve", ost[oi][:], pbs[b][:], rkv[:], ALU.mult, PBK(b) + ["rkv"], [f"ost{oi}"])
                          dma("sp", f"ost{oi}", KT[h, :, tok], ost[oi][:], r=[f"ost{oi}"], w=[("KT", h, g)])

                      kvv = KVb[:].rearrange("p c (h f) -> p c h f", h=4)

                      def xv(t):
                          b = nextpb()
                          for c in range(2):
                              mm(pbs[b][:].rearrange("p (h f) -> p h f", h=4), kvT[:, c, t * 128:(t + 1) * 128], kvv[:, c, :, 128:256],
                                 c == 0, c == 1, ["KVb", "kvT"], PBK(b))
                          vi = (g * 4 + t) % 2
                          tsc("dve", vst[vi][:], pbs[b][:], rkvt[:, t:t + 1], None, ALU.mult, None, PBK(b) + ["rkvt"], [f"vst{vi}"])
                          dma("sp", f"vst{vi}", VV[g * 512 + t * 128:g * 512 + (t + 1) * 128, :], vst[vi][:], r=[f"vst{vi}"], w=[("VV", g)])

                      def xkpe():
                          b = proj(640, 128)
                          tt("dve", t1[0][:], pbs[b][0:64, :], cst[:], ALU.mult, PBK(b) + ["cst"], ["t1_0"])
                          tt("dve", t2[0][:], pbs[b][64:128, :], sst[:], ALU.mult, PBK(b) + ["sst"], ["t2_0"])
                          oi = nextost()
                          tt("pool", ost[oi][0:64, :], t1[0][:], t2[0][:], ALU.add, ["t1_0", "t2_0"], [f"ost{oi}"])
                          dma("sp", f"ost{oi}", KPE[:, tok], ost[oi][0:64, :], r=[f"ost{oi}"], w=[("KPE", g)])

                      for h in range(4):
                          X.append(lambda h=h: xq(h))
                      for h in range(4):
                          X.append(lambda h=h: xk(h))
                      for t in range(4):
                          X.append(lambda t=t: xv(t))
                      X.append(xkpe)

                      def yz(zi, c):
                          base, dstD, zk = ((768, ZM, "ZM"), (2824, ZG, "ZG"))[zi]
                          b = proj(base + c * 128, 128)
                          act(zst[zi][:, c, :], pbs[b][:], AF.Silu, PBK(b), [f"zst{zi}"])
                          if c == 3:
                              dma("sp", f"zst{zi}", dstD[:, :, tok].rearrange("c p t -> p c t"), zst[zi][:], r=[f"zst{zi}"], w=[(zk, g)])

                      def conv_pe(ch):
                          sl = ch % 3
                          b2 = nextpb()
                          for j in range(4):
                              mm(pbs[b2][:], convd[:, j * 12 + ch, :], cvb[sl][:, j:j + 512], j == 0, j == 3, ["convd", f"cvb{sl}"], PBK(b2))
                          hh = ch % 4
                          if ch >= 8:
                              oi = nextost()
                              act(ost[oi][:], pbs[b2][:], AF.Silu, PBK(b2), [f"ost{oi}"])
                              dma("sp", f"ost{oi}", GV[hh, :, tok], ost[oi][:], r=[f"ost{oi}"], w=[("GV", hh, g)])
                          else:
                              act(sact[ch][:], pbs[b2][:], AF.Silu, PBK(b2), [f"sact{ch}"])

                      def yconv(ch):
                          sl = ch % 3
                          b = proj(1280 + ch * 128, 128)
                          cp("act", cvb[sl][:, 3:515], pbs[b][:], PBK(b), [f"cvb{sl}"])
                          cp("pool", cvb[sl][:, 0:3], cvc[:, ch, :], ["cvc"], [f"cvb{sl}"])
                          cp("pool", cvc[:, ch, :], cvb[sl][:, 512:515], [f"cvb{sl}"], ["cvc"])
                          if ch >= 1:
                              conv_pe(ch - 1)
                          if ch == 11:
                              conv_pe(11)

                      for zi in range(2):
                          for c in range(4):
                              Y.append(lambda zi=zi, c=c: yz(zi, c))
                      for ch in range(12):
                          Y.append(lambda ch=ch: yconv(ch))
                      xi = 0
                      for yi, yf in enumerate(Y):
                          yf()
                          while xi < len(X) and xi < (yi + 1) * len(X) / len(Y):
                              X[xi]()
                              xi += 1
                          if yi == 9 and g + 1 < NGr:
                              xstat_all(g + 1)
                      while xi < len(X):
                          X[xi]()
                          xi += 1
                      def l2sq(ch):
                          if ch < 8:
                              act(sq[ch % 3][:], sact[ch][:], AF.Square, [f"sact{ch}"], [f"sq{ch % 3}"])
                      l2sq(0)
                      l2sq(1)
                      for ch in range(8):
                          l2sq(ch + 2)
                          b2 = nextpb()
                          mm(pbs[b2][:], onesf[:], sq[ch % 3][:], True, True, [f"sq{ch % 3}", "onesf"], PBK(b2))
                          r_ = rn[ch % 2]
                          rk_ = f"rn{ch % 2}"
                          rsqrt(r_[:], pbs[b2][:], 1.0, epsc[:, 0:1], PBK(b2) + ["epsc"], [rk_])
                          oi = nextost()
                          sc_ = (128.0 ** -0.5) if ch < 4 else 1.0
                          stt("dve", ost[oi][:], sact[ch][:], sc_, r_[:], ALU.mult, ALU.mult, [f"sact{ch}", rk_], [f"ost{oi}"])
                          hh = ch % 4
                          dma("sp", f"ost{oi}", (GQ if ch < 4 else GK)[hh, :, tok], ost[oi][:], r=[f"ost{oi}"],
                              w=[("GQ" if ch < 4 else "GK", hh, g)])
                          if ch % 2 == 1:
                              ntrans(ch // 2)
                      b = proj(2816, 8)
                      bet, ea, ga, gb2 = abw
                      act(bet[:], pbs[b][0:8, :], AF.Exp, PBK(b), ["abw0"], scale=-1.0)
                      tsc("dve", bet[:], bet[:], 1.0, None, ALU.add, None, ["abw0"], ["abw0"])
                      rcp(bet[:], bet[:], ["abw0"], ["abw0"])
                      act(ea[:], pbs[b][0:8, :], AF.Exp, PBK(b) + ["dtb"], ["abw1"], bias=dtb[:, l:l + 1])
                      act(ea[:], ea[:], AF.Ln, ["abw1"], ["abw1"], bias=1.0)
                      tsc("pool", ga[:], ea[:], nA[:, l:l + 1], None, ALU.mult, None, ["abw1", "nA"], ["abw2"])
                      src_, sk_, dst_, dk2 = ga, "abw2", gb2, "abw3"
                      for s_ in (1, 2, 4, 8, 16, 32):
                          sv = src_[:].rearrange("p (n c) -> p n c", c=64)
                          dv = dst_[:].rearrange("p (n c) -> p n c", c=64)
                          cp("pool", dv[:, :, 0:s_], sv[:, :, 0:s_], [sk_], [dk2])
                          tt("pool", dv[:, :, s_:64], sv[:, :, s_:64], sv[:, :, 0:64 - s_], ALU.add, [sk_], [dk2])
                          src_, sk_, dst_, dk2 = dst_, dk2, src_, sk_
                      stt("dve", bet[:], bet[:], small[0:8, 3:4], src_[:], ALU.mult, ALU.add, ["abw0", sk_, "small"], ["abw0"])
                      dma("sp", "abw0", COMB[:, tok], bet[:], r=["abw0"], w=[("COMB", g)])
                  P.barrier()
              if stop_after == "A" or stop_after == "W" or (stop_after or "").startswith("g"):
                  break

              with ExitStack() as pb_:
                  KTs = [sbuf(pb_, f"KTs{i}", [128, S], BF16) for i in range(2)]
                  KPs = sbuf(pb_, "KPs", [128, S], BF16)
                  Vs = [sbuf(pb_, f"Vs{i}", [128, NT, 128], BF16) for i in range(2)]
                  NQ = 4
                  NPT = 6
                  LAG = 2
                  qn = [sbuf(pb_, f"qn{i}", [128, 512], BF16) for i in range(NQ)]
                  qp = [sbuf(pb_, f"qp{i}", [128, 512], BF16) for i in range(NQ)]
                  zt = [sbuf(pb_, f"zt{i}", [128, 512], BF16) for i in range(NQ)]
                  PT = [sbuf(pb_, f"PT{i}", [128, 512], BF16) for i in range(NPT)]
                  accD = [sbuf(pb_, f"accD{i}", [128, 512], F32) for i in range(2)]
                  accP = [sbuf(pb_, f"accP{i}", [128, 512], F32) for i in range(2)]
                  rcs = [sbuf(pb_, f"rcs{i}", [128, 512], F32) for i in range(2)]
                  yo = [sbuf(pb_, f"yo{i}", [128, 512], BF16) for i in range(2)]
                  SCL = 192.0 ** -0.5
                  allg = list(range(NG))
                  memset("pool", KPs[64:128, :], 0.0, ["KPs"])
                  for i in range(NQ):
                      memset("pool", qp[i][64:128, :], 0.0, [f"qp{i}"])
                  dma("sp", "KPs", KPs[0:64, :], KPE, r=[("KPE", g) for g in allg], w=["KPs"])

                  def load_head(h):
                      i = h % 2
                      dma("sp", f"KTs{i}", KTs[i][:], KT[h], r=[("KT", h, g) for g in allg], w=[f"KTs{i}"])
                      vsrc = VV[:, h * 128:(h + 1) * 128].rearrange("(t p) f -> p t f", p=128)
                      nsp = max(1, NT // 16)
                      for ii in range(nsp):
                          tsl = slice(ii * (NT // nsp), (ii + 1) * (NT // nsp))
                          dma("sp", f"Vs{i}", Vs[i][:, tsl, :], vsrc[:, tsl, :], r=[("VV", g) for g in allg], w=[f"Vs{i}"])

                  units = []
                  groups = []
                  for h in range(4):
                      for g in range(NG):
                          gi = len(groups)
                          groups.append((h, g))
                          for jp in range(2 * (g + 1)):
                              units.append((h, g, jp, gi))
                  nU = len(units)
                  loaded = set()
                  touched = {}
                  NP2 = 4
                  PT2 = [sbuf(pb_, f"PT2_{i}", [128, 2, 512], BF16) for i in range(NP2)]
                  pairb = [sbuf(pb_, f"pairb{i}", [128, 512], BF16) for i in range(2)]

                  def ensure_group(gi):
                      if gi in loaded or gi >= len(groups):
                          return
                      loaded.add(gi)
                      h, g = groups[gi]
                      tok = slice(g * 512, (g + 1) * 512)
                      qi = gi % NQ
                      dma("sp", f"qn{qi}", qn[qi][:], QT[h, 0:128, tok], r=[("QT", h, g)], w=[f"qn{qi}"])
                      dma("sp", f"qp{qi}", qp[qi][0:64, :], QT[h, 128:192, tok], r=[("QT", h, g)], w=[f"qp{qi}"])
                      dma("sp", f"zt{qi}", zt[qi][:], ZM[h, :, tok], r=[("ZM", g)], w=[f"zt{qi}"])

                  load_head(0)
                  LAGP = 1
                  for s_ in range(nU + LAGP):
                      if s_ < nU:
                          h, g, jp, gi = units[s_]
                          if jp == 0:
                              ensure_group(gi)
                              ensure_group(gi + 1)
                              ensure_group(gi + 2)
                          qi = gi % NQ
                          hb = h % 2
                          b0 = 2 * (s_ % 2)
                          pi = s_ % NP2
                          for e_ in range(2):
                              j = 2 * jp + e_
                              ks = slice(j * 128, (j + 1) * 128)
                              mm(pbs[b0 + e_][:], KTs[hb][:, ks], qn[qi][:], True, False, [f"KTs{hb}", f"qn{qi}"], PBK(b0 + e_))
                              mm(pbs[b0 + e_][:], KPs[:, ks], qp[qi][:], False, True, ["KPs", f"qp{qi}"], PBK(b0 + e_))
                      if s_ - LAGP >= 0:
                          uh, ug, ujp, ugi = units[s_ - LAGP]
                          ob = 4 + ugi % 2
                          ppi = (s_ - LAGP) % NP2
                          lastp = ujp == 2 * (ug + 1) - 1
                          for e_ in range(2):
                              uj = 2 * ujp + e_
                              mm(pbs[ob][:], Vs[uh % 2][:, uj, :], PT2[ppi][:, e_, :], uj == 0, lastp and e_ == 1,
                                 [f"Vs{uh % 2}", f"PT2_{ppi}"], PBK(ob))
                          if lastp:
                              a2i = ugi % 2
                              uqi = ugi % NQ
                              mm(pbs[6][:], onesf[:], accD[a2i][:], True, True, [f"accD{a2i}", "onesf"], PBK(6))
                              rcp(rcs[a2i][:], pbs[6][:], PBK(6), [f"rcs{a2i}"])
                              tt("dve", rcs[a2i][:], rcs[a2i][:], pbs[ob][:], ALU.mult, [f"rcs{a2i}"] + PBK(ob), [f"rcs{a2i}"])
                              tt("dve", yo[a2i][:], rcs[a2i][:], zt[uqi][:], ALU.mult, [f"rcs{a2i}", f"zt{uqi}"], [f"yo{a2i}"])
                              dma("sp", f"yo{a2i}", YT[uh, :, ug * 512:(ug + 1) * 512], yo[a2i][:], r=[f"yo{a2i}"], w=[("YT", uh, ug)])
                          if ug == 0 and ujp == 1 and uh + 1 < 4:
                              load_head(uh + 1)
                      if s_ < nU:
                          pk = f"PT2_{pi}"
                          psrc = pbs[b0][:].rearrange("p (o n) -> p o n", o=1)
                          act(PT2[pi][:, 0, :], pbs[b0][:], AF.Exp, PBK(b0), [pk], scale=SCL)
                          act(PT2[pi][:, 1, :], pbs[b0 + 1][:], AF.Exp, PBK(b0 + 1), [pk], scale=SCL)
                          if 2 * jp >= 4 * g:
                              m0 = 2 * jp - 4 * g
                              tt("pool", PT2[pi][:], PT2[pi][:], attm[:, m0:m0 + 2, :], ALU.mult, [pk, "attm"], [pk])
                          a2i = gi % 2
                          pb2 = pairb[s_ % 2]
                          pbk = f"pairb{s_ % 2}"
                          tt("dve", pb2[:], PT2[pi][:, 0, :], PT2[pi][:, 1, :], ALU.add, [pk], [pbk])
                          if not touched.get(gi, False):
                              cp("dve", accD[a2i][:], pb2[:], [pbk], [f"accD{a2i}"])
                          else:
                              tt("dve", accD[a2i][:], accD[a2i][:], pb2[:], ALU.add, [pbk, f"accD{a2i}"], [f"accD{a2i}"])
                          touched[gi] = True
                  P.barrier()
              if stop_after == "B":
                  break

              with ExitStack() as pc:
                  combs = [sbuf(pc, f"combs{p}", [128, 512], F32) for p in range(2)]
                  combT = [sbuf(pc, f"combT{p}", [128, 4, 8], F32) for p in range(2)]
                  tok4 = [sbuf(pc, f"tok4{p}", [128, 4, 20], F32) for p in range(2)]
                  gcb = [[sbuf(pc, f"gcb{h}_{p}", [128, 512], F32) for p in range(2)] for h in range(4)]
                  btb = [sbuf(pc, f"btb{h}", [128, 512], F32) for h in range(4)]
                  gqs = [[sbuf(pc, f"gqs{h}_{p}", [128, 512], BF16) for p in range(2)] for h in range(4)]
                  gks = [[sbuf(pc, f"gks{h}_{p}", [128, 512], BF16) for p in range(2)] for h in range(4)]
                  gvs = [[sbuf(pc, f"gvs{h}_{p}", [128, 512], BF16) for p in range(2)] for h in range(4)]
                  kbT = [[sbuf(pc, f"kbT{h}_{p}", [128, 512], BF16) for p in range(2)] for h in range(4)]
                  egb = [[sbuf(pc, f"egb{h}_{p}", [128, 512], F32) for p in range(2)] for h in range(4)]
                  zgs = [[sbuf(pc, f"zgs{h}_{p}", [128, 512], BF16) for p in range(2)] for h in range(4)]
                  qdT = [[sbuf(pc, f"qdT{h}_{p}", [128, 512], BF16) for p in range(2)] for h in range(4)]
                  ygs = [[sbuf(pc, f"ygs{h}_{p}", [128, 512], BF16) for p in range(2)] for h in range(4)]
                  Sf = [sbuf(pc, f"Sf{h}", [128, 128], F32) for h in range(4)]
                  Sb = [sbuf(pc, f"Sb{h}", [128, 128], BF16) for h in range(4)]

                  def mk(name, dt, n=1):
                      return [[sbuf(pc, f"{name}{h}_{i}", [128, 128], dt) for i in range(n)] for h in range(4)]

                  Dm = mk("Dm", F32)
                  DTs = mk("DTs", F32)
                  DTi = mk("DTi", F32)
                  Ab = mk("Ab", F32, 2)
                  Bb = mk("Bb", F32, 2)
                  Pb = mk("Pb", F32, 2)
                  TTb = mk("TTb", BF16)
                  kbg = mk("kbg", BF16)
                  vb = mk("vb", BF16)
                  QKT = mk("QKT", BF16, 2)
                  kd = mk("kd", BF16, 4)
                  usb = mk("usb", F32, 2)
                  wT = mk("wT", BF16, 2)
                  vn = mk("vn", BF16)
                  on = mk("on", BF16)
                  ost2 = sbuf(pc, "ost2", [128, 4, 4], F32)
                  junk2 = sbuf(pc, "junk2", [64, 128], BF16)
                  for p in range(2):
                      memset("pool", combs[p][:], 0.0, [f"combs{p}"])
                  for h in range(4):
                      memset("pool", vn[h][0][:], 0.0, [f"vn{h}"])
                      memset("pool", on[h][0][:], 0.0, [f"on{h}"])
                      memset("pool", Sf[h][:], 0.0, [f"Sf{h}"])
                      memset("pool", Sb[h][:], 0.0, [f"Sb{h}"])
                  pctr = [0]
                  sctr = [0]

                  def nq():
                      i = pctr[0] % 16
                      pctr[0] += 1
                      b, q = i % 4, i // 4
                      nq.last = (b, q)
                      return pbs[b][:, q * 128:(q + 1) * 128], [("pb", b)]

                  def nqb():
                      ap, k = nq()
                      b, q = nq.last
                      return pbs[b][:].bitcast(BF16)[:, q * 256:q * 256 + 128], [("pb", b)]

                  def sq_():
                      i = sctr[0] % 12
                      sctr[0] += 1
                      b, q = 4 + i % 3, i // 3
                      return pbs[b][:, q * 128:(q + 1) * 128], [("pb", b)]

                  HH = range(4)

                  def setup(g):
                      gp = g % 2
                      tok = slice(g * 512, (g + 1) * 512)
                      dma("sp", f"combs{gp}", combs[gp][0:8, :], COMB[:, tok], r=[("COMB", g)], w=[f"combs{gp}"])
                      for h in HH:
                          dma("sp", f"gqs{h}_{gp}", gqs[h][gp][:], GQ[h, :, tok], r=[("GQ", h, g)], w=[f"gqs{h}_{gp}"])
                          dma("sp", f"gks{h}_{gp}", gks[h][gp][:], GK[h, :, tok], r=[("GK", h, g)], w=[f"gks{h}_{gp}"])
                          dma("sp", f"gvs{h}_{gp}", gvs[h][gp][:], GV[h, :, tok], r=[("GV", h, g)], w=[f"gvs{h}_{gp}"])
                          dma("sp", f"zgs{h}_{gp}", zgs[h][gp][:], ZG[h, :, tok], r=[("ZG", g)], w=[f"zgs{h}_{gp}"])
                      for h in HH:
                          b1, b2 = (2 * h) % 4, (2 * h + 1) % 4
                          mm(pbs[b1][:], sel[:, h, :], combs[gp][:], True, True, ["sel", f"combs{gp}"], PBK(b1))
                          cp("act", gcb[h][gp][:], pbs[b1][:], PBK(b1), [f"gcb{h}_{gp}"])
                          act(egb[h][gp][:], pbs[b1][:], AF.Exp, PBK(b1), [f"egb{h}_{gp}"])
                          mm(pbs[b2][:], sel[:, 4 + h, :], combs[gp][:], True, True, ["sel", f"combs{gp}"], PBK(b2))
                          cp("act", btb[h][:], pbs[b2][:], PBK(b2), [f"btb{h}"])
                          tt("dve", kbT[h][gp][:], gks[h][gp][:], btb[h][:], ALU.mult, [f"gks{h}_{gp}", f"btb{h}"], [f"kbT{h}_{gp}"])
                          tt("pool", qdT[h][gp][:], gqs[h][gp][:], egb[h][gp][:], ALU.mult, [f"gqs{h}_{gp}", f"egb{h}_{gp}"], [f"qdT{h}_{gp}"])
                      for t in range(4):
                          ap, k = nq()
                          tr(ap, combs[gp][:, t * 128:(t + 1) * 128], identf[:], [f"combs{gp}", "identf"], k)
                          cp("dve", combT[gp][:, t, :], ap[:, 0:8], k, [("combT", gp, t)])
                          act(tok4[gp][:, t, 0:4], combT[gp][:, t, 0:4], AF.Exp, [("combT", gp, t)], [("tok4", gp, t)])
                          tt("dve", tok4[gp][:, t, 4:8], combT[gp][:, t, 4:8], tok4[gp][:, t, 0:4], ALU.mult, [("combT", gp, t), ("tok4", gp, t)], [("tok4", gp, t)])
                          for h in HH:
                              for c2 in range(2):
                                  rs = slice(c2 * 64, (c2 + 1) * 64)
                                  lc = t * 128 + c2 * 64 + 63
                                  tt("dve", tok4[gp][rs, t, 8 + h:9 + h], gcb[h][gp][rs, lc:lc + 1], combT[gp][rs, t, h:h + 1], ALU.subtract,
                                     [f"gcb{h}_{gp}", ("combT", gp, t)], [("tok4", gp, t)])
                          act(tok4[gp][:, t, 8:12], tok4[gp][:, t, 8:12], AF.Exp, [("tok4", gp, t)], [("tok4", gp, t)])
                          tsc("dve", tok4[gp][:, t, 12:16], tok4[gp][:, t, 8:12], pmask[:, 0:1], None, ALU.mult, None, [("tok4", gp, t), "pmask"], [("tok4", gp, t)])
                          tsc("dve", tok4[gp][:, t, 16:20], tok4[gp][:, t, 8:12], pmask[:, 1:2], None, ALU.mult, None, [("tok4", gp, t), "pmask"], [("tok4", gp, t)])

                  def par_steps(g, t):
                      tp = (g * 4 + t) % 2
                      gp = g % 2
                      ts_ = slice(t * 128, (t + 1) * 128)
                      steps = []

                      def p0():
                          for h in HH:
                              tsc("dve", Dm[h][0][:], gcb[h][gp][:, ts_], combT[gp][:, t, h:h + 1], 0.0, ALU.subtract, ALU.max,
                                  [f"gcb{h}_{gp}", ("combT", gp, t)], [f"Dm{h}"])
                              act(Dm[h][0][:], Dm[h][0][:], AF.Exp, [f"Dm{h}"], [f"Dm{h}"], scale=-1.0)
                              tsc("dve", DTs[h][0][:], gcb[h][gp][:, ts_], combT[gp][:, t, h:h + 1], 0.0, ALU.subtract, ALU.min,
                                  [f"gcb{h}_{gp}", ("combT", gp, t)], [f"DTs{h}"])
                              act(DTs[h][0][:], DTs[h][0][:], AF.Exp, [f"DTs{h}"], [f"DTs{h}"])
                              tt("pool", Dm[h][0][:], Dm[h][0][:], gmask[:, 0, :], ALU.mult, [f"Dm{h}", "gmask"], [f"Dm{h}"])
                              tt("pool", DTi[h][0][:], DTs[h][0][:], gmask[:, 2, :], ALU.mult, [f"DTs{h}", "gmask"], [f"DTi{h}"])
                              tt("pool", DTs[h][0][:], DTs[h][0][:], gmask[:, 1, :], ALU.mult, [f"DTs{h}", "gmask"], [f"DTs{h}"])
                          for h in HH:
                              ap, k = nq()
                              mm(ap, kbT[h][gp][:, ts_], gks[h][gp][:, ts_], True, True, [f"kbT{h}_{gp}", f"gks{h}_{gp}"], k)
                              tt("dve", Ab[h][0][:], ap, Dm[h][0][:], ALU.mult, k + [f"Dm{h}"], [f"Ab{h}_0"])
                              ap, k = nq()
                              mm(ap, gks[h][gp][:, ts_], kbT[h][gp][:, ts_], True, True, [f"kbT{h}_{gp}", f"gks{h}_{gp}"], k)
                              tt("dve", Bb[h][0][:], ap, DTs[h][0][:], ALU.mult, k + [f"DTs{h}"], [f"Bb{h}_0"])
                              ap, k = nq()
                              mm(ap, gks[h][gp][:, ts_], gqs[h][gp][:, ts_], True, True, [f"gqs{h}_{gp}", f"gks{h}_{gp}"], k)
                              tt("dve", QKT[h][tp][:], ap, DTi[h][0][:], ALU.mult, k + [f"DTi{h}"], [f"QKT{h}_{tp}"])
                              tt("pool", Pb[h][0][:], identf[:], Bb[h][0][:], ALU.subtract, ["identf", f"Bb{h}_0"], [f"Pb{h}_0"])
                      steps.append(p0)

                      def mklevel(kk):
                          def lev():
                              ci, co = (kk - 1) % 2, kk % 2
                              for h in HH:
                                  ap, k = nq()
                                  mm(ap, Bb[h][ci][:], Ab[h][ci][:], True, True, [f"Bb{h}_{ci}", f"Ab{h}_{ci}"], k)
                                  cp("act", Ab[h][co][:], ap, k, [f"Ab{h}_{co}"])
                                  if kk < 5:
                                      ap2, k2 = nq()
                                      mm(ap2, Ab[h][ci][:], Bb[h][ci][:], True, True, [f"Bb{h}_{ci}", f"Ab{h}_{ci}"], k2)
                                      cp("dve", Bb[h][co][:], ap2, k2, [f"Bb{h}_{co}"])
                              for h in HH:
                                  ap, k = nq()
                                  mm(ap, Ab[h][co][:], Pb[h][ci][:], True, True, [f"Ab{h}_{co}", f"Pb{h}_{ci}"], k)
                                  tt("dve", Pb[h][co][:], ap, Pb[h][ci][:], ALU.add, k + [f"Pb{h}_{ci}"], [f"Pb{h}_{co}"])
                          return lev
                      for kk in range(1, 6):
                          steps.append(mklevel(kk))

                      def p6():
                          PF = 5 % 2
                          for h in HH:
                              cp("act", TTb[h][0][:], Pb[h][PF][:], [f"Pb{h}_{PF}"], [f"TTb{h}"])
                              ap, k = nqb()
                              tr(ap, gks[h][gp][:, ts_], identb[:], [f"gks{h}_{gp}", "identb"], k)
                              tsc("dve", kbg[h][0][:], ap, tok4[gp][:, t, 4 + h:5 + h], None, ALU.mult, None, k + [("tok4", gp, t)], [f"kbg{h}"])
                              tsc("dve", kd[h][2 * tp][:], ap, tok4[gp][:, t, 12 + h:13 + h], None, ALU.mult, None, k + [("tok4", gp, t)], [f"kd{h}_{tp}"])
                              tsc("dve", kd[h][2 * tp + 1][:], ap, tok4[gp][:, t, 16 + h:17 + h], None, ALU.mult, None, k + [("tok4", gp, t)], [f"kd{h}_{tp}"])
                              ap, k = nqb()
                              tr(ap, gvs[h][gp][:, ts_], identb[:], [f"gvs{h}_{gp}", "identb"], k)
                              tsc("dve", vb[h][0][:], ap, combT[gp][:, t, 4 + h:5 + h], None, ALU.mult, None, k + [("combT", gp, t)], [f"vb{h}"])
                          for h in HH:
                              ap, k = nq()
                              mm(ap, TTb[h][0][:], vb[h][0][:], True, True, [f"TTb{h}", f"vb{h}"], k)
                              cp("act", usb[h][tp][:], ap, k, [f"usb{h}_{tp}"])
                              ap, k = nq()
                              mm(ap, kbg[h][0][:], TTb[h][0][:], True, True, [f"TTb{h}", f"kbg{h}"], k)
                              cp("act", wT[h][tp][:], ap, k, [f"wT{h}_{tp}"])
                      steps.append(p6)
                      return steps

                  def seq_steps(g, t):
                      gp = g % 2
                      tp = (g * 4 + t) % 2
                      ts_ = slice(t * 128, (t + 1) * 128)
                      tok = slice(g * 512, (g + 1) * 512)
                      steps = []
                      yps = {h: (pbs[7][:].bitcast(BF16)[:, h * 256:(h + 1) * 256].rearrange("p (c t) -> p c t", c=2), [("pb", 7)]) for h in HH}
                      state = {}
                      for c2 in range(2):
                          rs = slice(c2 * 64, (c2 + 1) * 64)
                          cs_ = slice(t * 128 + c2 * 64, t * 128 + (c2 + 1) * 64)
                          lc = t * 128 + c2 * 64 + 63

                          def sa(c2=c2, rs=rs):
                              wsp = {}
                              for h in HH:
                                  ap, k = sq_()
                                  mm(ap[0:64, :], wT[h][tp][:, rs], Sb[h][:], True, True, [f"wT{h}_{tp}", f"Sb{h}"], k)
                                  wsp[h] = (ap, k)
                              for h in HH:
                                  ap, k = wsp[h]
                                  tt("dve", vn[h][0][rs, :], usb[h][tp][rs, :], ap[0:64, :], ALU.subtract, k + [f"usb{h}_{tp}"], [f"vn{h}"])

                          def sb_(c2=c2, rs=rs, cs_=cs_, lc=lc):
                              osp = {}
                              for h in HH:
                                  ap, k = sq_()
                                  mm(ap[0:64, :], qdT[h][gp][:, cs_], Sb[h][:], True, False, [f"qdT{h}_{gp}", f"Sb{h}"], k)
                                  mm(ap[0:64, :], QKT[h][tp][:, rs], vn[h][0][:, :], False, True, [f"QKT{h}_{tp}", f"vn{h}"], k)
                                  osp[h] = (ap, k)
                                  ap2, k2 = sq_()
                                  mm(ap2, kd[h][2 * tp + c2][:, :], vn[h][0][:, :], True, True, [f"kd{h}_{tp}", f"vn{h}"], k2)
                                  stt("dve", Sf[h][:], Sf[h][:], egb[h][gp][:, lc:lc + 1], ap2, ALU.mult, ALU.add,
                                      k2 + [f"Sf{h}", f"egb{h}_{gp}"], [f"Sf{h}"])
                                  cp("pool", Sb[h][:], Sf[h][:], [f"Sf{h}"], [f"Sb{h}"])
                              state[c2] = osp

                          def sc(c2=c2):
                              osp = state[c2]
                              for h in HH:
                                  ap, k = osp[h]
                                  act(junk2[:], ap[0:64, :], AF.Square, k, ["junk2", "ost2"], accum=ost2[0:64, h, 0:1])
                              rsqrt(ost2[0:64, :, 1:2], ost2[0:64, :, 0:1], 1.0 / 128.0, epsc[0:64, 0:1], ["ost2", "epsc"], ["ost2"])
                              for h in HH:
                                  ap, k = osp[h]
                                  tsc("dve", on[h][0][0:64, :], ap[0:64, :], ost2[0:64, h, 1:2], None, ALU.mult, None, k + ["ost2"], [f"on{h}"])
                                  yap, yk = yps[h]
                                  tr(yap[:, c2, :], on[h][0][:, :], identb[:], [f"on{h}", "identb"], yk)
                          steps += [sa, sb_, sc]

                      def sf():
                          for h in HH:
                              yap, yk = yps[h]
                              stt("dve", ygs[h][gp][:, ts_].rearrange("p (c t) -> p c t", c=2), yap[:, :, 0:64], cols[:, l, 29:30],
                                  zgs[h][gp][:, ts_].rearrange("p (c t) -> p c t", c=2), ALU.mult, ALU.mult,
                                  yk + [("cols", l), f"zgs{h}_{gp}"], [f"ygs{h}_{gp}"])
                          if t == 3:
                              for h in HH:
                                  dma("sp", f"ygs{h}_{gp}", YT[4 + h, :, tok], ygs[h][gp][:], r=[f"ygs{h}_{gp}"], w=[("YT", 4 + h, g)])
                      steps.append(sf)
                      return steps

                  prev = None
                  setup(0)
                  for g in range(NG):
                      for t in range(4):
                          ps_l = par_steps(g, t)
                          ss_l = seq_steps(*prev) if prev is not None else []
                          for i in range(max(len(ps_l), len(ss_l))):
                              if i < len(ps_l):
                                  ps_l[i]()
                              if i < len(ss_l):
                                  ss_l[i]()
                          prev = (g, t)
                          if t == 1 and g + 1 < NG:
                              setup(g + 1)
                  for f_ in seq_steps(*prev):
                      f_()
                  P.barrier()
              if stop_after == "C":
                  break

              with ExitStack() as pd:
                  WOb = sbuf(pd, "WOb", [128, 8, D], BF16)
                  stg = [sbuf(pd, f"stgd{i}", [128, D], F32) for i in range(2)]
                  for c in range(8):
                      sk = f"stgd{c % 2}"
                      dma("sp", sk, stg[c % 2][:], w_out[l, c * 128:(c + 1) * 128, :], w=[sk])
                      cp("dve" if c % 2 == 0 else "pool", WOb[:, c, :], stg[c % 2][:], [sk], ["WOb"])
                  NS = 3
                  yt = [sbuf(pd, f"yt{i}", [128, 8, 128], BF16) for i in range(NS)]
                  xd = [sbuf(pd, f"xd{i}", [128, D], F32) for i in range(NS)]
                  xo = [sbuf(pd, f"xo{i}", [128, D], F32) for i in range(NS)]
                  junk3 = sbuf(pd, "junk3", [128, D], BF16)
                  st3 = sbuf(pd, "st3", [128, NS, 4], F32)

                  def d_load(ti):
                      if ti >= NT:
                          return
                      g = ti // 4
                      i2 = ti % NS
                      dma("sp", f"yt{i2}", yt[i2][:], YT[:, :, ti * 128:(ti + 1) * 128].rearrange("c p t -> p c t"),
                          r=[("YT", c, g) for c in range(8)], w=[f"yt{i2}"])
                      dma("sp", f"xd{i2}", xd[i2][:], xsrc[ti * 128:(ti + 1) * 128, :], r=[(xkey, ti)], w=[f"xd{i2}"])

                  def d_mm(ti):
                      i2 = ti % NS
                      b0 = 2 * (ti % 4)
                      for half in range(2):
                          for c in range(8):
                              mm(pbs[b0 + half][:], yt[i2][:, c, :], WOb[:, c, half * 512:(half + 1) * 512], c == 0, c == 7,
                                 [f"yt{i2}", "WOb"], PBK(b0 + half))

                  def d_epi(ti):
                      i2 = ti % NS
                      b0 = 2 * (ti % 4)
                      sk3 = ("st3", i2)
                      for half in range(2):
                          act(junk3[:, half * 512:(half + 1) * 512], pbs[b0 + half][:], AF.Square, PBK(b0 + half), ["junk3", sk3],
                              accum=st3[:, i2, half:half + 1])
                      tt("dve", st3[:, i2, 2:3], st3[:, i2, 0:1], st3[:, i2, 1:2], ALU.add, [sk3], [sk3])
                      rsqrt(st3[:, i2, 3:4], st3[:, i2, 2:3], 1.0 / D, epsc[:, 0:1], [sk3, "epsc"], [sk3])
                      for half in range(2):
                          hs = slice(half * 512, (half + 1) * 512)
                          stt("dve", xo[i2][:, hs], pbs[b0 + half][:], st3[:, i2, 3:4], gpb[:, hs], ALU.mult, ALU.mult,
                              PBK(b0 + half) + [sk3, "gpb"], [f"xo{i2}"])
                      tt("dve", xo[i2][:], xo[i2][:], xd[i2][:], ALU.add, [f"xo{i2}", f"xd{i2}"], [f"xo{i2}"])
                      dma("sp", f"xo{i2}", xdst[ti * 128:(ti + 1) * 128, :], xo[i2][:], r=[f"xo{i2}"], w=[(xdkey, ti)])

                  d_load(0)
                  d_load(1)
                  for ti in range(NT + 1):
                      if ti < NT:
                          d_mm(ti)
                      if ti >= 1:
                          d_epi(ti - 1)
                      d_load(ti + 2)
                  P.barrier()
        except _Stop:
            pass
        P.final_wait("sp", [("OUT", ti) for ti in range(NT)])
        build.stats = (dict(P.n_inst), dict(P.n_wait), P.nsem)
    return nc


def make_consts():
    ident = np.eye(128, dtype=np.float32)
    attm = np.zeros((128, 4, 512), np.float32)
    k = np.arange(128)[:, None]
    q = np.arange(512)[None, :]
    for j in range(4):
        attm[:, j, :] = (q >= 128 * j + k).astype(np.float32)
    i = np.arange(128)[:, None]
    jj = np.arange(128)[None, :]
    same = (i // 64) == (jj // 64)
    gm = np.zeros((128, 3, 128), np.float32)
    gm[:, 0, :] = (same & (i > jj))
    gm[:, 1, :] = (same & (i < jj))
    gm[:, 2, :] = (same & (i <= jj))
    sel = np.zeros((128, 8, 128), np.float32)
    for r in range(8):
        sel[r, r, :] = 1.0
    small = np.zeros((64, 4), np.float32)
    half = 32
    invf = np.power(np.float32(10000.0), -np.arange(half, dtype=np.float32) * np.float32(2.0) / np.float32(64)).astype(np.float32)
    small[:, 0] = np.concatenate([invf, invf])
    small[:32, 1] = -1.0
    small[32:, 1] = 1.0
    small[0:4, 2] = -1.0
    small[4:8, 3] = 1.0
    return dict(c_ident=ident, c_attm=attm.reshape(128, 2048), c_gmask=gm.reshape(128, 384), c_sel=sel.reshape(128, 1024), c_small=small,
                c_pm=np.stack([(np.arange(128) < 64), (np.arange(128) >= 64)], axis=1).astype(np.float32))


def make_in_map(inputs, b, S, NL):
    f = lambda a: np.ascontiguousarray(np.asarray(a))
    m = {}
    m["x"] = f(inputs["x"][b, :S])
    m["ccol"] = f(np.asarray(inputs["c"][b]).reshape(8, 128).T)
    m["posrep"] = f(np.broadcast_to(np.asarray(inputs["positions"][b, :S])[None, :], (64, S))).astype(np.int32)
    m["w_mod"] = f(inputs["w_mod"][:NL])
    m["b_mod"] = f(inputs["b_mod"][:NL])
    m["prew"] = f(inputs["pre_norm_w"][:NL])
    m["postw"] = f(inputs["post_norm_w"][:NL])
    m["w_in"] = f(inputs["w_in"][:NL])
    m["qnw"] = f(inputs["mla_q_norm_w"][:NL])
    m["q_up"] = f(inputs["mla_q_up"][:NL])
    m["kvnw"] = f(inputs["mla_kv_norm_w"][:NL])
    m["kv_up"] = f(inputs["mla_kv_up"][:NL])
    m["convw"] = f(np.asarray(inputs["gdn_conv_w"][:NL]).reshape(NL, 4 * 1536))
    a8 = np.zeros((8, NL), np.float32)
    a8[0:4, :] = np.asarray(inputs["gdn_a_log"][:NL]).T
    d8 = np.zeros((8, NL), np.float32)
    d8[0:4, :] = np.asarray(inputs["gdn_dt_bias"][:NL]).T
    m["alog8"] = a8
    m["dtb8"] = d8
    m["onw"] = f(inputs["gdn_o_norm_w"][:NL])
    m["w_out"] = f(inputs["w_out"][:NL])
    m.update(make_consts())
    return m


_NC_CACHE = {}


def kernel(**inputs):
    B, S, _ = inputs["x"].shape
    NL = inputs["w_in"].shape[0]
    key = (S, NL)
    if key not in _NC_CACHE:
        _NC_CACHE[key] = build(S, NL)
    nc = _NC_CACHE[key]
    in_maps = [make_in_map(inputs, c % B, S, NL) for c in range(8)]
    res = run_bass_kernel_spmd(nc, in_maps, core_ids=list(range(8)))
    outs = [np.asarray(res.results[b]["out"]) for b in range(B)]
    return np.stack(outs, axis=0).astype(np.float32)
```
